# Optimizing a Trainium2 kernel written in Bass

```python
import math
import numpy as np
import jax
import jax.numpy as jnp
from jax import lax

D_MODEL = 1024
BATCH = 32
SEQ = 2048
DEPTH = 4

N_EVEN = (DEPTH + 1) // 2
N_ODD = DEPTH // 2

Q_BLOCK = 128
N_BUCKETS = 32
MAX_DISTANCE = 128
NEG_INF = -1e30

MLA_HEADS = 8
MLA_Q_RANK = 384
MLA_KV_RANK = 256
MLA_NOPE = 64
MLA_ROPE = 32
MLA_V = 64
ROPE_BASE = 10000.0
MLA_COLS = MLA_Q_RANK + MLA_KV_RANK + MLA_ROPE

NSA_HEADS = 8
NSA_GROUPS = 2
NSA_HPG = NSA_HEADS // NSA_GROUPS
NSA_DH = 64
CMP_LEN = 32
CMP_STRIDE = 16
CMP_HIDDEN = 256
SEL_LEN = 64
SEL_TOP = 8
SEL_CHUNK = 32
FORCE_SCORE = 1e4
WINDOW = 256
NSA_Q_COLS = NSA_HEADS * NSA_DH
NSA_KV_COLS = 3 * 2 * NSA_GROUPS * NSA_DH
NSA_GATE_COLS = 3 * NSA_HEADS
NSA_COLS = NSA_Q_COLS + NSA_KV_COLS + NSA_GATE_COLS
EVEN_IN = MLA_COLS + NSA_COLS
EVEN_OUT = MLA_HEADS * MLA_V + NSA_HEADS * NSA_DH

DIFF_HEADS = 8
DIFF_DH = 64
DIFF_IN = DIFF_HEADS * 6 * DIFF_DH
DIFF_OUT = DIFF_HEADS * 2 * DIFF_DH

BIAS_HEADS = 8

D_FF = 2816
CONV_W = 3

ALPHA = (2.0 * DEPTH) ** 0.25
BETA = (8.0 * DEPTH) ** -0.25
LN_EPS = 1e-5
RMS_EPS = 1e-6

kernel_name = 'hybrid_mla_nsa_diffattn_convffn_deepnorm_adaln'


def _f32(t):
    return t.astype(jnp.float32)


def _layer_norm(t, g, b):
    tf = _f32(t)
    mu = jnp.mean(tf, axis=-1, keepdims=True)
    var = jnp.mean(jnp.square(tf - mu), axis=-1, keepdims=True)
    return ((tf - mu) * lax.rsqrt(var + LN_EPS) * _f32(g) + _f32(b)).astype(t.dtype)


def _rms_norm(t, g):
    tf = _f32(t)
    return (tf * lax.rsqrt(jnp.mean(tf * tf, axis=-1, keepdims=True) + RMS_EPS) * _f32(g)).astype(t.dtype)


def _t5_bucket(dist):
    n = jnp.maximum(dist, 0)
    max_exact = N_BUCKETS // 2
    nf = jnp.maximum(n, 1).astype(jnp.float32)
    large = max_exact + (jnp.log(nf / max_exact) / math.log(MAX_DISTANCE / max_exact)
                         * (N_BUCKETS - max_exact)).astype(jnp.int32)
    large = jnp.minimum(large, N_BUCKETS - 1)
    return jnp.where(n < max_exact, n, large)


def _rope_tables(seq, dim):
    inv = 1.0 / (ROPE_BASE ** (jnp.arange(0, dim, 2, dtype=jnp.float32) / dim))
    ang = jnp.arange(seq, dtype=jnp.float32)[:, None] * inv[None, :]
    return jnp.cos(ang), jnp.sin(ang)


def _apply_rope(t, cos, sin):
    t1, t2 = jnp.split(t, 2, axis=-1)
    cs = cos[None, :, None, :].astype(t.dtype)
    sn = sin[None, :, None, :].astype(t.dtype)
    return jnp.concatenate([t1 * cs - t2 * sn, t1 * sn + t2 * cs], axis=-1)


def _mla(proj, q_norm, kv_norm, w_uq, w_ukv, cos, sin):
    B, S, _ = proj.shape
    c_q = _rms_norm(proj[..., :MLA_Q_RANK], q_norm)
    c_kv = _rms_norm(proj[..., MLA_Q_RANK:MLA_Q_RANK + MLA_KV_RANK], kv_norm)
    k_rope = _apply_rope(proj[..., MLA_Q_RANK + MLA_KV_RANK:][:, :, None, :], cos, sin)[:, :, 0]
    q = (c_q @ w_uq).reshape(B, S, MLA_HEADS, MLA_NOPE + MLA_ROPE)
    q_nope = q[..., :MLA_NOPE]
    q_rope = _apply_rope(q[..., MLA_NOPE:], cos, sin)
    kv = (c_kv @ w_ukv).reshape(B, S, MLA_HEADS, MLA_NOPE + MLA_V)
    k_nope, v = kv[..., :MLA_NOPE], kv[..., MLA_NOPE:]
    scale = (MLA_NOPE + MLA_ROPE) ** -0.5
    outs = []
    for qb in range(S // Q_BLOCK):
        q0, q1 = qb * Q_BLOCK, (qb + 1) * Q_BLOCK
        s = (jnp.einsum('bqhd,bkhd->bhqk', q_nope[:, q0:q1], k_nope[:, :q1])
             + jnp.einsum('bqhr,bkr->bhqk', q_rope[:, q0:q1], k_rope[:, :q1]))
        causal = np.arange(q0, q1)[:, None] >= np.arange(q1)[None, :]
        p = jax.nn.softmax(jnp.where(causal, _f32(s) * scale, NEG_INF), axis=-1)
        outs.append(jnp.einsum('bhqk,bkhd->bqhd', p.astype(v.dtype), v[:, :q1]))
    return jnp.concatenate(outs, axis=1).reshape(B, S, MLA_HEADS * MLA_V)


def _nsa(proj, cmp_pe, cmp_w1, cmp_w2, table):
    B, S, _ = proj.shape
    G, J, dh = NSA_GROUPS, NSA_HPG, NSA_DH
    q = proj[..., :NSA_Q_COLS].reshape(B, S, G, J, dh)
    kv = proj[..., NSA_Q_COLS:NSA_Q_COLS + NSA_KV_COLS].reshape(B, S, 3, 2, G, dh)
    gates = jax.nn.sigmoid(_f32(proj[..., NSA_Q_COLS + NSA_KV_COLS:]).reshape(B, S, G, J, 3)).astype(q.dtype)
    scale = dh ** -0.5
    tbl = table.reshape(N_BUCKETS, G, J)
    t_pos = np.arange(S)

    n_cmp = (S - CMP_LEN) // CMP_STRIDE + 1
    starts = np.arange(n_cmp) * CMP_STRIDE
    tok = starts[:, None] + np.arange(CMP_LEN)[None, :]

    def compress(t, pe, w1, w2):
        blocks = t[:, tok] + pe[:, None, :]
        flat = jnp.moveaxis(blocks, 3, 2).reshape(B, n_cmp, G, CMP_LEN * dh)
        return jax.nn.gelu(flat @ w1) @ w2

    k_cmp = compress(kv[:, :, 0, 0], cmp_pe[0], cmp_w1[0], cmp_w2[0])
    v_cmp = compress(kv[:, :, 0, 1], cmp_pe[1], cmp_w1[1], cmp_w2[1])
    d_cmp = t_pos[:, None] - (starts + CMP_LEN - 1)[None, :]
    b_cmp = jnp.moveaxis(tbl[_t5_bucket(jnp.asarray(d_cmp))], 1, -1)
    s_cmp = _f32(jnp.einsum('bsgjd,bngd->bsgjn', q, k_cmp)) * scale + _f32(b_cmp)
    p_cmp = jax.nn.softmax(jnp.where((d_cmp >= 0)[:, None, None, :], s_cmp, NEG_INF), axis=-1)
    p_cmp = jnp.where((t_pos >= CMP_LEN - 1)[:, None, None, None], p_cmp, 0.0)
    o_cmp = jnp.einsum('bsgjn,bngd->bsgjd', p_cmp.astype(q.dtype), v_cmp)

    n_slc = S // SEL_LEN
    top = min(SEL_TOP, n_slc)
    jb = np.arange(n_slc)
    overlap = ((starts[:, None] < (jb[None, :] + 1) * SEL_LEN)
               & (starts[:, None] + CMP_LEN > jb[None, :] * SEL_LEN)).astype(np.float32)
    imp = jnp.einsum('bsgn,nm->bsgm', jnp.sum(p_cmp, axis=3), jnp.asarray(overlap))
    cur = t_pos // SEL_LEN
    forced = (jb[None, :] == 0) | (jb[None, :] == cur[:, None]) | (jb[None, :] == cur[:, None] - 1)
    causal_blk = jb[None, :] * SEL_LEN <= t_pos[:, None]
    score = jnp.where(forced[:, None, :], FORCE_SCORE, imp)
    score = jnp.where(causal_blk[:, None, :], score, -1.0)
    _, sel = lax.top_k(score, top)

    k_s = jnp.moveaxis(kv[:, :, 1, 0].reshape(B, n_slc, SEL_LEN, G, dh), 3, 1)
    v_s = jnp.moveaxis(kv[:, :, 1, 1].reshape(B, n_slc, SEL_LEN, G, dh), 3, 1)
    n_chunk = S // SEL_CHUNK
    bi = jnp.arange(B)[:, None, None, None]
    gi = jnp.arange(G)[None, None, :, None]
    offs = jnp.arange(SEL_LEN)
    n_keys = top * SEL_LEN

    def sel_chunk(args):
        qc, ic, tc = args
        kg = k_s[bi, gi, ic].reshape(B, SEL_CHUNK, G, n_keys, dh)
        vg = v_s[bi, gi, ic].reshape(B, SEL_CHUNK, G, n_keys, dh)
        pos = (ic[..., None] * SEL_LEN + offs).reshape(B, SEL_CHUNK, G, n_keys)
        dist = tc[None, :, None, None] - pos
        bias = jnp.moveaxis(tbl[_t5_bucket(dist), gi], -1, 3)
        s = _f32(jnp.einsum('bcgjd,bcgnd->bcgjn', qc, kg)) * scale + _f32(bias)
        p = jax.nn.softmax(jnp.where((dist >= 0)[:, :, :, None, :], s, NEG_INF), axis=-1)
        return jnp.einsum('bcgjn,bcgnd->bcgjd', p.astype(vg.dtype), vg)

    def chunks(a):
        return jnp.moveaxis(a.reshape((B, n_chunk, SEL_CHUNK) + a.shape[2:]), 1, 0)

    o_slc = lax.map(sel_chunk, (chunks(q), chunks(sel), jnp.asarray(t_pos.reshape(n_chunk, SEL_CHUNK))))
    o_slc = jnp.moveaxis(o_slc, 0, 1).reshape(B, S, G, J, dh)

    nb = S // Q_BLOCK
    wb = WINDOW // Q_BLOCK
    kb_len = (wb + 1) * Q_BLOCK

    def band(t):
        tp = jnp.pad(t, ((0, 0), (WINDOW, 0), (0, 0), (0, 0))).reshape(B, nb + wb, Q_BLOCK, G, dh)
        return jnp.concatenate([tp[:, i:i + nb] for i in range(wb + 1)], axis=2)

    k_band, v_band = band(kv[:, :, 2, 0]), band(kv[:, :, 2, 1])
    d_win = np.arange(Q_BLOCK)[:, None] + WINDOW - np.arange(kb_len)[None, :]
    b_win = _f32(jnp.moveaxis(tbl[_t5_bucket(jnp.asarray(d_win))], (2, 3), (0, 1)))
    key_abs = np.arange(nb)[:, None] * Q_BLOCK - WINDOW + np.arange(kb_len)[None, :]
    m_win = ((d_win >= 0) & (d_win < WINDOW))[None] & (key_abs >= 0)[:, None, :]

    def win_block(args):
        qi, ki, vi, mi = args
        s = _f32(jnp.einsum('bqgjd,bkgd->bgjqk', qi, ki)) * scale + b_win
        p = jax.nn.softmax(jnp.where(mi, s, NEG_INF), axis=-1)
        return jnp.einsum('bgjqk,bkgd->bqgjd', p.astype(vi.dtype), vi)

    q_blocks = jnp.moveaxis(q.reshape(B, nb, Q_BLOCK, G, J, dh), 1, 0)
    o_win = lax.map(win_block, (q_blocks, jnp.moveaxis(k_band, 1, 0), jnp.moveaxis(v_band, 1, 0), jnp.asarray(m_win)))
    o_win = jnp.moveaxis(o_win, 0, 1).reshape(B, S, G, J, dh)

    o = gates[..., 0:1] * o_cmp + gates[..., 1:2] * o_slc + gates[..., 2:3] * o_win
    return o.reshape(B, S, NSA_HEADS * dh)


def _diff_attention(proj, lam_vec, subln, table, layer_idx):
    B, S, _ = proj.shape
    H, d = DIFF_HEADS, DIFF_DH
    q = proj[..., :H * 2 * d].reshape(B, S, H, 2, d)
    k = proj[..., H * 2 * d:H * 4 * d].reshape(B, S, H, 2, d)
    v = proj[..., H * 4 * d:].reshape(B, S, H, 2 * d)
    lam_init = 0.8 - 0.6 * math.exp(-0.3 * layer_idx)
    lv = _f32(lam_vec)
    lam = jnp.exp(jnp.sum(lv[0] * lv[1])) - jnp.exp(jnp.sum(lv[2] * lv[3])) + lam_init
    scale = d ** -0.5
    outs = []
    for qb in range(S // Q_BLOCK):
        q0, q1 = qb * Q_BLOCK, (qb + 1) * Q_BLOCK
        dist = np.arange(q0, q1)[:, None] - np.arange(q1)[None, :]
        bias = jnp.moveaxis(_f32(table[_t5_bucket(jnp.asarray(dist))]), -1, 0)
        causal = dist >= 0
        maps = []
        for i in range(2):
            s = _f32(jnp.einsum('bqhd,bkhd->bhqk', q[:, q0:q1, :, i], k[:, :q1, :, i])) * scale + bias
            maps.append(jax.nn.softmax(jnp.where(causal, s, NEG_INF), axis=-1))
        a = maps[0] - lam * maps[1]
        outs.append(jnp.einsum('bhqk,bkhd->bqhd', a.astype(v.dtype), v[:, :q1]))
    o = _rms_norm(jnp.concatenate(outs, axis=1), subln) * (1.0 - lam_init)
    return o.reshape(B, S, H * 2 * d)


def _conv_ffn(h, w_up, w_gate, conv_w, conv_b, w_down):
    u = h @ w_up
    a = lax.conv_general_dilated(h @ w_gate, conv_w[:, None, :], window_strides=(1,),
                                 padding=[(CONV_W - 1, 0)], dimension_numbers=('NWC', 'WIO', 'NWC'),
                                 feature_group_count=D_FF) + conv_b
    return (jax.nn.silu(a) * u) @ w_down


def setup_inputs(seed: int = 0) -> dict:
    key = jax.random.key(seed)
    ks = iter(jax.random.split(key, 32))

    def nrm(shape, std):
        return std * jax.random.normal(next(ks), shape, jnp.float32)

    D = D_MODEL
    return {
        'x': nrm((BATCH, SEQ, D), 1.0),
        'c': nrm((BATCH, D), 1.0),
        'rel_bias': nrm((N_BUCKETS, BIAS_HEADS), 0.3),
        'ev_w_in': nrm((N_EVEN, D, EVEN_IN), D ** -0.5),
        'mla_q_norm': 1.0 + nrm((N_EVEN, MLA_Q_RANK), 0.01),
        'mla_kv_norm': 1.0 + nrm((N_EVEN, MLA_KV_RANK), 0.01),
        'mla_w_uq': nrm((N_EVEN, MLA_Q_RANK, MLA_HEADS * (MLA_NOPE + MLA_ROPE)), MLA_Q_RANK ** -0.5),
        'mla_w_ukv': nrm((N_EVEN, MLA_KV_RANK, MLA_HEADS * (MLA_NOPE + MLA_V)), MLA_KV_RANK ** -0.5),
        'nsa_cmp_pe': nrm((N_EVEN, 2, CMP_LEN, NSA_DH), 0.02),
        'nsa_cmp_w1': nrm((N_EVEN, 2, CMP_LEN * NSA_DH, CMP_HIDDEN), (CMP_LEN * NSA_DH) ** -0.5),
        'nsa_cmp_w2': nrm((N_EVEN, 2, CMP_HIDDEN, NSA_DH), CMP_HIDDEN ** -0.5),
        'ev_w_o': nrm((N_EVEN, EVEN_OUT, D), BETA * EVEN_OUT ** -0.5),
        'od_w_in': nrm((N_ODD, D, DIFF_IN), D ** -0.5),
        'diff_lambda': nrm((N_ODD, 4, DIFF_DH), 0.1),
        'diff_subln': 1.0 + nrm((N_ODD, 2 * DIFF_DH), 0.01),
        'od_w_o': nrm((N_ODD, DIFF_OUT, D), BETA * DIFF_OUT ** -0.5),
        'ada_w': nrm((DEPTH, D, 6 * D), 0.2 * D ** -0.5),
        'ada_b': nrm((DEPTH, 6 * D), 0.01),
        'ln_g': 1.0 + nrm((DEPTH, 2, D), 0.01),
        'ln_b': nrm((DEPTH, 2, D), 0.01),
        'ffn_w_up': nrm((DEPTH, D, D_FF), D ** -0.5),
        'ffn_w_gate': nrm((DEPTH, D, D_FF), D ** -0.5),
        'ffn_conv_w': nrm((DEPTH, CONV_W, D_FF), CONV_W ** -0.5),
        'ffn_conv_b': nrm((DEPTH, D_FF), 0.01),
        'ffn_w_down': nrm((DEPTH, D_FF, D), BETA * D_FF ** -0.5),
    }


def reference(x, c, rel_bias, ev_w_in, mla_q_norm, mla_kv_norm, mla_w_uq, mla_w_ukv,
              nsa_cmp_pe, nsa_cmp_w1, nsa_cmp_w2, ev_w_o, od_w_in, diff_lambda, diff_subln, od_w_o,
              ada_w, ada_b, ln_g, ln_b, ffn_w_up, ffn_w_gate, ffn_conv_w, ffn_conv_b, ffn_w_down):
    S = x.shape[1]
    cos, sin = _rope_tables(S, MLA_ROPE)
    c_act = jax.nn.silu(c)
    for l in range(DEPTH):
        mod = c_act @ ada_w[l] + ada_b[l]
        sh1, sc1, g1, sh2, sc2, g2 = [m[:, None, :] for m in jnp.split(mod, 6, axis=-1)]
        i = l // 2
        h = x * (1.0 + sc1) + sh1
        if l % 2 == 0:
            proj = h @ ev_w_in[i]
            o_a = _mla(proj[..., :MLA_COLS], mla_q_norm[i], mla_kv_norm[i], mla_w_uq[i], mla_w_ukv[i], cos, sin)
            o_b = _nsa(proj[..., MLA_COLS:], nsa_cmp_pe[i], nsa_cmp_w1[i], nsa_cmp_w2[i], rel_bias)
            y = jnp.concatenate([o_a, o_b], axis=-1) @ ev_w_o[i]
        else:
            y = _diff_attention(h @ od_w_in[i], diff_lambda[i], diff_subln[i], rel_bias, l) @ od_w_o[i]
        x = _layer_norm(ALPHA * x + (1.0 + g1) * y, ln_g[l, 0], ln_b[l, 0])
        h = x * (1.0 + sc2) + sh2
        y = _conv_ffn(h, ffn_w_up[l], ffn_w_gate[l], ffn_conv_w[l], ffn_conv_b[l], ffn_w_down[l])
        x = _layer_norm(ALPHA * x + (1.0 + g2) * y, ln_g[l, 1], ln_b[l, 1])
    return x
```

```python
import numpy as np
import math
import contextlib
from concourse.bass_utils import run_bass_kernel_spmd
import concourse.bass as bass
import concourse.mybir as mybir

F32 = mybir.dt.float32
BF16 = mybir.dt.bfloat16
ALU = mybir.AluOpType
AF = mybir.ActivationFunctionType
AX = mybir.AxisListType

SEM_LIM = 30000
NSLOT = 12


class Tok:
    __slots__ = ("w", "r", "const")

    def __init__(self, const=False):
        self.w = None
        self.r = []
        self.const = const


class Ins:
    __slots__ = ("eng", "fn", "deps", "dma", "pos", "sig", "slot", "slotcnt", "signo")


class Prog:
    ENGS = ["pe", "act", "dve", "pool", "sp"]

    def __init__(self, nc):
        self.nc = nc
        self.instrs = []
        self.ndma = {e: 0 for e in self.ENGS}
        self.slot_last = {}
        self.stack = contextlib.ExitStack()
        self.last = {}
        self.pending_bar = {}

    def sbuf(self, name, shape, dt):
        return self.stack.enter_context(self.nc.sbuf_tensor(name, list(shape), dt))

    def psum(self, name, shape, dt):
        return self.stack.enter_context(self.nc.psum_tensor(name, list(shape), dt))

    def op(self, eng, fn, reads=(), writes=(), dma=False):
        ins = Ins()
        ins.eng = eng
        ins.fn = fn
        ins.dma = dma
        ins.sig = False
        ins.signo = 0
        deps = set()
        for t in reads:
            if t.w is not None:
                deps.add(t.w)
        for t in writes:
            if t.w is not None:
                deps.add(t.w)
            deps.update(t.r)
        for t in reads:
            if not t.const:
                t.r.append(ins)
        for t in writes:
            t.w = ins
            t.r = []
        if dma:
            n = self.ndma[eng]
            self.ndma[eng] = n + 1
            ins.slot = n % NSLOT
            ins.slotcnt = n // NSLOT + 1
            prev = self.slot_last.get((eng, ins.slot))
            if prev is not None:
                deps.add(prev)
            self.slot_last[(eng, ins.slot)] = ins
        pb = self.pending_bar.pop(eng, None)
        if pb:
            deps.update(pb)
        deps.discard(ins)
        ins.deps = deps
        self.instrs.append(ins)
        self.last[eng] = ins
        return ins

    def barrier(self):
        bar = list(self.last.values()) + list(self.slot_last.values())
        for e in self.ENGS:
            self.pending_bar[e] = list(bar)

    def emit(self, final_waits=()):
        nc = self.nc
        per = {e: [] for e in self.ENGS}
        for ins in self.instrs:
            ins.pos = len(per[ins.eng])
            per[ins.eng].append(ins)

        def needs_wait(ins, d):
            if d.dma:
                return True
            if d.eng == ins.eng:
                if d.eng == "pe":
                    return False
                if ins.dma:
                    return True
                return (ins.pos - d.pos) <= 3
            return True

        for ins in self.instrs:
            best = {}
            for d in ins.deps:
                if not d.dma and needs_wait(ins, d):
                    if d.eng not in best or best[d.eng].pos < d.pos:
                        best[d.eng] = d
            nd = set(d for d in ins.deps if d.dma)
            for d in best.values():
                d.sig = True
                nd.add(d)
            ins.deps = nd
        for e in self.ENGS:
            n = 0
            for ins in per[e]:
                if ins.sig and not ins.dma:
                    n += 1
                    ins.signo = n
        nsig = {e: max([i.signo for i in per[e]] + [0]) for e in self.ENGS}
        esems = {}
        for e in self.ENGS:
            nep = (nsig[e] + SEM_LIM - 1) // SEM_LIM
            esems[e] = [self.stack.enter_context(nc.semaphore(f"s_{e}_{k}")) for k in range(max(nep, 1))]
        ssems = {}
        for e in self.ENGS:
            if self.ndma[e]:
                ssems[e] = [self.stack.enter_context(nc.semaphore(f"d_{e}_{k}")) for k in range(NSLOT)]
                assert (self.ndma[e] // NSLOT + 1) * 16 < 65000, "too many dmas on queue"
        self.stats = {e: [len(per[e]), nsig[e], self.ndma[e]] for e in self.ENGS}
        nwaits = {e: 0 for e in self.ENGS}

        def run(ename, handle):
            waited = {}
            for ins in per[ename]:
                need = {}
                for d in ins.deps:
                    if not needs_wait(ins, d):
                        continue
                    if d.dma:
                        key = ("s", d.eng, d.slot)
                        v = d.slotcnt * 16
                    else:
                        key = ("e", d.eng)
                        v = d.signo
                    if waited.get(key, 0) >= v:
                        continue
                    if need.get(key, 0) < v:
                        need[key] = v
                for key, v in need.items():
                    waited[key] = v
                    nwaits[ename] += 1
                    if key[0] == "s":
                        handle.wait_ge(ssems[key[1]][key[2]], v)
                    else:
                        ep = (v - 1) // SEM_LIM
                        handle.wait_ge(esems[key[1]][ep], (v - 1) % SEM_LIM + 1)
                bi = ins.fn(handle)
                if ins.dma:
                    bi.then_inc(ssems[ename][ins.slot], 16)
                elif ins.sig:
                    ep = (ins.signo - 1) // SEM_LIM
                    bi.then_inc(esems[ename][ep], 1)
            if ename == "sp":
                for e in self.ENGS:
                    if self.ndma[e]:
                        for s in range(NSLOT):
                            last = self.slot_last.get((e, s))
                            if last is not None:
                                handle.wait_ge(ssems[e][s], last.slotcnt * 16)

        with nc.Block() as block:
            @block.tensor
            def _(e):
                run("pe", e)

            @block.scalar
            def _(e):
                run("act", e)

            @block.vector
            def _(e):
                run("dve", e)

            @block.gpsimd
            def _(e):
                run("pool", e)

            @block.sync
            def _(e):
                run("sp", e)
        self.stats["waits"] = nwaits
        self.stack.close()


S = 2048
D = 1024
DEPTH = 4
DFF = 2816
NFC = DFF // 128
NT = S // 128
ALPHA = (2.0 * DEPTH) ** 0.25
LN_EPS = 1e-5
RMS_EPS = 1e-6
NEG = -30000.0
N_ATT = 2 * 128 * 128
N_CMP = 128 * 247
N_OH = N_ATT + N_CMP
AR_EL = 60 * 1024


def T(const=False):
    return Tok(const)


def t5_bucket_np(dist):
    n = np.maximum(dist, 0)
    nf = np.maximum(n, 1).astype(np.float32)
    large = 16 + (np.log(nf / np.float32(16)) / np.float32(math.log(128 / 16)) * np.float32(16)).astype(np.int32)
    large = np.minimum(large, 31)
    return np.where(n < 16, n, large)


def onehot_consts():
    oh = np.zeros((33, N_OH), np.float32)
    tau, k, q = np.meshgrid(np.arange(2), np.arange(128), np.arange(128), indexing="ij")
    dist = (q - k + 128 * tau).reshape(-1)
    col = np.arange(N_ATT)
    b = t5_bucket_np(dist)
    pos = dist >= 0
    np.add.at(oh, (b[pos], col[pos]), 1.0)
    np.add.at(oh, (np.full(pos.sum(), 31), col[pos]), -1.0)
    oh[32, col[~pos]] = 1.0
    p, m = np.meshgrid(np.arange(128), np.arange(247), indexing="ij")
    dist = (p - 16 * (m - 120) - 31).reshape(-1)
    col = N_ATT + np.arange(N_CMP)
    b = t5_bucket_np(dist)
    pos = dist >= 0
    np.add.at(oh, (b[pos], col[pos]), 1.0)
    np.add.at(oh, (np.full(pos.sum(), 31), col[pos]), -1.0)
    oh[32, col[~pos]] = 1.0
    return oh


class K:
    def __init__(self, nseq, plan, dbg=False, nsa_only=None):
        self.nsa_only = nsa_only
        self.nseq = nseq
        self.plan = plan
        self.dbg = dbg
        nc = self.nc = bass.Bass("TRN2", target_bir_lowering=False)
        self.P = Prog(nc)
        self.dram_in = {}
        self.dbg_names = set()
        self.build()

    def din(self, name, shape, dt=F32):
        t = self.nc.dram_tensor(name, list(shape), dt, kind="ExternalInput")
        self.dram_in[name] = (tuple(shape), dt)
        return t.ap()

    def dscr(self, name, shape, dt):
        return self.nc.dram_tensor(name, list(shape), dt, kind="Internal").ap()

    def dbg_out(self, name, src_ap, shape, dt, reads):
        if not self.dbg or name in self.dbg_names:
            return
        self.dbg_names.add(name)
        t = self.nc.dram_tensor(name, list(shape), dt, kind="ExternalOutput").ap()
        self.dma("sp", t, src_ap, reads=reads, writes=[Tok()])

    def dma(self, q, out, in_, reads=(), writes=()):
        return self.P.op(q, lambda e, o=out, i=in_: e.dma_start(out=o, in_=i), reads, writes, dma=True)

    def view(self, off, shape, dt):
        n = int(np.prod(shape[1:]))
        if dt == BF16:
            ap = self.ar[:, off:off + n]
        else:
            assert off % 2 == 0
            ap = self.ar[:, off:off + 2 * n].bitcast(F32)
        if len(shape) == 3:
            ap = ap.rearrange("p (a b) -> p a b", a=shape[1])
        elif len(shape) == 4:
            ap = ap.rearrange("p (a b c) -> p a b c", a=shape[1], b=shape[2])
        return ap

    def build(self):
        nc, P = self.nc, self.P
        nseq = self.nseq
        plan = self.plan
        layers = sorted(set(l for (l, s) in plan))
        has = lambda l, s: (l, s) in plan
        x_in = self.din("x", [nseq, S, D])
        cT = self.din("cT", [128, 8, nseq])
        out = nc.dram_tensor("out", [nseq, S, D], F32, kind="ExternalOutput").ap()
        ident_d = self.din("ident", [128, 128])
        ada_w = self.din("ada_w", [DEPTH, D, 6 * D])
        ada_b = self.din("ada_b", [DEPTH, 6 * D])
        ln_g = self.din("ln_g", [DEPTH, 2, D])
        ln_b = self.din("ln_b", [DEPTH, 2, D])
        wu_f = self.din("wu", [DEPTH, NFC, 128, 8, 128])
        wg_f = self.din("wg", [DEPTH, NFC, 128, 8, 128])
        wd_f = self.din("wd", [DEPTH, 128, NFC, D])
        cw_d = self.din("convp", [DEPTH, DFF, 4])
        relb = self.din("rel_bias", [32, 8])
        oh_d = self.din("onehot", [33, N_OH])
        odqk_f = self.din("od_wqk", [2, 16, 128, 8, 128])
        odv_f = self.din("od_wv", [2, 128, 8, D])
        odo_f = self.din("od_wo", [2, 128, 8, D])
        dlam = self.din("diff_lambda", [2, 256])
        dsub = self.din("diff_subln", [2, 128, 1])
        ev_specs = {"ev_wc": [15, 128, 8 * 128], "ev_wvn": [128, 8 * 256], "ev_uq": [8, 128, 3 * 128], "ev_ukvk": [8, 128, 2 * 128],
                    "ev_ukvv": [128, 2 * 512], "ev_w1": [2, 128, 32 * 256], "ev_w2k": [128, 2 * 2 * 128], "ev_w2v": [128, 2 * 64],
                    "ev_peT": [2, 64, 32], "ev_wo": [128, 8 * D]}
        ev_f = {k: self.din(k, [2] + v) for k, v in ev_specs.items()}
        ev_b = {k: self.dscr(k + "_b", [2] + v, BF16) for k, v in ev_specs.items()}
        ev_qn = self.din("ev_qn", [2, 128, 3])
        ev_kvn = self.din("ev_kvn", [2, 128, 2])
        rope_d = self.din("ropeT", [64, S])
        cmask_d = self.din("cmask", [2, 128, 128])
        exp_d = self.din("expmat", [64, 16 * 128])
        ovl_d = self.din("ovl", [127, 32])
        scab_d = self.din("scab", [2, 128, NT * 32])
        selb_d = self.din("selb", [24, 24 * 128])
        cst_b = {"expmat": self.dscr("expmat_b", [64, 16 * 128], BF16), "selb": self.dscr("selb_b", [24, 24 * 128], BF16),
                 "scab": self.dscr("scab_b", [2, 128, NT * 32], BF16)}
        t_cstb = T()
        wu_b = self.dscr("wu_b", [DEPTH, NFC, 128, 8 * 128], BF16)
        wg_b = self.dscr("wg_b", [DEPTH, NFC, 128, 8 * 128], BF16)
        wd_b = self.dscr("wd_b", [DEPTH, 128, NFC * D], BF16)
        odqk_b = self.dscr("odqk_b", [2, 16, 128, 8 * 128], BF16)
        odv_b = self.dscr("odv_b", [2, 128, 8 * D], BF16)
        odo_b = self.dscr("odo_b", [2, 128, 8 * D], BF16)
        mod_d = self.dscr("mod", [DEPTH, nseq, 6 * D], F32)
        G_d = self.dscr("Gd", [8, N_OH], F32)
        self.out = out

        ps = P.psum("ps", [128, 8, 512], F32)
        ps_t = [T() for _ in range(8)]
        ident = P.sbuf("identb", [128, 128], BF16)
        ident_f = P.sbuf("identf", [128, 128], F32)
        ones_b = P.sbuf("onesb", [128, 128], BF16)
        ones_f = P.sbuf("onesf", [128, 128], F32)
        t_ident = T(const=True)
        t_ones = T(const=True)
        hT = P.sbuf("hT", [128, 8, S], BF16)
        t_hT = [T() for _ in range(NT)]
        NB = 5
        bc = P.sbuf("bc", [128, NB, D], F32)
        t_bc = [T() for _ in range(NB)]
        xt = [P.sbuf(f"xt{i}", [128, D], F32) for i in range(2)]
        t_xt = [T(), T()]
        wk = [P.sbuf(f"wk{i}", [128, D], F32) for i in range(2)]
        t_wk = [T() for _ in range(2)]
        hb = [P.sbuf(f"hb{i}", [128, D], BF16) for i in range(2)]
        t_hb = [T(), T()]
        st6 = P.sbuf("st6", [128, 2, 6], F32)
        mv = P.sbuf("mv", [128, 4], F32)
        t_st = T()
        t_mv = T()
        halo = P.sbuf("halo", [128, NFC, 2], F32)
        t_halo = [T() for _ in range(NFC)]
        cwt = P.sbuf("cwt", [128, NFC, 4], F32)
        t_cw = T()
        cact = P.sbuf("cact", [128, 8, nseq], F32)
        t_cact = T()
        t_adb = [T(), T()]
        t_modsb = [T(), T()]
        tbl = P.sbuf("tbl", [33, 8], F32)
        t_tbl = T()
        Bsb = P.sbuf("Bsb", [128, 8, 3, 128], BF16)
        Bc = P.sbuf("Bc", [128, 8, 247], BF16)
        t_B = T()
        lamt = P.sbuf("lamt", [128, 264], F32)
        t_lam = T()
        subl = P.sbuf("subl", [128, 2], F32)
        t_subl = T()
        eps_t = P.sbuf("eps_t", [128, 2], F32)
        t_eps = T(const=True)
        Md = P.sbuf("Md", [128, 128], BF16)
        ovl = P.sbuf("ovl_s", [127, 32], BF16)
        Zt = P.sbuf("Zt", [128, 512], BF16)
        qkn = P.sbuf("qkn", [128, 8], F32)
        t_qkn = T()
        t_cst = T(const=True)
        self.ar = P.sbuf("arena", [128, AR_EL], BF16)

        t_mod = [[T() for _ in range(nseq)] for _ in range(DEPTH)]
        t_x = [[T() for _ in range(NT)] for _ in range(nseq)]
        t_wub = [T() for _ in range(DEPTH)]
        t_wdb = [T() for _ in range(DEPTH)]
        t_odb = [T() for _ in range(2)]
        t_evb = [T() for _ in range(2)]
        t_G = T()

        def conv(dst, src, tok):
            P.op("pool", lambda e, d=dst, s=src: e.dma_start(out=d, in_=s), writes=[tok], dma=True)
        self.dma("sp", ident_f[:], ident_d[:, :], writes=[t_ident])
        P.op("pool", lambda e: e.dma_start(out=ident[:], in_=ident_d[:, :]), writes=[t_ident], dma=True)
        P.op("pool", lambda e: e.memset(ones_b[:], 1.0), writes=[t_ones])
        P.op("pool", lambda e: e.memset(ones_f[:], 1.0), writes=[t_ones])
        P.op("pool", lambda e: e.memset(eps_t[:, 0:1], RMS_EPS), writes=[t_eps])
        P.op("pool", lambda e: e.memset(eps_t[:, 1:2], LN_EPS), writes=[t_eps])
        for l in layers:
            if has(l, 1):
                for fc0 in range(0, NFC, 11):
                    conv(wu_b[l, fc0:fc0 + 11], wu_f[l, fc0:fc0 + 11].rearrange("f p k j -> f p (k j)"), t_wub[l])
                    conv(wg_b[l, fc0:fc0 + 11], wg_f[l, fc0:fc0 + 11].rearrange("f p k j -> f p (k j)"), t_wub[l])
                conv(wd_b[l], wd_f[l].rearrange("p f d -> p (f d)"), t_wdb[l])
            if has(l, 0) and l % 2 == 0:
                i = l // 2
                for k in ev_specs:
                    if k == "ev_wc":
                        for c0_ in range(0, 15, 5):
                            conv(ev_b[k][i, c0_:c0_ + 5], ev_f[k][i, c0_:c0_ + 5], t_evb[i])
                    else:
                        conv(ev_b[k][i], ev_f[k][i], t_evb[i])
            if has(l, 0) and l % 2 == 1:
                i = l // 2
                conv(odqk_b[i], odqk_f[i].rearrange("f p k j -> f p (k j)"), t_odb[i])
                conv(odv_b[i], odv_f[i].rearrange("p k n -> p (k n)"), t_odb[i])
                conv(odo_b[i], odo_f[i].rearrange("p k n -> p (k n)"), t_odb[i])

        if any(s_ == 0 and l_ % 2 == 0 for (l_, s_) in plan):
            P.op("pool", lambda e: e.dma_start(out=Md[:], in_=cmask_d[0]), writes=[t_cst], dma=True)
            conv(cst_b["expmat"], exp_d, t_cstb)
            conv(cst_b["selb"], selb_d, t_cstb)
            conv(cst_b["scab"], scab_d, t_cstb)
            P.op("pool", lambda e: e.dma_start(out=ovl[:], in_=ovl_d[:, :]), writes=[t_cst], dma=True)
        P.op("pool", lambda e: e.memset(Zt[:], 0.0), writes=[t_cst])
        adw = [self.view(i * 8192, [128, 8, 512], F32) for i in range(2)]
        adb = [self.view(32768 + i * 1024, [128, 512], F32)[0:nseq] for i in range(2)]
        modsb = [self.view(36864 + i * 1024, [128, 512], F32)[0:nseq] for i in range(2)]
        t_adw = [T(), T()]
        self.dma("sp", cact[:], cT[:, :, :], writes=[t_cact])
        P.op("act", lambda e: e.activation(out=cact[:], in_=cact[:], func=AF.Silu), reads=[t_cact], writes=[t_cact])
        ib = 0
        for l in layers:
            for cb in range(12):
                a = adw[ib % 2]
                ta = t_adw[ib % 2]
                ab, tab = adb[ib % 2], t_adb[ib % 2]
                mb, tmb = modsb[ib % 2], t_modsb[ib % 2]
                ib += 1
                self.dma("sp", ab, ada_b[l:l + 1, cb * 512:(cb + 1) * 512].broadcast_to([nseq, 512]), writes=[tab])
                self.dma("sp", a, ada_w[l, :, cb * 512:(cb + 1) * 512].rearrange("(k p) n -> p k n", p=128), writes=[ta])
                bank = 6 + (cb % 2)
                for kc in range(8):
                    P.op("pe", lambda e, a=a, kc=kc, bank=bank: e.matmul(
                        ps[0:nseq, bank, :], cact[:, kc, :], a[:, kc, :], start=(kc == 0), stop=(kc == 7)),
                        reads=[t_cact, ta], writes=[ps_t[bank]])
                P.op("dve", lambda e, mb=mb, ab=ab, bank=bank: e.tensor_tensor(
                    out=mb, in0=ps[0:nseq, bank, :], in1=ab, op=ALU.add),
                    reads=[ps_t[bank], tab], writes=[tmb])
                self.dma("sp", mod_d[l, :, cb * 512:(cb + 1) * 512], mb, reads=[tmb], writes=t_mod[l])
        if self.dbg:
            self.dbg_out("dbg_mod", mod_d[layers[0]], [nseq, 6 * D], F32, t_mod[layers[0]])

        need_bias = any(s == 0 for (l, s) in plan)
        if need_bias:
            P.barrier()
            self.dma("sp", tbl[0:32, :], relb[:, :], writes=[t_tbl])
            P.op("pool", lambda e: e.memset(tbl[32:33, :], NEG), writes=[t_tbl])
            ohb = [self.view(i * 8192, [33, 4096], F32) for i in range(2)]
            gsb_ = [self.view(16384 + i * 8192, [8, 4096], F32) for i in range(2)]
            t_oh = [T(), T()]
            t_g = [T(), T()]
            nch = (N_OH + 4095) // 4096
            for c in range(nch):
                c0 = c * 4096
                w = min(4096, N_OH - c0)
                o_, to = ohb[c % 2], t_oh[c % 2]
                g_, tg = gsb_[c % 2], t_g[c % 2]
                self.dma("sp", o_[0:33, 0:w], oh_d[:, c0:c0 + w], writes=[to])
                for s0 in range(0, w, 512):
                    sw = min(512, w - s0)
                    bank = (s0 // 512) % 2
                    P.op("pe", lambda e, o_=o_, s0=s0, sw=sw, bank=bank: e.matmul(
                        ps[0:8, bank, 0:sw], tbl[0:33, :], o_[0:33, s0:s0 + sw], start=True, stop=True),
                        reads=[t_tbl, to], writes=[ps_t[bank]])
                    P.op("act", lambda e, g_=g_, s0=s0, sw=sw, bank=bank: e.copy(out=g_[0:8, s0:s0 + sw], in_=ps[0:8, bank, 0:sw]),
                         reads=[ps_t[bank]], writes=[tg])
                self.dma("sp", G_d[:, c0:c0 + w], g_[0:8, 0:w], reads=[tg], writes=[t_G])
            for tau in range(2):
                P.op("pool", lambda e, tau=tau: e.dma_start(
                    out=Bsb[:, :, tau, :], in_=G_d[:, tau * 16384:(tau + 1) * 16384].rearrange("h (k q) -> k h q", k=128)),
                    reads=[t_G], writes=[t_B], dma=True)
            P.op("pool", lambda e: e.dma_start(out=Bc[:], in_=G_d[:, N_ATT:N_OH].rearrange("h (p m) -> p h m", p=128)),
                 reads=[t_G], writes=[t_B], dma=True)
            for h_ in range(8):
                P.op("pool", lambda e, h_=h_: e.dma_start(out=Bsb[:, h_, 2, :], in_=cmask_d[1]), writes=[t_B], dma=True)
            self.dbg_out("dbg_Bsb", Bsb[:], [128, 8, 3, 128], BF16, [t_B])
            self.dbg_out("dbg_Bc", Bc[:], [128, 8, 247], BF16, [t_B])
        P.barrier()

        def bc_load(slot, src, rd, plus_one):
            self.dma("sp", bc[:, slot, :], src.broadcast_to([128, D]), reads=rd, writes=[t_bc[slot]])
            if plus_one:
                P.op("pool", lambda e, slot=slot: e.tensor_scalar_add(out=bc[:, slot, :], in0=bc[:, slot, :], scalar1=1.0),
                     reads=[t_bc[slot]], writes=[t_bc[slot]])

        def emit_hT(b, l, sub, xsrc):
            o = 0 if sub == 0 else 3
            bc_load(0, mod_d[l, b:b + 1, (o + 1) * D:(o + 2) * D], [t_mod[l][b]], True)
            bc_load(1, mod_d[l, b:b + 1, (o + 0) * D:(o + 1) * D], [t_mod[l][b]], False)
            for t in range(NT):
                xi = t % 2
                self.dma("sp", xt[xi][:], xsrc[b, t * 128:(t + 1) * 128, :], reads=[t_x[b][t]], writes=[t_xt[xi]])
                P.op("dve", lambda e, xi=xi: e.tensor_tensor(out=wk[0][:], in0=xt[xi][:], in1=bc[:, 0, :], op=ALU.mult),
                     reads=[t_xt[xi], t_bc[0]], writes=[t_wk[0]])
                P.op("pool", lambda e, xi=xi: e.tensor_tensor(out=hb[xi][:], in0=wk[0][:], in1=bc[:, 1, :], op=ALU.add),
                     reads=[t_wk[0], t_bc[1]], writes=[t_hb[xi]])
                psb = ps[:, 7, :].bitcast(BF16)
                for kc in range(8):
                    P.op("pe", lambda e, xi=xi, kc=kc, psb=psb: e.transpose(
                        psb[:, kc * 128:(kc + 1) * 128], hb[xi][:, kc * 128:(kc + 1) * 128], ident[:]),
                        reads=[t_hb[xi], t_ident], writes=[ps_t[7]])
                P.op("act", lambda e, t=t, psb=psb: e.copy(
                    out=hT[:, :, t * 128:(t + 1) * 128], in_=psb.rearrange("p (k j) -> p k j", k=8)),
                    reads=[ps_t[7]], writes=[t_hT[t]])

        def epilogue(b, l, sub, t, ybank, xsrc):
            xi = t % 2
            self.dma("sp", xt[xi][:], xsrc[b, t * 128:(t + 1) * 128, :], reads=[t_x[b][t]], writes=[t_xt[xi]])
            yv = ps[:, ybank:ybank + 2, :].rearrange("p a n -> p (a n)")
            P.op("dve", lambda e: e.tensor_tensor(out=wk[0][:], in0=yv, in1=bc[:, 2, :], op=ALU.mult),
                 reads=[ps_t[ybank], ps_t[ybank + 1], t_bc[2]], writes=[t_wk[0]])
            P.op("dve", lambda e, xi=xi: e.scalar_tensor_tensor(out=wk[1][:], in0=xt[xi][:], scalar=ALPHA, in1=wk[0][:],
                                                               op0=ALU.mult, op1=ALU.add),
                 reads=[t_xt[xi], t_wk[0]], writes=[t_wk[1]])
            for hh in range(2):
                P.op("dve", lambda e, hh=hh: e.bn_stats(out=st6[:, hh, :], in_=wk[1][:, hh * 512:(hh + 1) * 512]),
                     reads=[t_wk[1]], writes=[t_st])
            P.op("dve", lambda e: e.bn_aggr(out=mv[:, 0:2], in_=st6[:]), reads=[t_st], writes=[t_mv])
            P.op("act", lambda e: e.activation(out=mv[:, 2:3], in_=mv[:, 1:2], func=AF.Sqrt, bias=eps_t[:, 1:2], scale=1.0),
                 reads=[t_mv, t_eps], writes=[t_mv])
            P.op("dve", lambda e: e.reciprocal(out=mv[:, 2:3], in_=mv[:, 2:3]), reads=[t_mv], writes=[t_mv])
            P.op("dve", lambda e: e.scalar_tensor_tensor(out=mv[:, 3:4], in0=mv[:, 0:1], scalar=-1.0, in1=mv[:, 2:3],
                                                         op0=ALU.mult, op1=ALU.mult), reads=[t_mv], writes=[t_mv])
            P.op("act", lambda e: e.activation(out=wk[0][:], in_=wk[1][:], func=AF.Identity, bias=mv[:, 3:4], scale=mv[:, 2:3]),
                 reads=[t_wk[1], t_mv], writes=[t_wk[0]])
            P.op("pool", lambda e: e.tensor_tensor(out=wk[0][:], in0=wk[0][:], in1=bc[:, 3, :], op=ALU.mult),
                 reads=[t_wk[0], t_bc[3]], writes=[t_wk[0]])
            P.op("pool", lambda e, xi=xi: e.tensor_tensor(out=xt[xi][:], in0=wk[0][:], in1=bc[:, 4, :], op=ALU.add),
                 reads=[t_wk[0], t_bc[4]], writes=[t_xt[xi]])
            self.dma("sp", out[b, t * 128:(t + 1) * 128, :], xt[xi][:], reads=[t_xt[xi]], writes=[t_x[b][t]])

        def epi_setup(b, l, sub):
            o = 2 if sub == 0 else 5
            bc_load(2, mod_d[l, b:b + 1, o * D:(o + 1) * D], [t_mod[l][b]], True)
            bc_load(3, ln_g[l, sub:sub + 1, :], [], False)
            bc_load(4, ln_b[l, sub:sub + 1, :], [], False)

        def ffn(b, l, xsrc):
            P.barrier()
            wd = self.view(0, [128, NFC, D], BF16)
            t_wd = T()
            actT = self.view(22528, [128, NFC, 512], BF16)
            t_act = [T() for _ in range(NFC)]
            wug = [self.view(33792 + i * 2048, [128, 2, 8, 128], BF16) for i in range(2)]
            t_wug = [T(), T()]
            gsb = [self.view(37888 + i * 1032, [128, 514], F32) for i in range(2)]
            t_gsb = [T(), T()]
            tmp = [self.view(39952 + i * 1024, [128, 512], F32) for i in range(3)]
            t_tmp = [T() for _ in range(3)]
            emit_hT(b, l, 1, xsrc)
            epi_setup(b, l, 1)
            self.dma("sp", wd, wd_b[l].rearrange("p (f d) -> p f d", f=NFC), reads=[t_wdb[l]], writes=[t_wd])
            self.dma("sp", cwt[:], cw_d[l].rearrange("(f p) k -> p f k", p=128), writes=[t_cw])
            iw = 0
            for tb in range(4):
                c0 = tb * 512
                hts = [t_hT[tb * 4 + i] for i in range(4)]
                for fc in range(NFC):
                    w = wug[iw % 2]
                    tw = t_wug[iw % 2]
                    self.dma("sp", w[:, 0], wu_b[l, fc].rearrange("p (k j) -> p k j", k=8), reads=[t_wub[l]], writes=[tw])
                    self.dma("sp", w[:, 1], wg_b[l, fc].rearrange("p (k j) -> p k j", k=8), reads=[t_wub[l]], writes=[tw])
                    bu = iw % 2
                    bg = 2 + iw % 2
                    g = gsb[iw % 2]
                    tg = t_gsb[iw % 2]
                    iw += 1
                    for kc in range(8):
                        P.op("pe", lambda e, w=w, kc=kc, bu=bu, c0=c0: e.matmul(ps[:, bu, :], w[:, 0, kc, :], hT[:, kc, c0:c0 + 512],
                                                                               start=(kc == 0), stop=(kc == 7)),
                             reads=[tw] + hts, writes=[ps_t[bu]])
                    for kc in range(8):
                        P.op("pe", lambda e, w=w, kc=kc, bg=bg, c0=c0: e.matmul(ps[:, bg, :], w[:, 1, kc, :], hT[:, kc, c0:c0 + 512],
                                                                               start=(kc == 0), stop=(kc == 7)),
                             reads=[tw] + hts, writes=[ps_t[bg]])
                    if tb == 0:
                        P.op("pool", lambda e, g=g: e.memset(g[:, 0:2], 0.0), writes=[tg])
                    else:
                        P.op("pool", lambda e, g=g, fc=fc: e.tensor_copy(out=g[:, 0:2], in_=halo[:, fc, :]),
                             reads=[t_halo[fc]], writes=[tg])
                    P.op("act", lambda e, g=g, bg=bg: e.copy(out=g[:, 2:514], in_=ps[:, bg, :]), reads=[ps_t[bg]], writes=[tg])
                    P.op("pool", lambda e, g=g, fc=fc: e.tensor_copy(out=halo[:, fc, :], in_=g[:, 512:514]),
                         reads=[tg], writes=[t_halo[fc]])
                    P.op("dve", lambda e, g=g, fc=fc: e.tensor_scalar(out=tmp[0], in0=g[:, 2:514], scalar1=cwt[:, fc, 2:3],
                                                                      scalar2=cwt[:, fc, 3:4], op0=ALU.mult, op1=ALU.add),
                         reads=[tg, t_cw], writes=[t_tmp[0]])
                    P.op("dve", lambda e, g=g, fc=fc: e.scalar_tensor_tensor(out=tmp[1], in0=g[:, 1:513], scalar=cwt[:, fc, 1:2],
                                                                             in1=tmp[0], op0=ALU.mult, op1=ALU.add),
                         reads=[tg, t_cw, t_tmp[0]], writes=[t_tmp[1]])
                    P.op("dve", lambda e, g=g, fc=fc: e.scalar_tensor_tensor(out=tmp[2], in0=g[:, 0:512], scalar=cwt[:, fc, 0:1],
                                                                             in1=tmp[1], op0=ALU.mult, op1=ALU.add),
                         reads=[tg, t_cw, t_tmp[1]], writes=[t_tmp[2]])
                    P.op("act", lambda e: e.activation(out=tmp[0], in_=tmp[2], func=AF.Silu),
                         reads=[t_tmp[2]], writes=[t_tmp[0]])
                    P.op("dve", lambda e, fc=fc, bu=bu: e.tensor_tensor(out=actT[:, fc, :], in0=tmp[0], in1=ps[:, bu, :], op=ALU.mult),
                         reads=[t_tmp[0], ps_t[bu]], writes=[t_act[fc]])
                for tt in range(4):
                    t = tb * 4 + tt
                    for dh in range(2):
                        for fc in range(NFC):
                            P.op("pe", lambda e, fc=fc, tt=tt, dh=dh: e.matmul(
                                ps[:, 4 + dh, :], actT[:, fc, tt * 128:(tt + 1) * 128], wd[:, fc, dh * 512:(dh + 1) * 512],
                                start=(fc == 0), stop=(fc == NFC - 1)),
                                reads=[t_act[fc], t_wd], writes=[ps_t[4 + dh]])
                    epilogue(b, l, 1, t, 4, xsrc)

        def diff_attn(b, l, xsrc):
            i = l // 2
            lam_init = 0.8 - 0.6 * math.exp(-0.3 * l)
            P.barrier()
            V = self.view(0, [128, NT, D], BF16)
            t_V = [T() for _ in range(NT)]
            oT = self.view(16384, [128, 8, S], BF16)
            t_oT = [[T() for _ in range(4)] for _ in range(8)]
            qk = [[self.view(32768 + (2 * j + c) * 2048, [128, S], BF16) for c in range(2)] for j in range(2)]
            t_qk = [[T(), T()], [T(), T()]]
            E = [self.view(40960 + j * 512, [128, 512], BF16) for j in range(4)]
            t_E = [T() for _ in range(4)]
            wo = self.view(43008, [128, 8, D], BF16)
            t_wo = T()
            wqk = [self.view(51200 + j * 2048, [128, 2, 8, 128], BF16) for j in range(2)]
            t_wqk = [T(), T()]
            f32w = [self.view(55296 + j * 1024, [128, 512], F32) for j in range(4)]
            t_f = [T() for _ in range(4)]

            emit_hT(b, l, 0, xsrc)
            epi_setup(b, l, 0)
            self.dma("sp", lamt[:, 0:256], dlam[i:i + 1, :].broadcast_to([128, 256]), writes=[t_lam])
            P.op("dve", lambda e: e.tensor_tensor(out=lamt[:, 0:64], in0=lamt[:, 0:64], in1=lamt[:, 64:128], op=ALU.mult),
                 reads=[t_lam], writes=[t_lam])
            P.op("dve", lambda e: e.tensor_tensor(out=lamt[:, 128:192], in0=lamt[:, 128:192], in1=lamt[:, 192:256], op=ALU.mult),
                 reads=[t_lam], writes=[t_lam])
            P.op("dve", lambda e: e.reduce_sum(out=lamt[:, 256:257], in_=lamt[:, 0:64], axis=AX.X), reads=[t_lam], writes=[t_lam])
            P.op("dve", lambda e: e.reduce_sum(out=lamt[:, 257:258], in_=lamt[:, 128:192], axis=AX.X), reads=[t_lam], writes=[t_lam])
            P.op("act", lambda e: e.activation(out=lamt[:, 258:260], in_=lamt[:, 256:258], func=AF.Exp), reads=[t_lam], writes=[t_lam])
            P.op("dve", lambda e: e.scalar_tensor_tensor(out=lamt[:, 260:261], in0=lamt[:, 259:260], scalar=-lam_init,
                                                         in1=lamt[:, 258:259], op0=ALU.add, op1=ALU.subtract),
                 reads=[t_lam], writes=[t_lam])
            self.dma("sp", subl[:, 0:1], dsub[i], writes=[t_subl])
            P.op("pool", lambda e: e.tensor_scalar_mul(out=subl[:, 0:1], in0=subl[:, 0:1], scalar1=1.0 - lam_init),
                 reads=[t_subl], writes=[t_subl])
            self.dma("sp", wo, odv_b[i].rearrange("p (k n) -> p k n", k=8), reads=[t_odb[i]], writes=[t_wo])
            for t in range(NT):
                for dh in range(2):
                    bank = 5 + dh
                    for kc in range(8):
                        P.op("pe", lambda e, t=t, dh=dh, kc=kc, bank=bank: e.matmul(
                            ps[:, bank, :], hT[:, kc, t * 128:(t + 1) * 128], wo[:, kc, dh * 512:(dh + 1) * 512],
                            start=(kc == 0), stop=(kc == 7)), reads=[t_hT[t], t_wo], writes=[ps_t[bank]])
                    if dh == 0:
                        P.op("act", lambda e, t=t, bank=bank: e.copy(out=V[:, t, 0:512], in_=ps[:, bank, :]),
                             reads=[ps_t[bank]], writes=[t_V[t]])
                    else:
                        P.op("dve", lambda e, t=t, bank=bank: e.tensor_copy(out=V[:, t, 512:1024], in_=ps[:, bank, :]),
                             reads=[ps_t[bank]], writes=[t_V[t]])
            def hq_block(h, Q, qT, kT, tq, tk, sidx):
                nk = 4 * Q + 4
                tiles = [(kt, m) for kt in range(nk) for m in range(2)]
                LAG = 2
                pend = []

                def do_scores(kt, m, sidx):
                    jj = kt - 4 * Q
                    c0 = max(0, jj) * 128
                    sb = sidx % 3
                    eb = sidx % 4
                    q0 = Q * 512
                    hasb = kt >= 4 * Q - 1
                    P.op("pe", lambda e: e.matmul(
                        ps[:, sb, c0:512], kT[m * 64:(m + 1) * 64, kt * 128:(kt + 1) * 128],
                        qT[m * 64:(m + 1) * 64, q0 + c0:q0 + 512], start=True, stop=not hasb),
                        reads=[tq, tk], writes=[ps_t[sb]])
                    if hasb:
                        if jj < 0:
                            rhs = Bsb[:, h, 1, :]
                            cs, ce = 0, 128
                        elif jj == 3:
                            rhs = Bsb[:, h, 0, :]
                            cs, ce = 384, 512
                        else:
                            rhs = Bsb[:, h, 0:2, :].rearrange("p t q -> p (t q)")
                            cs, ce = c0, c0 + 256
                        P.op("pe", lambda e: e.matmul(ps[:, sb, cs:ce], ident[:], rhs, start=False, stop=True),
                             reads=[t_ident, t_B], writes=[ps_t[sb]])
                    P.op("act", lambda e: e.activation(out=E[eb][:, c0:512], in_=ps[:, sb, c0:512], func=AF.Exp),
                         reads=[ps_t[sb]], writes=[t_E[eb]])
                    return (kt, m, c0, eb)

                def do_pv(kt, m, c0, eb):
                    first = (kt == 0)
                    last = (kt == nk - 1)
                    P.op("pe", lambda e: e.matmul(ps[:, 3 + 2 * m, c0:512], V[:, kt, h * 128:(h + 1) * 128], E[eb][:, c0:512],
                                                  start=first, stop=last),
                         reads=[t_V[kt], t_E[eb]], writes=[ps_t[3 + 2 * m]])
                    P.op("pe", lambda e: e.matmul(ps[:, 4 + 2 * m, c0:512], ones_b[:], E[eb][:, c0:512],
                                                  start=first, stop=last),
                         reads=[t_ones, t_E[eb]], writes=[ps_t[4 + 2 * m]])

                for (kt, m) in tiles:
                    pend.append(do_scores(kt, m, sidx))
                    sidx += 1
                    if len(pend) > LAG:
                        do_pv(*pend.pop(0))
                while pend:
                    do_pv(*pend.pop(0))
                r0, r1, u0, u1 = f32w
                P.op("dve", lambda e: e.reciprocal(out=r0, in_=ps[:, 4, :]), reads=[ps_t[4]], writes=[t_f[0]])
                P.op("dve", lambda e: e.reciprocal(out=r1, in_=ps[:, 6, :]), reads=[ps_t[6]], writes=[t_f[1]])
                P.op("dve", lambda e: e.tensor_tensor(out=u0, in0=ps[:, 3, :], in1=r0, op=ALU.mult),
                     reads=[ps_t[3], t_f[0]], writes=[t_f[2]])
                P.op("dve", lambda e: e.tensor_tensor(out=u1, in0=ps[:, 5, :], in1=r1, op=ALU.mult),
                     reads=[ps_t[5], t_f[1]], writes=[t_f[3]])
                P.op("dve", lambda e: e.scalar_tensor_tensor(out=u0, in0=u1, scalar=lamt[:, 260:261], in1=u0,
                                                             op0=ALU.mult, op1=ALU.add),
                     reads=[t_f[3], t_f[2], t_lam], writes=[t_f[2]])
                P.op("act", lambda e: e.activation(out=r0, in_=u0, func=AF.Square), reads=[t_f[2]], writes=[t_f[0]])
                P.op("pe", lambda e: e.matmul(ps[:, 7, :], ones_f[:], r0, start=True, stop=True),
                     reads=[t_ones, t_f[0]], writes=[ps_t[7]])
                P.op("act", lambda e: e.activation(out=r1, in_=ps[:, 7, :], func=AF.Sqrt, bias=eps_t[:, 0:1], scale=1.0 / 128.0),
                     reads=[ps_t[7], t_eps], writes=[t_f[1]])
                P.op("dve", lambda e: e.reciprocal(out=r1, in_=r1), reads=[t_f[1]], writes=[t_f[1]])
                P.op("dve", lambda e: e.scalar_tensor_tensor(out=oT[:, h, Q * 512:(Q + 1) * 512], in0=u0, scalar=subl[:, 0:1], in1=r1,
                                                             op0=ALU.mult, op1=ALU.mult),
                     reads=[t_f[2], t_f[1], t_subl], writes=[t_oT[h][Q]])
                return sidx

            sidx = 0
            for h in range(8):
                j2 = h % 2
                w = wqk[j2]
                tw = t_wqk[j2]
                qT, kT = qk[j2]
                tq, tk = t_qk[j2]
                self.dma("sp", w[:, 0], odqk_b[i, h].rearrange("p (k j) -> p k j", k=8), reads=[t_odb[i]], writes=[tw])
                self.dma("sp", w[:, 1], odqk_b[i, 8 + h].rearrange("p (k j) -> p k j", k=8), reads=[t_odb[i]], writes=[tw])
                for c in range(2):
                    for tb in range(4):
                        for kc in range(8):
                            P.op("pe", lambda e, w=w, c=c, kc=kc, tb=tb: e.matmul(
                                ps[:, 7, :], w[:, c, kc, :], hT[:, kc, tb * 512:(tb + 1) * 512], start=(kc == 0), stop=(kc == 7)),
                                reads=[tw] + t_hT[tb * 4:tb * 4 + 4], writes=[ps_t[7]])
                        dst = qT if c == 0 else kT
                        P.op("act", lambda e, dst=dst, tb=tb, c=c: e.activation(
                            out=dst[:, tb * 512:(tb + 1) * 512], in_=ps[:, 7, :], func=AF.Identity, scale=(0.125 if c == 0 else 1.0)),
                            reads=[ps_t[7]], writes=[tq if c == 0 else tk])
                for Q in range(4):
                    sidx = hq_block(h, Q, qT, kT, tq, tk, sidx)
            self.dbg_out("dbg_oT", oT, [128, 8, S], BF16, [t for r in t_oT for t in r])
            self.dma("sp", wo, odo_b[i].rearrange("p (k n) -> p k n", k=8), reads=[t_odb[i]], writes=[t_wo])
            for t in range(NT):
                for dh in range(2):
                    for h in range(8):
                        P.op("pe", lambda e, t=t, dh=dh, h=h: e.matmul(
                            ps[:, 5 + dh, :], oT[:, h, t * 128:(t + 1) * 128], wo[:, h, dh * 512:(dh + 1) * 512],
                            start=(h == 0), stop=(h == 7)), reads=[t_oT[h][t // 4], t_wo], writes=[ps_t[5 + dh]])
                epilogue(b, l, 0, t, 5, xsrc)

        def run_tiles(specs, acc, E, t_E, sidx, LAG=2):
            P.op("pe", lambda e: e.matmul(ps[:, acc, :], Zt[:, 0:128], Zt[:], start=True, stop=False),
                 reads=[t_cst], writes=[ps_t[acc]])
            pend = []
            n = len(specs)

            def scores(sp, sidx):
                sb = sidx % 3
                eb = sidx % 4
                c0, c1 = sp["c0"], sp["c1"]
                ex = sp["extra"]
                P.op("pe", lambda e: e.matmul(ps[:, sb, c0:c1], sp["lhsT"], sp["rhs"], start=True, stop=(len(ex) == 0)),
                     reads=sp["rd"], writes=[ps_t[sb]])
                for xi_, (xl, xr, cs, ce, xrd) in enumerate(ex):
                    P.op("pe", lambda e, xl=xl, xr=xr, cs=cs, ce=ce, last=(xi_ == len(ex) - 1): e.matmul(
                        ps[:, sb, cs:ce], xl, xr, start=False, stop=last), reads=xrd, writes=[ps_t[sb]])
                P.op("act", lambda e: e.activation(out=E[eb][:, c0:c1], in_=ps[:, sb, c0:c1], func=AF.Exp),
                     reads=[ps_t[sb]], writes=[t_E[eb]])
                return (sp, eb)

            def pv(sp, eb, idx):
                c0, c1 = sp["c0"], sp["c1"]
                P.op("pe", lambda e: e.matmul(ps[:, acc, c0:c1], sp["v"], E[eb][:, c0:c1], start=False, stop=(idx == n - 1)),
                     reads=sp["vrd"] + [t_E[eb]], writes=[ps_t[acc]])

            done = 0
            for sp in specs:
                pend.append(scores(sp, sidx))
                sidx += 1
                if len(pend) > LAG:
                    a_, b_ = pend.pop(0)
                    pv(a_, b_, done)
                    done += 1
            while pend:
                a_, b_ = pend.pop(0)
                pv(a_, b_, done)
                done += 1
            return sidx

        def even_attn(b, l, xsrc):
            i = l // 2
            evb = {k: v[i] for k, v in ev_b.items()}
            rd_w = [t_evb[i]]
            sc_mla = 96 ** -0.5
            P.barrier()
            off = [0]

            def alloc(n):
                o = off[0]
                off[0] = o + n + (n % 2)
                assert off[0] <= AR_EL, off[0]
                return o

            oM = self.view(alloc(4 * S), [128, 4, S], BF16)
            t_oM = [[T() for _ in range(4)] for _ in range(4)]
            t_oN = [[T() for _ in range(4)] for _ in range(4)]
            E = [self.view(alloc(512), [128, 512], BF16) for _ in range(4)]
            t_E = [T() for _ in range(4)]
            fw_ = [self.view(alloc(1024), [128, 512], F32) for _ in range(4)]
            t_fw = [T() for _ in range(4)]
            wcb = [self.view(alloc(3 * 1024), [128, 3, 8, 128], BF16) for _ in range(1)]
            t_wcb = [T()]
            base_off = off[0]
            cqT = self.view(alloc(3 * S), [128, 3, S], BF16)
            ckvT = self.view(alloc(2 * S), [128, 2, S], BF16)
            krT = self.view(alloc(S), [128, S], BF16)
            t_cq = [T() for _ in range(4)]
            t_ckv = [T() for _ in range(4)]
            t_kr = [T() for _ in range(4)]
            rope = self.view(alloc(2 * S), [128, S], F32)
            t_rope = T()
            lat = self.view(alloc(3 * 1024), [128, 3, 512], F32)
            t_lat = [T() for _ in range(3)]
            Vm = self.view(alloc(NT * 1024), [128, NT, 8, 128], BF16)
            t_Vm = [T() for _ in range(NT)]
            QK = [[self.view(alloc(S), [128, S], BF16) for _ in range(2)] for _ in range(2)]
            t_QK = [[T(), T()], [T(), T()]]
            wcoff = base_off - 3 * 1024
            uqb = [self.view(wcoff + j_ * 384, [128, 3, 128], BF16) for j_ in range(2)]
            ukb = [self.view(wcoff + 768 + j_ * 256, [128, 2, 128], BF16) for j_ in range(2)]
            t_ub = [T(), T()]
            ukv = self.view(wcoff + 1280, [128, 2, 512], BF16)
            t_ukv = T()

            emit_hT(b, l, 0, xsrc)
            epi_setup(b, l, 0)
            self.dma("sp", rope[0:64, :], rope_d[:, :], writes=[t_rope])
            self.dma("sp", qkn[:, 0:3], ev_qn[i], writes=[t_qkn])
            self.dma("sp", qkn[:, 3:5], ev_kvn[i], writes=[t_qkn])
            wc = wcb[0]
            for (nch, ch0, gcol, dstT, tdst) in [(3, 0, 0, cqT, t_cq), (2, 3, 3, ckvT, t_ckv)]:
                for c in range(nch):
                    self.dma("sp", wc[:, c], evb["ev_wc"][ch0 + c].rearrange("p (k j) -> p k j", k=8), reads=rd_w, writes=[t_wcb[0]])
                for tb in range(4):
                    for c in range(nch):
                        bank = 6 + (c % 2)
                        for kc in range(8):
                            P.op("pe", lambda e, c=c, kc=kc, tb=tb, bank=bank: e.matmul(
                                ps[:, bank, :], wc[:, c, kc, :], hT[:, kc, tb * 512:(tb + 1) * 512], start=(kc == 0), stop=(kc == 7)),
                                reads=[t_wcb[0]] + t_hT[tb * 4:tb * 4 + 4], writes=[ps_t[bank]])
                        P.op("act", lambda e, c=c, bank=bank: e.copy(out=lat[:, c, :], in_=ps[:, bank, :]),
                             reads=[ps_t[bank]], writes=[t_lat[c]])
                        sq = fw_[c % 2]
                        P.op("act", lambda e, sq=sq, bank=bank: e.activation(out=sq, in_=ps[:, bank, :], func=AF.Square),
                             reads=[ps_t[bank]], writes=[t_fw[c % 2]])
                        P.op("pe", lambda e, sq=sq, c=c, nch=nch: e.matmul(ps[:, 5, :], ones_f[:], sq, start=(c == 0), stop=(c == nch - 1)),
                             reads=[t_ones, t_fw[c % 2]], writes=[ps_t[5]])
                    rs = fw_[2]
                    P.op("act", lambda e, rs=rs, nch=nch: e.activation(out=rs, in_=ps[:, 5, :], func=AF.Sqrt, bias=eps_t[:, 0:1],
                                                                      scale=1.0 / (128.0 * nch)),
                         reads=[ps_t[5], t_eps], writes=[t_fw[2]])
                    P.op("dve", lambda e, rs=rs: e.reciprocal(out=rs, in_=rs), reads=[t_fw[2]], writes=[t_fw[2]])
                    for c in range(nch):
                        P.op("dve", lambda e, c=c, tb=tb, rs=rs, dstT=dstT, gcol=gcol: e.scalar_tensor_tensor(
                            out=dstT[:, c, tb * 512:(tb + 1) * 512], in0=lat[:, c, :], scalar=qkn[:, gcol + c:gcol + c + 1], in1=rs,
                            op0=ALU.mult, op1=ALU.mult), reads=[t_lat[c], t_fw[2], t_qkn], writes=[tdst[tb]])

            def rope_rot(psb, dst, cols, scale, rd, wr):
                tA, tB = fw_[0], fw_[1]
                P.op("dve", lambda e: e.scalar_tensor_tensor(out=tA[0:32, :], in0=ps[32:64, psb, :], scalar=scale, in1=rope[32:64, cols],
                                                             op0=ALU.mult, op1=ALU.mult),
                     reads=[ps_t[psb], t_rope] + rd, writes=[t_fw[0]])
                P.op("dve", lambda e: e.scalar_tensor_tensor(out=tB[0:32, :], in0=ps[0:32, psb, :], scalar=scale, in1=rope[0:32, cols],
                                                             op0=ALU.mult, op1=ALU.mult),
                     reads=[ps_t[psb], t_rope] + rd, writes=[t_fw[1]])
                P.op("pool", lambda e: e.tensor_tensor(out=dst[0:32, cols], in0=tA[0:32, :], in1=tB[0:32, :], op=ALU.add),
                     reads=[t_fw[0], t_fw[1]], writes=wr)

            self.dma("sp", wc[:, 0], evb["ev_wc"][5].rearrange("p (k j) -> p k j", k=8), reads=rd_w, writes=[t_wcb[0]])
            for tb in range(4):
                for kc in range(8):
                    P.op("pe", lambda e, kc=kc, tb=tb: e.matmul(ps[:, 7, :], wc[:, 0, kc, :], hT[:, kc, tb * 512:(tb + 1) * 512],
                                                                start=(kc == 0), stop=(kc == 7)),
                         reads=[t_wcb[0]] + t_hT[tb * 4:tb * 4 + 4], writes=[ps_t[7]])
                rope_rot(7, krT, slice(tb * 512, (tb + 1) * 512), 1.0, [], [t_kr[tb]])

            P.barrier()
            P.op("pool", lambda e: e.memset(Vm.rearrange("p t h d -> p (t h) d")[:, :, 64:128], 1.0), writes=t_Vm)
            self.dma("sp", ukv, evb["ev_ukvv"].rearrange("p (k n) -> p k n", k=2), reads=rd_w, writes=[t_ukv])
            for t in range(NT):
                for kc in range(2):
                    P.op("pe", lambda e, t=t, kc=kc: e.matmul(ps[:, 6, :], ckvT[:, kc, t * 128:(t + 1) * 128], ukv[:, kc, :],
                                                              start=(kc == 0), stop=(kc == 1)),
                         reads=[t_ckv[t // 4], t_ukv], writes=[ps_t[6]])
                P.op("act", lambda e, t=t: e.copy(out=Vm[:, t, :, 0:64], in_=ps[:, 6, :].rearrange("p (h d) -> p h d", h=8)),
                     reads=[ps_t[6]], writes=[t_Vm[t]])
            for j in range(2):
                for c in range(2):
                    P.op("pool", lambda e, j=j, c=c: e.memset(QK[j][c][32:64, :], 0.0), writes=[t_QK[j][c]])
                P.op("pool", lambda e, j=j: e.tensor_copy(out=QK[j][1][0:32, :], in_=krT[0:32, :]), reads=t_kr, writes=[t_QK[j][1]])
            sidx = 0
            for h in range(8):
                j = h % 2
                QT, KT = QK[j]
                tq, tk = t_QK[j]
                self.dma("sp", uqb[j], evb["ev_uq"][h].rearrange("p (k j) -> p k j", k=3), reads=rd_w, writes=[t_ub[j]])
                self.dma("sp", ukb[j], evb["ev_ukvk"][h].rearrange("p (k j) -> p k j", k=2), reads=rd_w, writes=[t_ub[j]])
                for tb in range(4):
                    cols = slice(tb * 512, (tb + 1) * 512)
                    for kc in range(3):
                        P.op("pe", lambda e, j=j, kc=kc, cols=cols: e.matmul(ps[:, 7, :], uqb[j][:, kc, :], cqT[:, kc, cols],
                                                                             start=(kc == 0), stop=(kc == 2)),
                             reads=[t_ub[j], t_cq[tb]], writes=[ps_t[7]])
                    P.op("act", lambda e, QT=QT, cols=cols: e.activation(out=QT[64:128, cols], in_=ps[64:128, 7, :], func=AF.Identity,
                                                                         scale=sc_mla),
                         reads=[ps_t[7]], writes=[tq])
                    rope_rot(7, QT, cols, sc_mla, [], [tq])
                    for kc in range(2):
                        P.op("pe", lambda e, j=j, kc=kc, cols=cols: e.matmul(ps[:, 6, :], ukb[j][:, kc, :], ckvT[:, kc, cols],
                                                                             start=(kc == 0), stop=(kc == 1)),
                             reads=[t_ub[j], t_ckv[tb]], writes=[ps_t[6]])
                    P.op("act", lambda e, KT=KT, cols=cols: e.copy(out=KT[64:128, cols], in_=ps[64:128, 6, :]),
                         reads=[ps_t[6]], writes=[tk])
                for Q in range(4):
                    specs = []
                    for kt in range(4 * Q + 4):
                        jj = kt - 4 * Q
                        c0 = max(0, jj) * 128
                        ex = []
                        if jj >= 0:
                            ex.append((ident[:], Md[:], c0, c0 + 128, [t_ident, t_cst]))
                        specs.append(dict(c0=c0, c1=512, lhsT=KT[:, kt * 128:(kt + 1) * 128], rhs=QT[:, Q * 512 + c0:(Q + 1) * 512],
                                          rd=[tq, tk], extra=ex, v=Vm[:, kt, h, :], vrd=[t_Vm[kt]]))
                    sidx = run_tiles(specs, 3, E, t_E, sidx)
                    rl = fw_[3]
                    P.op("dve", lambda e, rl=rl: e.reciprocal(out=rl[64:128, :], in_=ps[64:128, 3, :]), reads=[ps_t[3]], writes=[t_fw[3]])
                    hp = (h % 2) * 64
                    P.op("dve", lambda e, rl=rl, hp=hp, h=h, Q=Q: e.tensor_tensor(
                        out=oM[hp:hp + 64, h // 2, Q * 512:(Q + 1) * 512], in0=ps[0:64, 3, :], in1=rl[64:128, :], op=ALU.mult),
                        reads=[ps_t[3], t_fw[3]], writes=[t_oM[h // 2][Q]])
            self.dbg_out("dbg_oM", oM, [128, 4, S], BF16, [t for r in t_oM for t in r])

            P.barrier()
            off[0] = base_off
            oN = self.view(alloc(4 * S), [128, 4, S], BF16)
            NQ = self.view(alloc(4 * S), [128, 4, S], BF16)
            t_NQ = [[T() for _ in range(4)] for _ in range(4)]
            fm = [self.view(alloc(S), [128, S], BF16) for _ in range(4)]
            t_fm = [[T() for _ in range(4)] for _ in range(4)]
            srcK, srcV, kselT, kwinT = fm
            ghl = self.view(alloc(2 * S), [128, 2, S], BF16)
            t_ghl = [T() for _ in range(4)]
            Vn = self.view(alloc(NT * 512), [128, NT, 4, 128], BF16)
            t_Vn = [T() for _ in range(NT)]
            wvn = self.view(alloc(2048), [128, 8, 256], BF16)
            t_wvn = T()
            maskT = self.view(alloc(S), [128, S], BF16)
            t_mask = [T() for _ in range(NT)]
            P.op("pool", lambda e: e.memset(Vn.rearrange("p t h d -> p (t h) d")[:, :, 64:128], 1.0), writes=t_Vn)
            self.dma("sp", wvn, evb["ev_wvn"].rearrange("p (k n) -> p k n", k=8), reads=rd_w, writes=[t_wvn])
            for t in range(NT):
                for kc in range(8):
                    P.op("pe", lambda e, t=t, kc=kc: e.matmul(ps[:, 6, 0:256], hT[:, kc, t * 128:(t + 1) * 128], wvn[:, kc, :],
                                                              start=(kc == 0), stop=(kc == 7)),
                         reads=[t_hT[t], t_wvn], writes=[ps_t[6]])
                P.op("act", lambda e, t=t: e.copy(out=Vn[:, t, :, 0:64], in_=ps[:, 6, 0:256].rearrange("p (h d) -> p h d", h=4)),
                     reads=[ps_t[6]], writes=[t_Vn[t]])
            for ch in range(6, 15):
                self.dma("sp", wc[:, 0], evb["ev_wc"][ch].rearrange("p (k j) -> p k j", k=8), reads=rd_w, writes=[t_wcb[0]])
                for tb in range(4):
                    cols = slice(tb * 512, (tb + 1) * 512)
                    bank = 6 + (tb % 2)
                    for kc in range(8):
                        P.op("pe", lambda e, kc=kc, cols=cols, bank=bank: e.matmul(ps[:, bank, :], wc[:, 0, kc, :], hT[:, kc, cols],
                                                                                   start=(kc == 0), stop=(kc == 7)),
                             reads=[t_wcb[0]] + t_hT[tb * 4:tb * 4 + 4], writes=[ps_t[bank]])
                    if ch <= 9:
                        P.op("act", lambda e, ch=ch, cols=cols, bank=bank: e.activation(out=NQ[:, ch - 6, cols], in_=ps[:, bank, :],
                                                                                       func=AF.Identity, scale=0.125),
                             reads=[ps_t[bank]], writes=[t_NQ[ch - 6][tb]])
                    elif ch <= 13:
                        P.op("act", lambda e, ch=ch, cols=cols, bank=bank: e.copy(out=fm[ch - 10][:, cols], in_=ps[:, bank, :]),
                             reads=[ps_t[bank]], writes=[t_fm[ch - 10][tb]])
                    else:
                        gt = fw_[0]
                        P.op("act", lambda e, gt=gt, bank=bank: e.activation(out=gt[0:24, :], in_=ps[0:24, bank, :], func=AF.Sigmoid),
                             reads=[ps_t[bank]], writes=[t_fw[0]])
                        P.op("dve", lambda e, gt=gt, cols=cols: e.tensor_copy(out=ghl[0:24, 0, cols], in_=gt[0:24, :]),
                             reads=[t_fw[0]], writes=[t_ghl[tb]])
                        P.op("dve", lambda e, gt=gt, cols=cols: e.tensor_tensor(out=ghl[0:24, 1, cols], in0=gt[0:24, :], in1=ghl[0:24, 0, cols],
                                                                                op=ALU.subtract),
                             reads=[t_fw[0], t_ghl[tb]], writes=[t_ghl[tb]])

            P.barrier()
            hx = hT.rearrange("p k s -> p (k s)")
            hoff = [0]

            def hview(n, shape, dt):
                o = hoff[0]
                hoff[0] = o + n + (n % 2)
                assert hoff[0] <= 8 * S
                if dt == BF16:
                    ap = hx[:, o:o + n]
                else:
                    ap = hx[:, o:o + n].bitcast(F32)
                if len(shape) == 3:
                    ap = ap.rearrange("p (a b) -> p a b", a=shape[1])
                elif len(shape) == 4:
                    ap = ap.rearrange("p (a b c) -> p a b c", a=shape[1], b=shape[2])
                return ap
            t_h = lambda: T()
            w1sb = hview(32 * 256, [128, 32, 256], BF16); t_w1 = T()
            after_w1 = hoff[0]
            hoff[0] = 0
            efp = hview(2 * 8 * 128, [128, 8, 128], F32); t_e = T()
            pbf = hview(8 * 128, [128, 8, 128], BF16); t_p = T()
            pT = hview(4 * 8 * 128, [128, 4, 8, 128], BF16); t_pT = [T() for _ in range(4)]
            assert hoff[0] <= after_w1
            hoff[0] = after_w1
            peT = hview(32, [128, 32], BF16); t_pe = T()
            w2k = hview(512, [128, 2, 2, 128], BF16); t_w2 = T()
            w2v = hview(128, [128, 2, 64], BF16)
            hid = hview(256, [128, 2, 128], BF16); t_hid = [T(), T()]
            xa = hview(256, [128, 128], F32); t_xa = T()
            xb = hview(256, [128, 128], F32); t_xb = T()
            kcT = hview(128, [128, 128], BF16); t_kc = T()
            vc = hview(128, [128, 2, 64], BF16); t_vc = T()
            pebias = hview(4, [128, 2], F32); t_peb = T()
            ssum = hview(32, [128, 16], F32); t_ss = T()
            scr = hview(2 * 64, [128, 64], F32); t_scr = T()
            m8 = hview(2 * 16, [128, 16], F32); t_m8 = T()
            mbb = hview(64, [128, 64], BF16); t_mb = T()
            EXP = hview(16 * 128, [128, 16, 128], BF16)
            selb = hview(24 * 128, [128, 24, 128], BF16)
            scab = hview(2 * NT * 32, [128, 2, NT, 32], BF16)
            t_c3 = T()
            self.dma("sp", EXP[0:64], cst_b["expmat"].rearrange("p (a b) -> p a b", a=16), reads=[t_cstb], writes=[t_c3])
            self.dma("sp", selb[0:24], cst_b["selb"].rearrange("p (a b) -> p a b", a=24), reads=[t_cstb], writes=[t_c3])
            for a_ in range(2):
                self.dma("sp", scab[:, a_], cst_b["scab"][a_].rearrange("p (a b) -> p a b", a=NT), reads=[t_cstb], writes=[t_c3])

            self.dma("sp", w2k, evb["ev_w2k"].rearrange("p (g c j) -> p g c j", g=2, c=2), reads=rd_w, writes=[t_w2])
            self.dma("sp", w2v, evb["ev_w2v"].rearrange("p (c j) -> p c j", c=2), reads=rd_w, writes=[t_w2])
            for kv in range(2):
                self.dma("sp", w1sb, evb["ev_w1"][kv].rearrange("p (l c) -> p l c", l=32), reads=rd_w, writes=[t_w1])
                self.dma("sp", peT[0:64, :], evb["ev_peT"][kv], reads=rd_w, writes=[t_pe])
                for cc in range(2):
                    for l_ in range(32):
                        P.op("pe", lambda e, cc=cc, l_=l_: e.matmul(ps[:, 7, 0:1], w1sb[0:64, l_, cc * 128:(cc + 1) * 128], peT[0:64, l_:l_ + 1],
                                                                    start=(l_ == 0), stop=(l_ == 31)),
                             reads=[t_w1, t_pe], writes=[ps_t[7]])
                    P.op("dve", lambda e, cc=cc: e.tensor_copy(out=pebias[:, cc:cc + 1], in_=ps[:, 7, 0:1]), reads=[ps_t[7]], writes=[t_peb])
                src = srcK if kv == 0 else srcV
                tsrc = t_fm[kv]
                for g in range(2):
                    gp = slice(g * 64, (g + 1) * 64)
                    for cc in range(2):
                        for l_ in range(32):
                            P.op("pe", lambda e, gp=gp, cc=cc, l_=l_, src=src: e.matmul(
                                ps[:, 6, 0:127], w1sb[gp, l_, cc * 128:(cc + 1) * 128], src[gp, l_:l_ + 2017:16],
                                start=(l_ == 0), stop=(l_ == 31)), reads=[t_w1] + tsrc, writes=[ps_t[6]])
                        P.op("act", lambda e, cc=cc: e.activation(out=xa[:, 0:127], in_=ps[:, 6, 0:127], func=AF.Identity,
                                                                  bias=pebias[:, cc:cc + 1], scale=1.0),
                             reads=[ps_t[6], t_peb], writes=[t_xa])
                        P.op("act", lambda e: e.activation(out=xb[:, 0:127], in_=xa[:, 0:127], func=AF.Square), reads=[t_xa], writes=[t_xb])
                        P.op("dve", lambda e: e.tensor_scalar(out=xb[:, 0:127], in0=xb[:, 0:127], scalar1=0.044715, scalar2=1.0,
                                                              op0=ALU.mult, op1=ALU.add), reads=[t_xb], writes=[t_xb])
                        P.op("dve", lambda e: e.tensor_tensor(out=xb[:, 0:127], in0=xb[:, 0:127], in1=xa[:, 0:127], op=ALU.mult),
                             reads=[t_xb, t_xa], writes=[t_xb])
                        P.op("act", lambda e: e.activation(out=xb[:, 0:127], in_=xb[:, 0:127], func=AF.Sigmoid, scale=1.5957691216057308),
                             reads=[t_xb], writes=[t_xb])
                        P.op("dve", lambda e, cc=cc: e.tensor_tensor(out=hid[:, cc, 0:127], in0=xa[:, 0:127], in1=xb[:, 0:127], op=ALU.mult),
                             reads=[t_xa, t_xb], writes=[t_hid[cc]])
                    if kv == 0:
                        for cc in range(2):
                            P.op("pe", lambda e, g=g, cc=cc: e.matmul(ps[:, 7, 0:127], w2k[:, g, cc, :], hid[:, cc, 0:127],
                                                                      start=(cc == 0), stop=(cc == 1)),
                                 reads=[t_w2] + t_hid, writes=[ps_t[7]])
                        P.op("dve", lambda e, gp=gp: e.tensor_copy(out=kcT[gp, 0:127], in_=ps[gp, 7, 0:127]), reads=[ps_t[7]], writes=[t_kc])
                    else:
                        for cc in range(2):
                            P.op("pe", lambda e, cc=cc: e.matmul(ps[0:127, 7, 0:64], hid[:, cc, 0:127], w2v[:, cc, :],
                                                                 start=(cc == 0), stop=(cc == 1)),
                                 reads=[t_w2] + t_hid, writes=[ps_t[7]])
                        P.op("dve", lambda e, g=g: e.tensor_copy(out=vc[0:127, g, :], in_=ps[0:127, 7, 0:64]), reads=[ps_t[7]], writes=[t_vc])
            P.barrier()
            self.dbg_out("dbg_kcT", kcT, [128, 128], BF16, [t_kc])
            self.dbg_out("dbg_vc", vc, [128, 2, 64], BF16, [t_vc])

            for Q in range(4):
                for tt in range(4):
                    qt = 4 * Q + tt
                    qc = slice(qt * 128, (qt + 1) * 128)
                    for h in range(8):
                        g, jn = h // 4, h % 4
                        gp = slice(g * 64, (g + 1) * 64)
                        bank = h // 4
                        cs = (h % 4) * 127
                        P.op("pe", lambda e, gp=gp, jn=jn, qc=qc, bank=bank, cs=cs: e.matmul(
                            ps[:, bank, cs:cs + 127], NQ[gp, jn, qc], kcT[gp, 0:127], start=True, stop=False),
                            reads=[t_NQ[jn][Q], t_kc], writes=[ps_t[bank]])
                        P.op("pe", lambda e, h=h, qt=qt, bank=bank, cs=cs: e.matmul(
                            ps[:, bank, cs:cs + 127], ident[:], Bc[:, h, 120 - 8 * qt:247 - 8 * qt], start=False, stop=True),
                            reads=[t_ident, t_B], writes=[ps_t[bank]])
                    for h in range(8):
                        bank = h // 4
                        cs = (h % 4) * 127
                        P.op("act", lambda e, h=h, bank=bank, cs=cs: e.activation(out=efp[:, h, 0:127], in_=ps[:, bank, cs:cs + 127], func=AF.Exp,
                                                                                   accum_out=ssum[:, h:h + 1]),
                             reads=[ps_t[bank]], writes=[t_e, t_ss])
                    P.op("dve", lambda e: e.tensor_scalar_add(out=ssum[:, 8:16], in0=ssum[:, 0:8], scalar1=1e-30), reads=[t_ss], writes=[t_ss])
                    P.op("dve", lambda e: e.reciprocal(out=ssum[:, 8:16], in_=ssum[:, 8:16]), reads=[t_ss], writes=[t_ss])
                    for h in range(8):
                        P.op("dve" if h % 2 == 0 else "pool", lambda e, h=h: e.tensor_scalar_mul(out=pbf[:, h, 0:127], in0=efp[:, h, 0:127],
                                                                                               scalar1=ssum[:, 8 + h:9 + h]),
                             reads=[t_e, t_ss], writes=[t_p])
                    psT = ps[:, 5, :].bitcast(BF16)
                    for h in range(8):
                        P.op("pe", lambda e, h=h, psT=psT: e.transpose(psT[0:127, h * 128:(h + 1) * 128], pbf[:, h, 0:127], ident[:]),
                             reads=[t_p, t_ident], writes=[ps_t[5]])
                    P.op("act", lambda e, tt=tt, psT=psT: e.copy(out=pT[0:127, tt, :, :], in_=psT[0:127, :].rearrange("p (h q) -> p h q", h=8)),
                         reads=[ps_t[5]], writes=[t_pT[tt]])
                    for h in range(8):
                        g = h // 4
                        P.op("pe", lambda e, h=h, g=g, tt=tt: e.matmul(ps[:, 7, g * 32:(g + 1) * 32], pT[0:127, tt, h, :], ovl[:, :],
                                                                       start=(h % 4 == 0), stop=(h % 4 == 3)),
                             reads=[t_pT[tt], t_cst], writes=[ps_t[7]])
                    for g in range(2):
                        P.op("dve", lambda e, g=g, qt=qt: e.tensor_tensor(out=scr[:, g * 32:(g + 1) * 32], in0=ps[:, 7, g * 32:(g + 1) * 32],
                                                                          in1=scab[:, 0, qt, :], op=ALU.mult),
                             reads=[ps_t[7], t_c3], writes=[t_scr])
                        P.op("dve", lambda e, g=g, qt=qt: e.tensor_tensor(out=scr[:, g * 32:(g + 1) * 32], in0=scr[:, g * 32:(g + 1) * 32],
                                                                          in1=scab[:, 1, qt, :], op=ALU.add),
                             reads=[t_scr, t_c3], writes=[t_scr])
                        P.op("dve", lambda e, g=g: e.max(out=m8[:, g * 8:(g + 1) * 8], in_=scr[:, g * 32:(g + 1) * 32]),
                             reads=[t_scr], writes=[t_m8])
                        P.op("dve", lambda e, g=g: e.tensor_scalar(out=scr[:, g * 32:(g + 1) * 32], in0=scr[:, g * 32:(g + 1) * 32],
                                                                   scalar1=m8[:, g * 8 + 7:g * 8 + 8], scalar2=1.0,
                                                                   op0=ALU.is_ge, op1=ALU.subtract),
                             reads=[t_scr, t_m8], writes=[t_scr])
                    P.op("dve", lambda e: e.tensor_scalar_mul(out=mbb[:, :], in0=scr[:, :], scalar1=-NEG), reads=[t_scr], writes=[t_mb])
                    psM = ps[:, 7, 256:512].bitcast(BF16)
                    P.op("pe", lambda e, psM=psM: e.transpose(psM[0:64, 0:128], mbb[:, :], ident[:]), reads=[t_mb, t_ident], writes=[ps_t[7]])
                    P.op("act", lambda e, qc=qc, psM=psM: e.copy(out=maskT[0:64, qc], in_=psM[0:64, 0:128]), reads=[ps_t[7]], writes=[t_mask[qt]])
                for h in range(8):
                    g, jn = h // 4, h % 4
                    gp = slice(g * 64, (g + 1) * 64)
                    mp = slice(g * 32, (g + 1) * 32)
                    qrd = [t_NQ[jn][Q]]
                    specs = []
                    for kt in range(4 * Q + 4):
                        jj = kt - 4 * Q
                        c0 = max(0, jj) * 128
                        ex = [(EXP[mp, kt, :], maskT[mp, Q * 512 + c0:(Q + 1) * 512], c0, 512, [t_c3] + t_mask[4 * Q:4 * Q + 4])]
                        if jj == -1:
                            ex.append((ident[:], Bsb[:, h, 1, :], 0, 128, [t_ident, t_B]))
                        elif jj == 3:
                            ex.append((ident[:], Bsb[:, h, 0, :], 384, 512, [t_ident, t_B]))
                        elif jj >= 0:
                            ex.append((ident[:], Bsb[:, h, 0:2, :].rearrange("p t q -> p (t q)"), c0, c0 + 256, [t_ident, t_B]))
                        specs.append(dict(c0=c0, c1=512, lhsT=kselT[gp, kt * 128:(kt + 1) * 128], rhs=NQ[gp, jn, Q * 512 + c0:(Q + 1) * 512],
                                          rd=qrd + t_fm[2], extra=ex, v=Vn[:, kt, 0 + g, :], vrd=[t_Vn[kt]]))
                    sidx = run_tiles(specs, 3, E, t_E, sidx)
                    specs = []
                    for kt in range(max(0, 4 * Q - 2), 4 * Q + 4):
                        jj = kt - 4 * Q
                        lo = max(0, jj)
                        hi = min(3, jj + 2)
                        c0, c1 = lo * 128, (hi + 1) * 128
                        t0, t1 = lo - jj, hi - jj
                        ex = [(ident[:], Bsb[:, h, t0:t1 + 1, :].rearrange("p t q -> p (t q)"), c0, c1, [t_ident, t_B])]
                        specs.append(dict(c0=c0, c1=c1, lhsT=kwinT[gp, kt * 128:(kt + 1) * 128], rhs=NQ[gp, jn, Q * 512 + c0:Q * 512 + c1],
                                          rd=qrd + t_fm[3], extra=ex, v=Vn[:, kt, 2 + g, :], vrd=[t_Vn[kt]]))
                    sidx = run_tiles(specs, 4, E, t_E, sidx)
                    for tt in range(4):
                        P.op("pe", lambda e, g=g, tt=tt, h=h: e.matmul(ps[0:64, 6, tt * 128:(tt + 1) * 128], vc[0:127, g, :], pT[0:127, tt, h, :],
                                                                       start=True, stop=True),
                             reads=[t_vc, t_pT[tt]], writes=[ps_t[6]])
                    acc = fw_[0]
                    hp = (h % 2) * 64
                    brs = [(0, 6, False), (1, 3, True), (2, 4, True)]
                    if self.nsa_only is not None:
                        brs = [brs[self.nsa_only]]
                    for bi_, (br, src_bank, norm) in enumerate(brs):
                        r = 3 * h + br
                        first_, last_ = (bi_ == 0), (bi_ == len(brs) - 1)
                        for part in range(2):
                            P.op("pe", lambda e, r=r, part=part, Q=Q: e.matmul(ps[:, 5, :], selb[0:24, r, :], ghl[0:24, part, Q * 512:(Q + 1) * 512],
                                                                          start=(part == 0), stop=(part == 1)),
                                 reads=[t_c3, t_ghl[Q]], writes=[ps_t[5]])
                        f = fw_[1]
                        if norm:
                            P.op("dve", lambda e, f=f, src_bank=src_bank: e.reciprocal(out=f[64:128, :], in_=ps[64:128, src_bank, :]),
                                 reads=[ps_t[src_bank]], writes=[t_fw[1]])
                            P.op("dve", lambda e, f=f: e.tensor_tensor(out=f[64:128, :], in0=ps[64:128, 5, :], in1=f[64:128, :], op=ALU.mult),
                                 reads=[ps_t[5], t_fw[1]], writes=[t_fw[1]])
                        else:
                            P.op("dve", lambda e, f=f: e.tensor_copy(out=f[64:128, :], in_=ps[64:128, 5, :]), reads=[ps_t[5]], writes=[t_fw[1]])
                        dst = oN[hp:hp + 64, h // 2, Q * 512:(Q + 1) * 512] if last_ else acc[0:64, :]
                        tdst = t_oN[h // 2][Q] if last_ else t_fw[0]
                        if first_:
                            P.op("dve", lambda e, f=f, src_bank=src_bank, dst=dst: e.tensor_tensor(out=dst, in0=ps[0:64, src_bank, :], in1=f[64:128, :],
                                                                                                  op=ALU.mult),
                                 reads=[ps_t[src_bank], t_fw[1]], writes=[tdst])
                        else:
                            tmp_ = fw_[2]
                            P.op("dve", lambda e, f=f, src_bank=src_bank, tmp_=tmp_: e.tensor_tensor(out=tmp_[0:64, :], in0=ps[0:64, src_bank, :],
                                                                                                    in1=f[64:128, :], op=ALU.mult),
                                 reads=[ps_t[src_bank], t_fw[1]], writes=[t_fw[2]])
                            P.op("pool", lambda e, tmp_=tmp_, dst=dst: e.tensor_tensor(out=dst, in0=acc[0:64, :], in1=tmp_[0:64, :], op=ALU.add),
                                 reads=[t_fw[0], t_fw[2]], writes=[tdst])
            self.dbg_out("dbg_oN", oN, [128, 4, S], BF16, [t for r in t_oN for t in r])
            self.dbg_out("dbg_maskT", maskT, [128, S], BF16, t_mask)

            P.barrier()
            off[0] = base_off + 4 * S
            wo = self.view(alloc(8 * D), [128, 8, D], BF16)
            t_wo = T()
            self.dma("sp", wo, evb["ev_wo"].rearrange("p (k n) -> p k n", k=8), reads=rd_w, writes=[t_wo])
            for t in range(NT):
                for dh in range(2):
                    for c in range(8):
                        src_ = oM if c < 4 else oN
                        tsrc_ = (t_oM if c < 4 else t_oN)[c % 4][t // 4]
                        P.op("pe", lambda e, t=t, dh=dh, c=c, src_=src_: e.matmul(
                            ps[:, 5 + dh, :], src_[:, c % 4, t * 128:(t + 1) * 128], wo[:, c, dh * 512:(dh + 1) * 512],
                            start=(c == 0), stop=(c == 7)), reads=[tsrc_, t_wo], writes=[ps_t[5 + dh]])
                epilogue(b, l, 0, t, 5, xsrc)

        first = {b: True for b in range(nseq)}
        for b in range(nseq):
            for (l, sub) in plan:
                xsrc = x_in if first[b] else out
                first[b] = False
                if sub == 1:
                    ffn(b, l, xsrc)
                elif l % 2 == 1:
                    diff_attn(b, l, xsrc)
                else:
                    even_attn(b, l, xsrc)
        P.emit()


def host_prep(inputs, core, nseq, seq0=None):
    b0 = core * nseq if seq0 is None else seq0
    m = {}
    m["x"] = np.ascontiguousarray(inputs["x"][b0:b0 + nseq])
    m["cT"] = np.ascontiguousarray(inputs["c"][b0:b0 + nseq].reshape(nseq, 8, 128).transpose(2, 1, 0))
    for k in ["ada_w", "ada_b", "ln_g", "ln_b", "rel_bias"]:
        m[k] = np.ascontiguousarray(inputs[k])
    up = inputs["ffn_w_up"].reshape(DEPTH, 8, 128, NFC, 128)
    m["wu"] = np.ascontiguousarray(up.transpose(0, 3, 2, 1, 4))
    gt = inputs["ffn_w_gate"].reshape(DEPTH, 8, 128, NFC, 128)
    m["wg"] = np.ascontiguousarray(gt.transpose(0, 3, 2, 1, 4))
    dn = inputs["ffn_w_down"].reshape(DEPTH, NFC, 128, D)
    m["wd"] = np.ascontiguousarray(dn.transpose(0, 2, 1, 3))
    cw = np.concatenate([inputs["ffn_conv_w"].transpose(0, 2, 1), inputs["ffn_conv_b"][:, :, None]], axis=2)
    m["convp"] = np.ascontiguousarray(cw.astype(np.float32))
    w = inputs["od_w_in"]
    qk = w[:, :, :2048].reshape(2, 8, 128, 16, 128)
    m["od_wqk"] = np.ascontiguousarray(qk.transpose(0, 3, 2, 1, 4))
    v = w[:, :, 2048:].reshape(2, 8, 128, D)
    m["od_wv"] = np.ascontiguousarray(v.transpose(0, 2, 1, 3))
    wo = inputs["od_w_o"].reshape(2, 8, 128, D)
    m["od_wo"] = np.ascontiguousarray(wo.transpose(0, 2, 1, 3))
    m["diff_lambda"] = np.ascontiguousarray(inputs["diff_lambda"].reshape(2, 256))
    m["diff_subln"] = np.ascontiguousarray(inputs["diff_subln"].reshape(2, 128, 1))
    ew = inputs["ev_w_in"]
    def fm_chunk(W, cols, nk):
        Z = np.zeros((W.shape[0], W.shape[1], 128), np.float32)
        Z[:, :, :len(cols)] = W[:, :, cols]
        return Z.reshape(W.shape[0], nk, 128, 128).transpose(0, 2, 1, 3).reshape(W.shape[0], 128, nk * 128)
    sw = [(r + 16) % 32 for r in range(32)]
    lists = [list(range(c * 128, (c + 1) * 128)) for c in range(3)]
    lists += [list(range(384 + c * 128, 384 + (c + 1) * 128)) for c in range(2)]
    lists += [[640 + r for r in range(32)] + [640 + r for r in sw]]
    for c in range(4):
        lists += [[672 + c * 64 + d for d in range(64)] + [672 + (c + 4) * 64 + d for d in range(64)]]
    lists += [[1184 + d for d in range(0, 128)], [1184 + d for d in range(128, 256)],
              [1184 + d for d in range(256, 384)], [1184 + d for d in range(512, 640)]]
    lists += [list(range(1952, 1976))]
    m["ev_wc"] = np.ascontiguousarray(np.stack([fm_chunk(ew, L, 8) for L in lists], axis=1))
    vcols = [1184 + d for d in range(384, 512)] + [1184 + d for d in range(640, 768)]
    m["ev_wvn"] = np.ascontiguousarray(ew[:, :, vcols].reshape(2, 8, 128, 256).transpose(0, 2, 1, 3).reshape(2, 128, 8 * 256))
    uq = inputs["mla_w_uq"]
    m["ev_uq"] = np.ascontiguousarray(np.stack([fm_chunk(uq, [h * 96 + 64 + r for r in range(32)] + [h * 96 + 64 + r for r in sw]
                                                          + [h * 96 + d for d in range(64)], 3) for h in range(8)], axis=1))
    ukv = inputs["mla_w_ukv"]
    def kchunk(h):
        Z = np.zeros((2, 256, 128), np.float32)
        Z[:, :, 64:] = ukv[:, :, h * 128:h * 128 + 64]
        return Z.reshape(2, 2, 128, 128).transpose(0, 2, 1, 3).reshape(2, 128, 256)
    m["ev_ukvk"] = np.ascontiguousarray(np.stack([kchunk(h) for h in range(8)], axis=1))
    vc_ = [h * 128 + 64 + d for h in range(8) for d in range(64)]
    m["ev_ukvv"] = np.ascontiguousarray(ukv[:, :, vc_].reshape(2, 2, 128, 512).transpose(0, 2, 1, 3).reshape(2, 128, 1024))
    m["ev_qn"] = np.ascontiguousarray(inputs["mla_q_norm"].reshape(2, 3, 128).transpose(0, 2, 1))
    m["ev_kvn"] = np.ascontiguousarray(inputs["mla_kv_norm"].reshape(2, 2, 128).transpose(0, 2, 1))
    w1 = inputs["nsa_cmp_w1"].reshape(2, 2, 32, 64, 256).transpose(0, 1, 3, 2, 4)
    m["ev_w1"] = np.ascontiguousarray(np.concatenate([w1, w1], axis=2).reshape(2, 2, 128, 32 * 256))
    w2 = inputs["nsa_cmp_w2"]
    w2k = np.zeros((2, 128, 2, 2, 128), np.float32)
    for g in range(2):
        w2k[:, :, g, :, g * 64:(g + 1) * 64] = w2[:, 0].reshape(2, 2, 128, 64).transpose(0, 2, 1, 3)
    m["ev_w2k"] = np.ascontiguousarray(w2k.reshape(2, 128, 512))
    m["ev_w2v"] = np.ascontiguousarray(w2[:, 1].reshape(2, 2, 128, 64).transpose(0, 2, 1, 3).reshape(2, 128, 128))
    m["ev_peT"] = np.ascontiguousarray(inputs["nsa_cmp_pe"].transpose(0, 1, 3, 2))
    m["ev_wo"] = np.ascontiguousarray(inputs["ev_w_o"].reshape(2, 8, 128, D).transpose(0, 2, 1, 3).reshape(2, 128, 8 * D))
    m.update(static_consts())
    return m


_SC = {}


def static_consts():
    if _SC:
        return _SC
    inv = (1.0 / (np.float32(10000.0) ** (np.arange(0, 32, 2, dtype=np.float32) / np.float32(32)))).astype(np.float32)
    ang = (np.arange(S, dtype=np.float32)[:, None] * inv[None, :]).astype(np.float32)
    cos, sin = np.cos(ang).astype(np.float32), np.sin(ang).astype(np.float32)
    rt = np.zeros((64, S), np.float32)
    for r in range(32):
        rt[r] = cos[:, r % 16]
        rt[32 + r] = -sin[:, r] if r < 16 else sin[:, r - 16]
    _SC["ropeT"] = rt
    kk, qq = np.meshgrid(np.arange(128), np.arange(128), indexing="ij")
    _SC["cmask"] = np.stack([np.where(qq >= kk, 0.0, NEG), np.where(qq < kk, 0.0, NEG)]).astype(np.float32)
    ex = np.zeros((2, 32, 16, 128), np.float32)
    for kt in range(16):
        ex[:, 2 * kt, kt, 0:64] = 1.0
        ex[:, 2 * kt + 1, kt, 64:128] = 1.0
    _SC["expmat"] = ex.reshape(64, 16 * 128)
    starts = np.arange(127) * 16
    jb = np.arange(32)
    _SC["ovl"] = ((starts[:, None] < (jb[None, :] + 1) * 64) & (starts[:, None] + 32 > jb[None, :] * 64)).astype(np.float32)
    t = np.arange(S)
    cur = t // 64
    forced = (jb[None, :] == 0) | (jb[None, :] == cur[:, None]) | (jb[None, :] == cur[:, None] - 1)
    causal = jb[None, :] * 64 <= t[:, None]
    A = (causal & ~forced).astype(np.float32)
    Bf = np.where(~causal, -1.0, np.where(forced, 1e4, 0.0)).astype(np.float32)
    sc = np.stack([A, Bf]).reshape(2, NT, 128, 32).transpose(0, 2, 1, 3).reshape(2, 128, NT * 32)
    _SC["scab"] = np.ascontiguousarray(sc)
    sb = np.zeros((24, 24, 128), np.float32)
    for r in range(24):
        sb[r, r, :] = 1.0
    _SC["selb"] = sb.reshape(24, 24 * 128)
    _SC["onehot"] = onehot_consts()
    _SC["ident"] = np.eye(128, dtype=np.float32)
    return _SC


FULL_PLAN = [(l, s) for l in range(DEPTH) for s in range(2)]
NCORES = 8
_CACHE = {}


def kernel(**inputs):
    inputs = {k: np.asarray(v) for k, v in inputs.items()}
    B = inputs["x"].shape[0]
    nseq = B // NCORES
    kb = K(nseq, FULL_PLAN)
    in_maps = []
    shared = host_prep(inputs, 0, nseq)
    shared = {k: v for k, v in shared.items() if k in kb.dram_in}
    for core in range(NCORES):
        m = dict(shared)
        b0 = core * nseq
        m["x"] = np.ascontiguousarray(inputs["x"][b0:b0 + nseq])
        m["cT"] = np.ascontiguousarray(inputs["c"][b0:b0 + nseq].reshape(nseq, 8, 128).transpose(2, 1, 0))
        in_maps.append(m)
    res = run_bass_kernel_spmd(kb.nc, in_maps, core_ids=list(range(NCORES)))
    outs = [np.asarray(r["out"]) for r in res.results]
    return np.concatenate(outs, axis=0).astype(np.float32)
```

```python
import numpy as np
import math
import contextlib
from concourse.bass_utils import run_bass_kernel_spmd
import concourse.bass as bass
import concourse.mybir as mybir

F32 = mybir.dt.float32
BF16 = mybir.dt.bfloat16
ALU = mybir.AluOpType
AF = mybir.ActivationFunctionType
AX = mybir.AxisListType

SEM_LIM = 30000
NSLOT = 12


class Tok:
    __slots__ = ("w", "r", "const")

    def __init__(self, const=False):
        self.w = None
        self.r = []
        self.const = const


class Ins:
    __slots__ = ("eng", "fn", "deps", "dma", "pos", "sig", "slot", "slotcnt", "signo")


class Prog:
    ENGS = ["pe", "act", "dve", "pool", "sp"]

    def __init__(self, nc):
        self.nc = nc
        self.instrs = []
        self.ndma = {e: 0 for e in self.ENGS}
        self.slot_last = {}
        self.stack = contextlib.ExitStack()
        self.last = {}
        self.pending_bar = {}

    def sbuf(self, name, shape, dt):
        return self.stack.enter_context(self.nc.sbuf_tensor(name, list(shape), dt))

    def psum(self, name, shape, dt):
        return self.stack.enter_context(self.nc.psum_tensor(name, list(shape), dt))

    def op(self, eng, fn, reads=(), writes=(), dma=False):
        ins = Ins()
        ins.eng = eng
        ins.fn = fn
        ins.dma = dma
        ins.sig = False
        ins.signo = 0
        deps = set()
        for t in reads:
            if t.w is not None:
                deps.add(t.w)
        for t in writes:
            if t.w is not None:
                deps.add(t.w)
            deps.update(t.r)
        for t in reads:
            if not t.const:
                t.r.append(ins)
        for t in writes:
            t.w = ins
            t.r = []
        if dma:
            n = self.ndma[eng]
            self.ndma[eng] = n + 1
            ins.slot = n % NSLOT
            ins.slotcnt = n // NSLOT + 1
            prev = self.slot_last.get((eng, ins.slot))
            if prev is not None:
                deps.add(prev)
            self.slot_last[(eng, ins.slot)] = ins
        pb = self.pending_bar.pop(eng, None)
        if pb:
            deps.update(pb)
        deps.discard(ins)
        ins.deps = deps
        self.instrs.append(ins)
        self.last[eng] = ins
        return ins

    def barrier(self):
        bar = list(self.last.values()) + list(self.slot_last.values())
        for e in self.ENGS:
            self.pending_bar[e] = list(bar)

    def emit(self, final_waits=()):
        nc = self.nc
        per = {e: [] for e in self.ENGS}
        for ins in self.instrs:
            ins.pos = len(per[ins.eng])
            per[ins.eng].append(ins)

        def needs_wait(ins, d):
            if d.dma:
                return True
            if d.eng == ins.eng:
                if d.eng == "pe":
                    return False
                if ins.dma:
                    return True
                return (ins.pos - d.pos) <= 3
            return True

        for ins in self.instrs:
            best = {}
            for d in ins.deps:
                if not d.dma and needs_wait(ins, d):
                    if d.eng not in best or best[d.eng].pos < d.pos:
                        best[d.eng] = d
            nd = set(d for d in ins.deps if d.dma)
            for d in best.values():
                d.sig = True
                nd.add(d)
            ins.deps = nd
        for e in self.ENGS:
            n = 0
            for ins in per[e]:
                if ins.sig and not ins.dma:
                    n += 1
                    ins.signo = n
        nsig = {e: max([i.signo for i in per[e]] + [0]) for e in self.ENGS}
        esems = {}
        for e in self.ENGS:
            nep = (nsig[e] + SEM_LIM - 1) // SEM_LIM
            esems[e] = [self.stack.enter_context(nc.semaphore(f"s_{e}_{k}")) for k in range(max(nep, 1))]
        ssems = {}
        for e in self.ENGS:
            if self.ndma[e]:
                ssems[e] = [self.stack.enter_context(nc.semaphore(f"d_{e}_{k}")) for k in range(NSLOT)]
                assert (self.ndma[e] // NSLOT + 1) * 16 < 65000, "too many dmas on queue"
        self.stats = {e: [len(per[e]), nsig[e], self.ndma[e]] for e in self.ENGS}
        nwaits = {e: 0 for e in self.ENGS}

        def run(ename, handle):
            waited = {}
            for ins in per[ename]:
                need = {}
                for d in ins.deps:
                    if not needs_wait(ins, d):
                        continue
                    if d.dma:
                        key = ("s", d.eng, d.slot)
                        v = d.slotcnt * 16
                    else:
                        key = ("e", d.eng)
                        v = d.signo
                    if waited.get(key, 0) >= v:
                        continue
                    if need.get(key, 0) < v:
                        need[key] = v
                for key, v in need.items():
                    waited[key] = v
                    nwaits[ename] += 1
                    if key[0] == "s":
                        handle.wait_ge(ssems[key[1]][key[2]], v)
                    else:
                        ep = (v - 1) // SEM_LIM
                        handle.wait_ge(esems[key[1]][ep], (v - 1) % SEM_LIM + 1)
                bi = ins.fn(handle)
                if ins.dma:
                    bi.then_inc(ssems[ename][ins.slot], 16)
                elif ins.sig:
                    ep = (ins.signo - 1) // SEM_LIM
                    bi.then_inc(esems[ename][ep], 1)
            if ename == "sp":
                for e in self.ENGS:
                    if self.ndma[e]:
                        for s in range(NSLOT):
                            last = self.slot_last.get((e, s))
                            if last is not None:
                                handle.wait_ge(ssems[e][s], last.slotcnt * 16)

        with nc.Block() as block:
            @block.tensor
            def _(e):
                run("pe", e)

            @block.scalar
            def _(e):
                run("act", e)

            @block.vector
            def _(e):
                run("dve", e)

            @block.gpsimd
            def _(e):
                run("pool", e)

            @block.sync
            def _(e):
                run("sp", e)
        self.stats["waits"] = nwaits
        self.stack.close()


S = 2048
D = 1024
DEPTH = 4
DFF = 2816
NFC = DFF // 128
NT = S // 128
ALPHA = (2.0 * DEPTH) ** 0.25
LN_EPS = 1e-5
RMS_EPS = 1e-6
NEG = -30000.0
N_ATT = 2 * 128 * 128
N_CMP = 128 * 247
N_OH = N_ATT + N_CMP
AR_EL = 60 * 1024


def T(const=False):
    return Tok(const)


def t5_bucket_np(dist):
    n = np.maximum(dist, 0)
    nf = np.maximum(n, 1).astype(np.float32)
    large = 16 + (np.log(nf / np.float32(16)) / np.float32(math.log(128 / 16)) * np.float32(16)).astype(np.int32)
    large = np.minimum(large, 31)
    return np.where(n < 16, n, large)


def onehot_consts():
    oh = np.zeros((33, N_OH), np.float32)
    tau, k, q = np.meshgrid(np.arange(2), np.arange(128), np.arange(128), indexing="ij")
    dist = (q - k + 128 * tau).reshape(-1)
    col = np.arange(N_ATT)
    b = t5_bucket_np(dist)
    pos = dist >= 0
    np.add.at(oh, (b[pos], col[pos]), 1.0)
    np.add.at(oh, (np.full(pos.sum(), 31), col[pos]), -1.0)
    oh[32, col[~pos]] = 1.0
    p, m = np.meshgrid(np.arange(128), np.arange(247), indexing="ij")
    dist = (p - 16 * (m - 120) - 31).reshape(-1)
    col = N_ATT + np.arange(N_CMP)
    b = t5_bucket_np(dist)
    pos = dist >= 0
    np.add.at(oh, (b[pos], col[pos]), 1.0)
    np.add.at(oh, (np.full(pos.sum(), 31), col[pos]), -1.0)
    oh[32, col[~pos]] = 1.0
    return oh


class K:
    def __init__(self, nseq, plan, dbg=False, nsa_only=None):
        self.nsa_only = nsa_only
        self.fuse = True
        self.have_hT = False
        self.nxt = None
        self.nseq = nseq
        self.plan = plan
        self.dbg = dbg
        nc = self.nc = bass.Bass("TRN2", target_bir_lowering=False)
        self.P = Prog(nc)
        self.dram_in = {}
        self.dbg_names = set()
        self.build()

    def din(self, name, shape, dt=F32):
        t = self.nc.dram_tensor(name, list(shape), dt, kind="ExternalInput")
        self.dram_in[name] = (tuple(shape), dt)
        return t.ap()

    def dscr(self, name, shape, dt):
        return self.nc.dram_tensor(name, list(shape), dt, kind="Internal").ap()

    def dbg_out(self, name, src_ap, shape, dt, reads):
        if not self.dbg or name in self.dbg_names:
            return
        self.dbg_names.add(name)
        t = self.nc.dram_tensor(name, list(shape), dt, kind="ExternalOutput").ap()
        self.dma("sp", t, src_ap, reads=reads, writes=[Tok()])

    def dma(self, q, out, in_, reads=(), writes=()):
        return self.P.op(q, lambda e, o=out, i=in_: e.dma_start(out=o, in_=i), reads, writes, dma=True)

    def view(self, off, shape, dt):
        n = int(np.prod(shape[1:]))
        if dt == BF16:
            ap = self.ar[:, off:off + n]
        else:
            assert off % 2 == 0
            ap = self.ar[:, off:off + 2 * n].bitcast(F32)
        if len(shape) == 3:
            ap = ap.rearrange("p (a b) -> p a b", a=shape[1])
        elif len(shape) == 4:
            ap = ap.rearrange("p (a b c) -> p a b c", a=shape[1], b=shape[2])
        return ap

    def build(self):
        nc, P = self.nc, self.P
        nseq = self.nseq
        plan = self.plan
        layers = sorted(set(l for (l, s) in plan))
        has = lambda l, s: (l, s) in plan
        x_in = self.din("x", [nseq, S, D])
        cT = self.din("cT", [128, 8, nseq])
        out = nc.dram_tensor("out", [nseq, S, D], F32, kind="ExternalOutput").ap()
        ident_d = self.din("ident", [128, 128])
        ada_w = self.din("ada_w", [DEPTH, D, 6 * D])
        ada_b = self.din("ada_b", [DEPTH, 6 * D])
        ln_g = self.din("ln_g", [DEPTH, 2, D])
        ln_b = self.din("ln_b", [DEPTH, 2, D])
        wu_f = self.din("wu", [DEPTH, NFC, 128, 8, 128])
        wg_f = self.din("wg", [DEPTH, NFC, 128, 8, 128])
        wd_f = self.din("wd", [DEPTH, 128, NFC, D])
        cw_d = self.din("convp", [DEPTH, DFF, 4])
        relb = self.din("rel_bias", [32, 8])
        oh_d = self.din("onehot", [33, N_OH])
        odqk_f = self.din("od_wqk", [2, 16, 128, 8, 128])
        odv_f = self.din("od_wv", [2, 128, 8, D])
        odo_f = self.din("od_wo", [2, 128, 8, D])
        dlam = self.din("diff_lambda", [2, 256])
        dsub = self.din("diff_subln", [2, 128, 1])
        ev_specs = {"ev_wc": [15, 128, 8 * 128], "ev_wvn": [128, 8 * 256], "ev_uq": [8, 128, 3 * 128], "ev_ukvk": [8, 128, 2 * 128],
                    "ev_ukvv": [128, 2 * 512], "ev_w1": [2, 128, 32 * 256], "ev_w2k": [128, 2 * 2 * 128], "ev_w2v": [128, 2 * 64],
                    "ev_peT": [2, 64, 32], "ev_wo": [128, 8 * D]}
        ev_f = {k: self.din(k, [2] + v) for k, v in ev_specs.items()}
        ev_b = {k: self.dscr(k + "_b", [2] + v, BF16) for k, v in ev_specs.items()}
        ev_qn = self.din("ev_qn", [2, 128, 3])
        ev_kvn = self.din("ev_kvn", [2, 128, 2])
        rope_d = self.din("ropeT", [64, S])
        cmask_d = self.din("cmask", [2, 128, 128])
        exp_d = self.din("expmat", [64, 16 * 128])
        ovl_d = self.din("ovl", [127, 32])
        scab_d = self.din("scab", [2, 128, NT * 32])
        selb_d = self.din("selb", [24, 24 * 128])
        cst_b = {"expmat": self.dscr("expmat_b", [64, 16 * 128], BF16), "selb": self.dscr("selb_b", [24, 24 * 128], BF16),
                 "scab": self.dscr("scab_b", [2, 128, NT * 32], BF16)}
        t_cstb = T()
        wu_b = self.dscr("wu_b", [DEPTH, NFC, 128, 8 * 128], BF16)
        wg_b = self.dscr("wg_b", [DEPTH, NFC, 128, 8 * 128], BF16)
        wd_b = self.dscr("wd_b", [DEPTH, 128, NFC * D], BF16)
        odqk_b = self.dscr("odqk_b", [2, 16, 128, 8 * 128], BF16)
        odv_b = self.dscr("odv_b", [2, 128, 8 * D], BF16)
        odo_b = self.dscr("odo_b", [2, 128, 8 * D], BF16)
        mod_d = self.dscr("mod", [DEPTH, nseq, 6 * D], F32)
        G_d = self.dscr("Gd", [8, N_OH], F32)
        self.out = out

        ps = P.psum("ps", [128, 8, 512], F32)
        ps_t = [T() for _ in range(8)]
        ident = P.sbuf("identb", [128, 128], BF16)
        ident_f = P.sbuf("identf", [128, 128], F32)
        ones_b = P.sbuf("onesb", [128, 128], BF16)
        ones_f = P.sbuf("onesf", [128, 128], F32)
        t_ident = T(const=True)
        t_ones = T(const=True)
        hT = P.sbuf("hT", [128, 8, S], BF16)
        t_hT = [T() for _ in range(NT)]
        NB = 5
        bc = P.sbuf("bc", [128, NB, D], F32)
        t_bc = [T() for _ in range(NB)]
        xt = [P.sbuf(f"xt{i}", [128, D], F32) for i in range(2)]
        t_xt = [T(), T()]
        wk = [P.sbuf(f"wk{i}", [128, D], F32) for i in range(2)]
        t_wk = [T() for _ in range(2)]
        hb = [P.sbuf(f"hb{i}", [128, D], BF16) for i in range(2)]
        t_hb = [T(), T()]
        st6 = P.sbuf("st6", [128, 2, 2, 6], F32)
        mv = P.sbuf("mv", [128, 2, 4], F32)
        t_st = [T(), T()]
        t_mv = [T(), T()]
        halo = P.sbuf("halo", [128, NFC, 2], F32)
        t_halo = [T() for _ in range(NFC)]
        cwt = P.sbuf("cwt", [128, NFC, 4], F32)
        t_cw = T()
        cact = P.sbuf("cact", [128, 8, nseq], F32)
        t_cact = T()
        t_adb = [T(), T()]
        t_modsb = [T(), T()]
        tbl = P.sbuf("tbl", [33, 8], F32)
        t_tbl = T()
        Bsb = P.sbuf("Bsb", [128, 8, 3, 128], BF16)
        Bc = P.sbuf("Bc", [128, 8, 247], BF16)
        t_B = T()
        lamt = P.sbuf("lamt", [128, 264], F32)
        t_lam = T()
        subl = P.sbuf("subl", [128, 2], F32)
        t_subl = T()
        eps_t = P.sbuf("eps_t", [128, 2], F32)
        t_eps = T(const=True)
        Md = P.sbuf("Md", [128, 128], BF16)
        ovl = P.sbuf("ovl_s", [127, 32], BF16)
        Zt = P.sbuf("Zt", [128, 512], BF16)
        qkn = P.sbuf("qkn", [128, 8], F32)
        t_qkn = T()
        t_cst = T(const=True)
        self.ar = P.sbuf("arena", [128, AR_EL], BF16)

        t_mod = [[T() for _ in range(nseq)] for _ in range(DEPTH)]
        t_x = [[T() for _ in range(NT)] for _ in range(nseq)]
        t_wub = [T() for _ in range(DEPTH)]
        t_wdb = [T() for _ in range(DEPTH)]
        t_odb = [T() for _ in range(2)]
        t_evb = [T() for _ in range(2)]
        t_G = T()

        def conv(dst, src, tok):
            P.op("pool", lambda e, d=dst, s=src: e.dma_start(out=d, in_=s), writes=[tok], dma=True)
        self.dma("sp", ident_f[:], ident_d[:, :], writes=[t_ident])
        P.op("pool", lambda e: e.dma_start(out=ident[:], in_=ident_d[:, :]), writes=[t_ident], dma=True)
        P.op("pool", lambda e: e.memset(ones_b[:], 1.0), writes=[t_ones])
        P.op("pool", lambda e: e.memset(ones_f[:], 1.0), writes=[t_ones])
        P.op("pool", lambda e: e.memset(eps_t[:, 0:1], RMS_EPS), writes=[t_eps])
        P.op("pool", lambda e: e.memset(eps_t[:, 1:2], LN_EPS), writes=[t_eps])
        for l in layers:
            if has(l, 1):
                for fc0 in range(0, NFC, 11):
                    conv(wu_b[l, fc0:fc0 + 11], wu_f[l, fc0:fc0 + 11].rearrange("f p k j -> f p (k j)"), t_wub[l])
                    conv(wg_b[l, fc0:fc0 + 11], wg_f[l, fc0:fc0 + 11].rearrange("f p k j -> f p (k j)"), t_wub[l])
                conv(wd_b[l], wd_f[l].rearrange("p f d -> p (f d)"), t_wdb[l])
            if has(l, 0) and l % 2 == 0:
                i = l // 2
                for k in ev_specs:
                    if k == "ev_wc":
                        for c0_ in range(0, 15, 5):
                            conv(ev_b[k][i, c0_:c0_ + 5], ev_f[k][i, c0_:c0_ + 5], t_evb[i])
                    else:
                        conv(ev_b[k][i], ev_f[k][i], t_evb[i])
            if has(l, 0) and l % 2 == 1:
                i = l // 2
                conv(odqk_b[i], odqk_f[i].rearrange("f p k j -> f p (k j)"), t_odb[i])
                conv(odv_b[i], odv_f[i].rearrange("p k n -> p (k n)"), t_odb[i])
                conv(odo_b[i], odo_f[i].rearrange("p k n -> p (k n)"), t_odb[i])

        if any(s_ == 0 and l_ % 2 == 0 for (l_, s_) in plan):
            P.op("pool", lambda e: e.dma_start(out=Md[:], in_=cmask_d[0]), writes=[t_cst], dma=True)
            conv(cst_b["expmat"], exp_d, t_cstb)
            conv(cst_b["selb"], selb_d, t_cstb)
            conv(cst_b["scab"], scab_d, t_cstb)
            P.op("pool", lambda e: e.dma_start(out=ovl[:], in_=ovl_d[:, :]), writes=[t_cst], dma=True)
        P.op("pool", lambda e: e.memset(Zt[:], 0.0), writes=[t_cst])
        adw = [self.view(i * 8192, [128, 8, 512], F32) for i in range(2)]
        adb = [self.view(32768 + i * 1024, [128, 512], F32)[0:nseq] for i in range(2)]
        modsb = [self.view(36864 + i * 1024, [128, 512], F32)[0:nseq] for i in range(2)]
        t_adw = [T(), T()]
        self.dma("sp", cact[:], cT[:, :, :], writes=[t_cact])
        P.op("act", lambda e: e.activation(out=cact[:], in_=cact[:], func=AF.Silu), reads=[t_cact], writes=[t_cact])
        ib = 0
        for l in layers:
            for cb in range(12):
                a = adw[ib % 2]
                ta = t_adw[ib % 2]
                ab, tab = adb[ib % 2], t_adb[ib % 2]
                mb, tmb = modsb[ib % 2], t_modsb[ib % 2]
                ib += 1
                self.dma("sp", ab, ada_b[l:l + 1, cb * 512:(cb + 1) * 512].broadcast_to([nseq, 512]), writes=[tab])
                self.dma("sp", a, ada_w[l, :, cb * 512:(cb + 1) * 512].rearrange("(k p) n -> p k n", p=128), writes=[ta])
                bank = 6 + (cb % 2)
                for kc in range(8):
                    P.op("pe", lambda e, a=a, kc=kc, bank=bank: e.matmul(
                        ps[0:nseq, bank, :], cact[:, kc, :], a[:, kc, :], start=(kc == 0), stop=(kc == 7)),
                        reads=[t_cact, ta], writes=[ps_t[bank]])
                P.op("dve", lambda e, mb=mb, ab=ab, bank=bank: e.tensor_tensor(
                    out=mb, in0=ps[0:nseq, bank, :], in1=ab, op=ALU.add),
                    reads=[ps_t[bank], tab], writes=[tmb])
                self.dma("sp", mod_d[l, :, cb * 512:(cb + 1) * 512], mb, reads=[tmb], writes=t_mod[l])
        if self.dbg:
            self.dbg_out("dbg_mod", mod_d[layers[0]], [nseq, 6 * D], F32, t_mod[layers[0]])

        need_bias = any(s == 0 for (l, s) in plan)
        if need_bias:
            P.barrier()
            self.dma("sp", tbl[0:32, :], relb[:, :], writes=[t_tbl])
            P.op("pool", lambda e: e.memset(tbl[32:33, :], NEG), writes=[t_tbl])
            ohb = [self.view(i * 8192, [33, 4096], F32) for i in range(2)]
            gsb_ = [self.view(16384 + i * 8192, [8, 4096], F32) for i in range(2)]
            t_oh = [T(), T()]
            t_g = [T(), T()]
            nch = (N_OH + 4095) // 4096
            for c in range(nch):
                c0 = c * 4096
                w = min(4096, N_OH - c0)
                o_, to = ohb[c % 2], t_oh[c % 2]
                g_, tg = gsb_[c % 2], t_g[c % 2]
                self.dma("sp", o_[0:33, 0:w], oh_d[:, c0:c0 + w], writes=[to])
                for s0 in range(0, w, 512):
                    sw = min(512, w - s0)
                    bank = (s0 // 512) % 2
                    P.op("pe", lambda e, o_=o_, s0=s0, sw=sw, bank=bank: e.matmul(
                        ps[0:8, bank, 0:sw], tbl[0:33, :], o_[0:33, s0:s0 + sw], start=True, stop=True),
                        reads=[t_tbl, to], writes=[ps_t[bank]])
                    P.op("act", lambda e, g_=g_, s0=s0, sw=sw, bank=bank: e.copy(out=g_[0:8, s0:s0 + sw], in_=ps[0:8, bank, 0:sw]),
                         reads=[ps_t[bank]], writes=[tg])
                self.dma("sp", G_d[:, c0:c0 + w], g_[0:8, 0:w], reads=[tg], writes=[t_G])
            for tau in range(2):
                P.op("pool", lambda e, tau=tau: e.dma_start(
                    out=Bsb[:, :, tau, :], in_=G_d[:, tau * 16384:(tau + 1) * 16384].rearrange("h (k q) -> k h q", k=128)),
                    reads=[t_G], writes=[t_B], dma=True)
            P.op("pool", lambda e: e.dma_start(out=Bc[:], in_=G_d[:, N_ATT:N_OH].rearrange("h (p m) -> p h m", p=128)),
                 reads=[t_G], writes=[t_B], dma=True)
            for h_ in range(8):
                P.op("pool", lambda e, h_=h_: e.dma_start(out=Bsb[:, h_, 2, :], in_=cmask_d[1]), writes=[t_B], dma=True)
            self.dbg_out("dbg_Bsb", Bsb[:], [128, 8, 3, 128], BF16, [t_B])
            self.dbg_out("dbg_Bc", Bc[:], [128, 8, 247], BF16, [t_B])
        P.barrier()

        def act_recip(out_ap, in_ap, rd, wr):
            P.op("act", lambda e: e.activation(out=out_ap, in_=in_ap, func=AF.Ln), reads=rd, writes=wr)
            P.op("act", lambda e: e.activation(out=out_ap, in_=out_ap, func=AF.Exp, scale=-1.0), reads=wr, writes=wr)

        def act_rstd(out_ap, in_ap, scale, eps_ap, rd, wr):
            P.op("act", lambda e: e.activation(out=out_ap, in_=in_ap, func=AF.Ln, bias=eps_ap, scale=scale), reads=rd + [t_eps], writes=wr)
            P.op("act", lambda e: e.activation(out=out_ap, in_=out_ap, func=AF.Exp, scale=-0.5), reads=wr, writes=wr)

        def bc_load(slot, src, rd, plus_one):
            self.dma("sp", bc[:, slot, :], src.broadcast_to([128, D]), reads=rd, writes=[t_bc[slot]])
            if plus_one:
                P.op("pool", lambda e, slot=slot: e.tensor_scalar_add(out=bc[:, slot, :], in0=bc[:, slot, :], scalar1=1.0),
                     reads=[t_bc[slot]], writes=[t_bc[slot]])

        def emit_hT(b, l, sub, xsrc):
            load_mod_h(b, l, sub)
            for t in range(NT):
                xi = t % 2
                self.dma("sp", xt[xi][:], xsrc[b, t * 128:(t + 1) * 128, :], reads=[t_x[b][t]], writes=[t_xt[xi]])
                h_from_x(xi, t)()

        def h_from_x(xi, t, tbank=7, src=None, tsrc=None):
            if src is None:
                src, tsrc = xt[xi], t_xt[xi]
            hps = ps[:, 0:2, :].rearrange("p a n -> p (a n)")
            P.op("dve", lambda e: e.tensor_tensor(out=hps, in0=src[:], in1=bc[:, 0, :], op=ALU.mult),
                 reads=[tsrc, t_bc[0]], writes=[ps_t[0], ps_t[1]])
            P.op("dve", lambda e: e.tensor_tensor(out=hb[xi][:], in0=hps, in1=bc[:, 1, :], op=ALU.add),
                 reads=[ps_t[0], ps_t[1], t_bc[1]], writes=[t_hb[xi]])

            def fin():
                psb = ps[:, tbank, :].bitcast(BF16)
                for kc in range(8):
                    P.op("pe", lambda e, kc=kc: e.transpose(
                        psb[:, kc * 128:(kc + 1) * 128], hb[xi][:, kc * 128:(kc + 1) * 128], ident[:]),
                        reads=[t_hb[xi], t_ident], writes=[ps_t[tbank]])
                P.op("act", lambda e: e.copy(
                    out=hT[:, :, t * 128:(t + 1) * 128], in_=psb.rearrange("p (k j) -> p k j", k=8)),
                    reads=[ps_t[tbank]], writes=[t_hT[t]])
            return fin

        def load_mod_h(b, l, sub):
            o = 0 if sub == 0 else 3
            bc_load(0, mod_d[l, b:b + 1, (o + 1) * D:(o + 2) * D], [t_mod[l][b]], True)
            bc_load(1, mod_d[l, b:b + 1, (o + 0) * D:(o + 1) * D], [t_mod[l][b]], False)

        def begin_sublayer(b, l, sub, xsrc):
            if not self.have_hT:
                emit_hT(b, l, sub, xsrc)
            epi_setup(b, l, sub)
            if self.nxt is not None:
                load_mod_h(b, self.nxt[0], self.nxt[1])

        def epilogue(b, l, sub, t, ybank, xsrc, tbank=7):
            xi = t % 2
            w_ = wk[xi]
            tw_ = t_wk[xi]
            st_ = st6[:, xi]
            mv_ = mv[:, xi]
            nxt = self.nxt
            self.dma("sp", xt[xi][:], xsrc[b, t * 128:(t + 1) * 128, :], reads=[t_x[b][t]], writes=[t_xt[xi]])
            yv = ps[:, ybank:ybank + 2, :].rearrange("p a n -> p (a n)")
            yt = [ps_t[ybank], ps_t[ybank + 1]]
            P.op("dve", lambda e: e.tensor_tensor(out=yv, in0=yv, in1=bc[:, 2, :], op=ALU.mult), reads=yt + [t_bc[2]], writes=yt)
            P.op("dve", lambda e: e.scalar_tensor_tensor(out=w_[:], in0=xt[xi][:], scalar=ALPHA, in1=yv, op0=ALU.mult, op1=ALU.add),
                 reads=[t_xt[xi]] + yt, writes=[tw_])
            for hh in range(2):
                P.op("dve", lambda e, hh=hh: e.bn_stats(out=st_[:, hh, :], in_=w_[:, hh * 512:(hh + 1) * 512]),
                     reads=[tw_], writes=[t_st[xi]])
            P.op("dve", lambda e: e.bn_aggr(out=mv_[:, 0:2], in_=st_), reads=[t_st[xi]], writes=[t_mv[xi]])
            act_rstd(mv_[:, 2:3], mv_[:, 1:2], 1.0, eps_t[:, 1:2], [t_mv[xi]], [t_mv[xi]])
            P.op("dve", lambda e: e.scalar_tensor_tensor(out=mv_[:, 3:4], in0=mv_[:, 0:1], scalar=-1.0, in1=mv_[:, 2:3],
                                                         op0=ALU.mult, op1=ALU.mult), reads=[t_mv[xi]], writes=[t_mv[xi]])
            P.op("act", lambda e: e.activation(out=w_[:], in_=w_[:], func=AF.Identity, bias=mv_[:, 3:4], scale=mv_[:, 2:3]),
                 reads=[tw_, t_mv[xi]], writes=[tw_])

            def stage2():
                P.op("pool", lambda e: e.tensor_tensor(out=w_[:], in0=w_[:], in1=bc[:, 3, :], op=ALU.mult),
                     reads=[tw_, t_bc[3]], writes=[tw_])
                P.op("pool", lambda e: e.tensor_tensor(out=w_[:], in0=w_[:], in1=bc[:, 4, :], op=ALU.add),
                     reads=[tw_, t_bc[4]], writes=[tw_])
                self.dma("sp", out[b, t * 128:(t + 1) * 128, :], w_[:], reads=[tw_], writes=[t_x[b][t]])
                if nxt is not None:
                    return h_from_x(xi, t, tbank, w_, tw_)
                return None
            return stage2

        def ep_pipeline(tiles, mm, ybank_of, b, l, sub, xsrc, tbank):
            s2_prev = None
            fins = []
            for t in tiles:
                mm(t)
                while len(fins) > 1:
                    f_ = fins.pop(0)
                    if f_ is not None:
                        f_()
                s2_cur = epilogue(b, l, sub, t, ybank_of(t), xsrc, tbank)
                if s2_prev is not None:
                    fins.append(s2_prev())
                s2_prev = s2_cur
            while len(fins) > 1:
                f_ = fins.pop(0)
                if f_ is not None:
                    f_()
            if s2_prev is not None:
                fins.append(s2_prev())
            for f_ in fins:
                if f_ is not None:
                    f_()

        def epi_setup(b, l, sub):
            o = 2 if sub == 0 else 5
            bc_load(2, mod_d[l, b:b + 1, o * D:(o + 1) * D], [t_mod[l][b]], True)
            bc_load(3, ln_g[l, sub:sub + 1, :], [], False)
            bc_load(4, ln_b[l, sub:sub + 1, :], [], False)

        def ffn(b, l, xsrc):
            P.barrier()
            wd = self.view(0, [128, NFC, D], BF16)
            t_wd = T()
            actT = self.view(22528, [128, NFC, 1024], BF16)
            t_act = [T() for _ in range(NFC)]
            NW = 4
            wug = [self.view(45056 + i * 2048, [128, 2, 8, 128], BF16) for i in range(NW)]
            t_wug = [T() for _ in range(NW)]
            gsb = [self.view(53248 + i * 1032, [128, 514], F32) for i in range(2)]
            t_gsb = [T(), T()]
            tmp = [self.view(55312 + i * 1024, [128, 512], F32) for i in range(3)]
            t_tmp = [T() for _ in range(3)]
            begin_sublayer(b, l, 1, xsrc)
            self.dbg_out("dbg_hT", hT[:], [128, 8, S], BF16, t_hT)
            self.dma("sp", wd, wd_b[l].rearrange("p (f d) -> p f d", f=NFC), reads=[t_wdb[l]], writes=[t_wd])
            self.dma("sp", cwt[:], cw_d[l].rearrange("(f p) k -> p f k", p=128), writes=[t_cw])
            iw = 0
            ih = 0
            for tb in range(2):
                for fc in range(NFC):
                    w = wug[iw % NW]
                    tw = t_wug[iw % NW]
                    iw += 1
                    self.dma("sp", w[:, 0], wu_b[l, fc].rearrange("p (k j) -> p k j", k=8), reads=[t_wub[l]], writes=[tw])
                    self.dma("sp", w[:, 1], wg_b[l, fc].rearrange("p (k j) -> p k j", k=8), reads=[t_wub[l]], writes=[tw])
                    for half in range(2):
                        c0 = tb * 1024 + half * 512
                        a0 = half * 512
                        hts = t_hT[c0 // 128:c0 // 128 + 4]
                        bu = ih % 2
                        bg = 2 + ih % 2
                        g = gsb[ih % 2]
                        tg = t_gsb[ih % 2]
                        ih += 1
                        for kc in range(8):
                            P.op("pe", lambda e, w=w, kc=kc, bu=bu, c0=c0: e.matmul(ps[:, bu, :], w[:, 0, kc, :], hT[:, kc, c0:c0 + 512],
                                                                                   start=(kc == 0), stop=(kc == 7)),
                                 reads=[tw] + hts, writes=[ps_t[bu]])
                        for kc in range(8):
                            P.op("pe", lambda e, w=w, kc=kc, bg=bg, c0=c0: e.matmul(ps[:, bg, :], w[:, 1, kc, :], hT[:, kc, c0:c0 + 512],
                                                                                   start=(kc == 0), stop=(kc == 7)),
                                 reads=[tw] + hts, writes=[ps_t[bg]])
                        if c0 == 0:
                            P.op("pool", lambda e, g=g: e.memset(g[:, 0:2], 0.0), writes=[tg])
                        else:
                            P.op("pool", lambda e, g=g, fc=fc: e.tensor_copy(out=g[:, 0:2], in_=halo[:, fc, :]),
                                 reads=[t_halo[fc]], writes=[tg])
                        P.op("act", lambda e, g=g, bg=bg: e.copy(out=g[:, 2:514], in_=ps[:, bg, :]), reads=[ps_t[bg]], writes=[tg])
                        P.op("pool", lambda e, g=g, fc=fc: e.tensor_copy(out=halo[:, fc, :], in_=g[:, 512:514]),
                             reads=[tg], writes=[t_halo[fc]])
                        P.op("dve", lambda e, g=g, fc=fc: e.tensor_scalar(out=tmp[0], in0=g[:, 2:514], scalar1=cwt[:, fc, 2:3],
                                                                          scalar2=cwt[:, fc, 3:4], op0=ALU.mult, op1=ALU.add),
                             reads=[tg, t_cw], writes=[t_tmp[0]])
                        P.op("dve", lambda e, g=g, fc=fc: e.scalar_tensor_tensor(out=tmp[1], in0=g[:, 1:513], scalar=cwt[:, fc, 1:2],
                                                                                 in1=tmp[0], op0=ALU.mult, op1=ALU.add),
                             reads=[tg, t_cw, t_tmp[0]], writes=[t_tmp[1]])
                        P.op("dve", lambda e, g=g, fc=fc: e.scalar_tensor_tensor(out=tmp[2], in0=g[:, 0:512], scalar=cwt[:, fc, 0:1],
                                                                                 in1=tmp[1], op0=ALU.mult, op1=ALU.add),
                             reads=[tg, t_cw, t_tmp[1]], writes=[t_tmp[2]])
                        P.op("act", lambda e: e.activation(out=tmp[0], in_=tmp[2], func=AF.Silu),
                             reads=[t_tmp[2]], writes=[t_tmp[0]])
                        P.op("dve", lambda e, fc=fc, bu=bu, a0=a0: e.tensor_tensor(out=actT[:, fc, a0:a0 + 512], in0=tmp[0], in1=ps[:, bu, :],
                                                                                  op=ALU.mult),
                             reads=[t_tmp[0], ps_t[bu]], writes=[t_act[fc]])
                def mm_down(t):
                    tt = t % 8
                    yb = 4 if tt % 2 == 0 else 6
                    for dh in range(2):
                        for fc in range(NFC):
                            P.op("pe", lambda e, fc=fc, tt=tt, dh=dh, yb=yb: e.matmul(
                                ps[:, yb + dh, :], actT[:, fc, tt * 128:(tt + 1) * 128], wd[:, fc, dh * 512:(dh + 1) * 512],
                                start=(fc == 0), stop=(fc == NFC - 1)),
                                reads=[t_act[fc], t_wd], writes=[ps_t[yb + dh]])
                ep_pipeline([tb * 8 + tt for tt in range(8)], mm_down, lambda t: 4 if t % 2 == 0 else 6, b, l, 1, xsrc, 2)

        def diff_attn(b, l, xsrc):
            i = l // 2
            lam_init = 0.8 - 0.6 * math.exp(-0.3 * l)
            P.barrier()
            V = self.view(0, [128, NT, D], BF16)
            t_V = [T() for _ in range(NT)]
            oT = self.view(16384, [128, 8, S], BF16)
            t_oT = [[T() for _ in range(4)] for _ in range(8)]
            qk = [[self.view(32768 + (2 * j + c) * 2048, [128, S], BF16) for c in range(2)] for j in range(2)]
            t_qk = [[T(), T()], [T(), T()]]
            E = [self.view(40960 + j * 512, [128, 512], BF16) for j in range(4)]
            t_E = [T() for _ in range(4)]
            wo = self.view(43008, [128, 8, D], BF16)
            t_wo = T()
            wqk = [self.view(51200 + j * 2048, [128, 2, 8, 128], BF16) for j in range(2)]
            t_wqk = [T(), T()]
            f32w = [self.view(55296 + j * 1024, [128, 512], F32) for j in range(4)]
            t_f = [T() for _ in range(4)]

            begin_sublayer(b, l, 0, xsrc)
            self.dma("sp", lamt[:, 0:256], dlam[i:i + 1, :].broadcast_to([128, 256]), writes=[t_lam])
            P.op("dve", lambda e: e.tensor_tensor(out=lamt[:, 0:64], in0=lamt[:, 0:64], in1=lamt[:, 64:128], op=ALU.mult),
                 reads=[t_lam], writes=[t_lam])
            P.op("dve", lambda e: e.tensor_tensor(out=lamt[:, 128:192], in0=lamt[:, 128:192], in1=lamt[:, 192:256], op=ALU.mult),
                 reads=[t_lam], writes=[t_lam])
            P.op("dve", lambda e: e.reduce_sum(out=lamt[:, 256:257], in_=lamt[:, 0:64], axis=AX.X), reads=[t_lam], writes=[t_lam])
            P.op("dve", lambda e: e.reduce_sum(out=lamt[:, 257:258], in_=lamt[:, 128:192], axis=AX.X), reads=[t_lam], writes=[t_lam])
            P.op("act", lambda e: e.activation(out=lamt[:, 258:260], in_=lamt[:, 256:258], func=AF.Exp), reads=[t_lam], writes=[t_lam])
            P.op("dve", lambda e: e.scalar_tensor_tensor(out=lamt[:, 260:261], in0=lamt[:, 259:260], scalar=-lam_init,
                                                         in1=lamt[:, 258:259], op0=ALU.add, op1=ALU.subtract),
                 reads=[t_lam], writes=[t_lam])
            self.dma("sp", subl[:, 0:1], dsub[i], writes=[t_subl])
            P.op("pool", lambda e: e.tensor_scalar_mul(out=subl[:, 0:1], in0=subl[:, 0:1], scalar1=1.0 - lam_init),
                 reads=[t_subl], writes=[t_subl])
            self.dma("sp", wo, odv_b[i].rearrange("p (k n) -> p k n", k=8), reads=[t_odb[i]], writes=[t_wo])
            for t in range(NT):
                for dh in range(2):
                    bank = 5 + dh
                    for kc in range(8):
                        P.op("pe", lambda e, t=t, dh=dh, kc=kc, bank=bank: e.matmul(
                            ps[:, bank, :], hT[:, kc, t * 128:(t + 1) * 128], wo[:, kc, dh * 512:(dh + 1) * 512],
                            start=(kc == 0), stop=(kc == 7)), reads=[t_hT[t], t_wo], writes=[ps_t[bank]])
                    if dh == 0:
                        P.op("act", lambda e, t=t, bank=bank: e.copy(out=V[:, t, 0:512], in_=ps[:, bank, :]),
                             reads=[ps_t[bank]], writes=[t_V[t]])
                    else:
                        P.op("dve", lambda e, t=t, bank=bank: e.tensor_copy(out=V[:, t, 512:1024], in_=ps[:, bank, :]),
                             reads=[ps_t[bank]], writes=[t_V[t]])
            Esum = [self.view(59392 + m_ * 1024, [128, 512], F32) for m_ in range(2)]
            t_Es = [T(), T()]

            def hq_block(h, Q, qT, kT, tq, tk, sidx, par, deferred):
                nk = 4 * Q + 4
                tiles = [(kt, m) for kt in range(nk) for m in range(2)]
                ob = (3, 5) if par == 0 else (4, 6)
                LAG = 2
                pend = []

                def do_scores(kt, m, sidx):
                    jj = kt - 4 * Q
                    c0 = max(0, jj) * 128
                    sb = sidx % 3
                    eb = sidx % 4
                    q0 = Q * 512
                    hasb = kt >= 4 * Q - 1
                    P.op("pe", lambda e: e.matmul(
                        ps[:, sb, c0:512], kT[m * 64:(m + 1) * 64, kt * 128:(kt + 1) * 128],
                        qT[m * 64:(m + 1) * 64, q0 + c0:q0 + 512], start=True, stop=not hasb),
                        reads=[tq, tk], writes=[ps_t[sb]])
                    if hasb:
                        if jj < 0:
                            rhs = Bsb[:, h, 1, :]
                            cs, ce = 0, 128
                        elif jj == 3:
                            rhs = Bsb[:, h, 0, :]
                            cs, ce = 384, 512
                        else:
                            rhs = Bsb[:, h, 0:2, :].rearrange("p t q -> p (t q)")
                            cs, ce = c0, c0 + 256
                        P.op("pe", lambda e: e.matmul(ps[:, sb, cs:ce], ident[:], rhs, start=False, stop=True),
                             reads=[t_ident, t_B], writes=[ps_t[sb]])
                    P.op("act", lambda e: e.activation(out=E[eb][:, c0:512], in_=ps[:, sb, c0:512], func=AF.Exp),
                         reads=[ps_t[sb]], writes=[t_E[eb]])
                    if kt == 0:
                        P.op("dve", lambda e: e.tensor_copy(out=Esum[m][:, :], in_=E[eb][:, :]), reads=[t_E[eb]], writes=[t_Es[m]])
                    else:
                        P.op("dve", lambda e: e.tensor_tensor(out=Esum[m][:, c0:512], in0=Esum[m][:, c0:512], in1=E[eb][:, c0:512], op=ALU.add),
                             reads=[t_E[eb], t_Es[m]], writes=[t_Es[m]])
                    return (kt, m, c0, eb)

                def do_pv(kt, m, c0, eb):
                    P.op("pe", lambda e: e.matmul(ps[:, ob[m], c0:512], V[:, kt, h * 128:(h + 1) * 128], E[eb][:, c0:512],
                                                  start=(kt == 0), stop=(kt == nk - 1)),
                         reads=[t_V[kt], t_E[eb]], writes=[ps_t[ob[m]]])

                for ti, (kt, m) in enumerate(tiles):
                    pend.append(do_scores(kt, m, sidx))
                    sidx += 1
                    if len(pend) > LAG:
                        do_pv(*pend.pop(0))
                    if ti == 5 and deferred is not None:
                        deferred()
                        deferred = None
                while pend:
                    do_pv(*pend.pop(0))
                if deferred is not None:
                    deferred()
                r0, r1, u0, u1 = f32w
                P.op("pe", lambda e: e.matmul(ps[:, 7, :], ones_f[:], Esum[0], start=True, stop=True), reads=[t_ones, t_Es[0]], writes=[ps_t[7]])
                act_recip(r0, ps[:, 7, :], [ps_t[7]], [t_f[0]])
                P.op("pe", lambda e: e.matmul(ps[:, 7, :], ones_f[:], Esum[1], start=True, stop=True), reads=[t_ones, t_Es[1]], writes=[ps_t[7]])
                act_recip(r1, ps[:, 7, :], [ps_t[7]], [t_f[1]])
                P.op("dve", lambda e: e.tensor_tensor(out=u0, in0=ps[:, ob[0], :], in1=r0, op=ALU.mult),
                     reads=[ps_t[ob[0]], t_f[0]], writes=[t_f[2]])
                P.op("dve", lambda e: e.tensor_tensor(out=u1, in0=ps[:, ob[1], :], in1=r1, op=ALU.mult),
                     reads=[ps_t[ob[1]], t_f[1]], writes=[t_f[3]])
                P.op("dve", lambda e: e.scalar_tensor_tensor(out=u0, in0=u1, scalar=lamt[:, 260:261], in1=u0,
                                                             op0=ALU.mult, op1=ALU.add),
                     reads=[t_f[3], t_f[2], t_lam], writes=[t_f[2]])
                P.op("dve", lambda e: e.tensor_tensor(out=r0, in0=u0, in1=u0, op=ALU.mult), reads=[t_f[2]], writes=[t_f[0]])

                def tail():
                    P.op("pe", lambda e: e.matmul(ps[:, 7, :], ones_f[:], r0, start=True, stop=True),
                         reads=[t_ones, t_f[0]], writes=[ps_t[7]])
                    act_rstd(r1, ps[:, 7, :], 1.0 / 128.0, eps_t[:, 0:1], [ps_t[7]], [t_f[1]])
                    P.op("dve", lambda e: e.scalar_tensor_tensor(out=oT[:, h, Q * 512:(Q + 1) * 512], in0=u0, scalar=subl[:, 0:1], in1=r1,
                                                                 op0=ALU.mult, op1=ALU.mult),
                         reads=[t_f[2], t_f[1], t_subl], writes=[t_oT[h][Q]])
                return sidx, tail

            sidx = 0
            par = 0
            deferred = None
            for h in range(8):
                j2 = h % 2
                w = wqk[j2]
                tw = t_wqk[j2]
                qT, kT = qk[j2]
                tq, tk = t_qk[j2]
                self.dma("sp", w[:, 0], odqk_b[i, h].rearrange("p (k j) -> p k j", k=8), reads=[t_odb[i]], writes=[tw])
                self.dma("sp", w[:, 1], odqk_b[i, 8 + h].rearrange("p (k j) -> p k j", k=8), reads=[t_odb[i]], writes=[tw])
                for c in range(2):
                    for tb in range(4):
                        for kc in range(8):
                            P.op("pe", lambda e, w=w, c=c, kc=kc, tb=tb: e.matmul(
                                ps[:, 7, :], w[:, c, kc, :], hT[:, kc, tb * 512:(tb + 1) * 512], start=(kc == 0), stop=(kc == 7)),
                                reads=[tw] + t_hT[tb * 4:tb * 4 + 4], writes=[ps_t[7]])
                        dst = qT if c == 0 else kT
                        P.op("act", lambda e, dst=dst, tb=tb, c=c: e.activation(
                            out=dst[:, tb * 512:(tb + 1) * 512], in_=ps[:, 7, :], func=AF.Identity, scale=(0.125 if c == 0 else 1.0)),
                            reads=[ps_t[7]], writes=[tq if c == 0 else tk])
                for Q in range(4):
                    sidx, deferred = hq_block(h, Q, qT, kT, tq, tk, sidx, par, deferred)
                    par ^= 1
            if deferred is not None:
                deferred()
            self.dbg_out("dbg_oT", oT, [128, 8, S], BF16, [t for r in t_oT for t in r])
            self.dma("sp", wo, odo_b[i].rearrange("p (k n) -> p k n", k=8), reads=[t_odb[i]], writes=[t_wo])
            def mm_wo(t):
                yb = 3 if t % 2 == 0 else 5
                for dh in range(2):
                    for h in range(8):
                        P.op("pe", lambda e, t=t, dh=dh, h=h, yb=yb: e.matmul(
                            ps[:, yb + dh, :], oT[:, h, t * 128:(t + 1) * 128], wo[:, h, dh * 512:(dh + 1) * 512],
                            start=(h == 0), stop=(h == 7)), reads=[t_oT[h][t // 4], t_wo], writes=[ps_t[yb + dh]])
            ep_pipeline(list(range(NT)), mm_wo, lambda t: 3 if t % 2 == 0 else 5, b, l, 0, xsrc, 7)

        def run_tiles(specs, acc, E, t_E, sidx, LAG=2):
            P.op("pe", lambda e: e.matmul(ps[:, acc, :], Zt[:, 0:128], Zt[:], start=True, stop=False),
                 reads=[t_cst], writes=[ps_t[acc]])
            pend = []
            n = len(specs)

            def scores(sp, sidx):
                sb = sidx % 3
                eb = sidx % 4
                c0, c1 = sp["c0"], sp["c1"]
                ex = sp["extra"]
                P.op("pe", lambda e: e.matmul(ps[:, sb, c0:c1], sp["lhsT"], sp["rhs"], start=True, stop=(len(ex) == 0)),
                     reads=sp["rd"], writes=[ps_t[sb]])
                for xi_, (xl, xr, cs, ce, xrd) in enumerate(ex):
                    P.op("pe", lambda e, xl=xl, xr=xr, cs=cs, ce=ce, last=(xi_ == len(ex) - 1): e.matmul(
                        ps[:, sb, cs:ce], xl, xr, start=False, stop=last), reads=xrd, writes=[ps_t[sb]])
                P.op("act", lambda e: e.activation(out=E[eb][:, c0:c1], in_=ps[:, sb, c0:c1], func=AF.Exp),
                     reads=[ps_t[sb]], writes=[t_E[eb]])
                return (sp, eb)

            def pv(sp, eb, idx):
                c0, c1 = sp["c0"], sp["c1"]
                P.op("pe", lambda e: e.matmul(ps[:, acc, c0:c1], sp["v"], E[eb][:, c0:c1], start=False, stop=(idx == n - 1)),
                     reads=sp["vrd"] + [t_E[eb]], writes=[ps_t[acc]])

            done = 0
            for sp in specs:
                pend.append(scores(sp, sidx))
                sidx += 1
                if len(pend) > LAG:
                    a_, b_ = pend.pop(0)
                    pv(a_, b_, done)
                    done += 1
            while pend:
                a_, b_ = pend.pop(0)
                pv(a_, b_, done)
                done += 1
            return sidx

        def even_attn(b, l, xsrc):
            i = l // 2
            evb = {k: v[i] for k, v in ev_b.items()}
            rd_w = [t_evb[i]]
            sc_mla = 96 ** -0.5
            P.barrier()
            off = [0]

            def alloc(n):
                o = off[0]
                off[0] = o + n + (n % 2)
                assert off[0] <= AR_EL, off[0]
                return o

            oM = self.view(alloc(4 * S), [128, 4, S], BF16)
            t_oM = [[T() for _ in range(4)] for _ in range(4)]
            t_oN = [[T() for _ in range(4)] for _ in range(4)]
            E = [self.view(alloc(512), [128, 512], BF16) for _ in range(4)]
            t_E = [T() for _ in range(4)]
            fw_ = [self.view(alloc(1024), [128, 512], F32) for _ in range(4)]
            t_fw = [T() for _ in range(4)]
            wcb = [self.view(alloc(3 * 1024), [128, 3, 8, 128], BF16) for _ in range(1)]
            t_wcb = [T()]
            base_off = off[0]
            cqT = self.view(alloc(3 * S), [128, 3, S], BF16)
            ckvT = self.view(alloc(2 * S), [128, 2, S], BF16)
            krT = self.view(alloc(S), [128, S], BF16)
            t_cq = [T() for _ in range(4)]
            t_ckv = [T() for _ in range(4)]
            t_kr = [T() for _ in range(4)]
            rope = self.view(alloc(2 * S), [128, S], F32)
            t_rope = T()
            lat = self.view(alloc(3 * 1024), [128, 3, 512], F32)
            t_lat = [T() for _ in range(3)]
            Vm = self.view(alloc(NT * 1024), [128, NT, 8, 128], BF16)
            t_Vm = [T() for _ in range(NT)]
            QK = [[self.view(alloc(S), [128, S], BF16) for _ in range(2)] for _ in range(2)]
            t_QK = [[T(), T()], [T(), T()]]
            wcoff = base_off - 3 * 1024
            uqb = [self.view(wcoff + j_ * 384, [128, 3, 128], BF16) for j_ in range(2)]
            ukb = [self.view(wcoff + 768 + j_ * 256, [128, 2, 128], BF16) for j_ in range(2)]
            t_ub = [T(), T()]
            ukv = self.view(wcoff + 1280, [128, 2, 512], BF16)
            t_ukv = T()

            begin_sublayer(b, l, 0, xsrc)
            self.dma("sp", rope[0:64, :], rope_d[:, :], writes=[t_rope])
            self.dma("sp", qkn[:, 0:3], ev_qn[i], writes=[t_qkn])
            self.dma("sp", qkn[:, 3:5], ev_kvn[i], writes=[t_qkn])
            wc = wcb[0]
            for (nch, ch0, gcol, dstT, tdst) in [(3, 0, 0, cqT, t_cq), (2, 3, 3, ckvT, t_ckv)]:
                for c in range(nch):
                    self.dma("sp", wc[:, c], evb["ev_wc"][ch0 + c].rearrange("p (k j) -> p k j", k=8), reads=rd_w, writes=[t_wcb[0]])
                for tb in range(4):
                    for c in range(nch):
                        bank = 6 + (c % 2)
                        for kc in range(8):
                            P.op("pe", lambda e, c=c, kc=kc, tb=tb, bank=bank: e.matmul(
                                ps[:, bank, :], wc[:, c, kc, :], hT[:, kc, tb * 512:(tb + 1) * 512], start=(kc == 0), stop=(kc == 7)),
                                reads=[t_wcb[0]] + t_hT[tb * 4:tb * 4 + 4], writes=[ps_t[bank]])
                        P.op("act", lambda e, c=c, bank=bank: e.copy(out=lat[:, c, :], in_=ps[:, bank, :]),
                             reads=[ps_t[bank]], writes=[t_lat[c]])
                        sq = fw_[c % 2]
                        P.op("act", lambda e, sq=sq, bank=bank: e.activation(out=sq, in_=ps[:, bank, :], func=AF.Square),
                             reads=[ps_t[bank]], writes=[t_fw[c % 2]])
                        P.op("pe", lambda e, sq=sq, c=c, nch=nch: e.matmul(ps[:, 5, :], ones_f[:], sq, start=(c == 0), stop=(c == nch - 1)),
                             reads=[t_ones, t_fw[c % 2]], writes=[ps_t[5]])
                    rs = fw_[2]
                    act_rstd(rs, ps[:, 5, :], 1.0 / (128.0 * nch), eps_t[:, 0:1], [ps_t[5]], [t_fw[2]])
                    for c in range(nch):
                        P.op("dve", lambda e, c=c, tb=tb, rs=rs, dstT=dstT, gcol=gcol: e.scalar_tensor_tensor(
                            out=dstT[:, c, tb * 512:(tb + 1) * 512], in0=lat[:, c, :], scalar=qkn[:, gcol + c:gcol + c + 1], in1=rs,
                            op0=ALU.mult, op1=ALU.mult), reads=[t_lat[c], t_fw[2], t_qkn], writes=[tdst[tb]])

            def rope_rot(psb, dst, cols, scale, rd, wr):
                tA, tB = fw_[0], fw_[1]
                P.op("dve", lambda e: e.scalar_tensor_tensor(out=tA[0:32, :], in0=ps[32:64, psb, :], scalar=scale, in1=rope[32:64, cols],
                                                             op0=ALU.mult, op1=ALU.mult),
                     reads=[ps_t[psb], t_rope] + rd, writes=[t_fw[0]])
                P.op("dve", lambda e: e.scalar_tensor_tensor(out=tB[0:32, :], in0=ps[0:32, psb, :], scalar=scale, in1=rope[0:32, cols],
                                                             op0=ALU.mult, op1=ALU.mult),
                     reads=[ps_t[psb], t_rope] + rd, writes=[t_fw[1]])
                P.op("pool", lambda e: e.tensor_tensor(out=dst[0:32, cols], in0=tA[0:32, :], in1=tB[0:32, :], op=ALU.add),
                     reads=[t_fw[0], t_fw[1]], writes=wr)

            self.dma("sp", wc[:, 0], evb["ev_wc"][5].rearrange("p (k j) -> p k j", k=8), reads=rd_w, writes=[t_wcb[0]])
            for tb in range(4):
                for kc in range(8):
                    P.op("pe", lambda e, kc=kc, tb=tb: e.matmul(ps[:, 7, :], wc[:, 0, kc, :], hT[:, kc, tb * 512:(tb + 1) * 512],
                                                                start=(kc == 0), stop=(kc == 7)),
                         reads=[t_wcb[0]] + t_hT[tb * 4:tb * 4 + 4], writes=[ps_t[7]])
                rope_rot(7, krT, slice(tb * 512, (tb + 1) * 512), 1.0, [], [t_kr[tb]])

            P.barrier()
            P.op("pool", lambda e: e.memset(Vm.rearrange("p t h d -> p (t h) d")[:, :, 64:128], 1.0), writes=t_Vm)
            self.dma("sp", ukv, evb["ev_ukvv"].rearrange("p (k n) -> p k n", k=2), reads=rd_w, writes=[t_ukv])
            for t in range(NT):
                for kc in range(2):
                    P.op("pe", lambda e, t=t, kc=kc: e.matmul(ps[:, 6, :], ckvT[:, kc, t * 128:(t + 1) * 128], ukv[:, kc, :],
                                                              start=(kc == 0), stop=(kc == 1)),
                         reads=[t_ckv[t // 4], t_ukv], writes=[ps_t[6]])
                P.op("act", lambda e, t=t: e.copy(out=Vm[:, t, :, 0:64], in_=ps[:, 6, :].rearrange("p (h d) -> p h d", h=8)),
                     reads=[ps_t[6]], writes=[t_Vm[t]])
            for j in range(2):
                for c in range(2):
                    P.op("pool", lambda e, j=j, c=c: e.memset(QK[j][c][32:64, :], 0.0), writes=[t_QK[j][c]])
                P.op("pool", lambda e, j=j: e.tensor_copy(out=QK[j][1][0:32, :], in_=krT[0:32, :]), reads=t_kr, writes=[t_QK[j][1]])
            sidx = 0
            for h in range(8):
                j = h % 2
                QT, KT = QK[j]
                tq, tk = t_QK[j]
                self.dma("sp", uqb[j], evb["ev_uq"][h].rearrange("p (k j) -> p k j", k=3), reads=rd_w, writes=[t_ub[j]])
                self.dma("sp", ukb[j], evb["ev_ukvk"][h].rearrange("p (k j) -> p k j", k=2), reads=rd_w, writes=[t_ub[j]])
                for tb in range(4):
                    cols = slice(tb * 512, (tb + 1) * 512)
                    for kc in range(3):
                        P.op("pe", lambda e, j=j, kc=kc, cols=cols: e.matmul(ps[:, 7, :], uqb[j][:, kc, :], cqT[:, kc, cols],
                                                                             start=(kc == 0), stop=(kc == 2)),
                             reads=[t_ub[j], t_cq[tb]], writes=[ps_t[7]])
                    P.op("act", lambda e, QT=QT, cols=cols: e.activation(out=QT[64:128, cols], in_=ps[64:128, 7, :], func=AF.Identity,
                                                                         scale=sc_mla),
                         reads=[ps_t[7]], writes=[tq])
                    rope_rot(7, QT, cols, sc_mla, [], [tq])
                    for kc in range(2):
                        P.op("pe", lambda e, j=j, kc=kc, cols=cols: e.matmul(ps[:, 6, :], ukb[j][:, kc, :], ckvT[:, kc, cols],
                                                                             start=(kc == 0), stop=(kc == 1)),
                             reads=[t_ub[j], t_ckv[tb]], writes=[ps_t[6]])
                    P.op("act", lambda e, KT=KT, cols=cols: e.copy(out=KT[64:128, cols], in_=ps[64:128, 6, :]),
                         reads=[ps_t[6]], writes=[tk])
                for Q in range(4):
                    specs = []
                    for kt in range(4 * Q + 4):
                        jj = kt - 4 * Q
                        c0 = max(0, jj) * 128
                        ex = []
                        if jj >= 0:
                            ex.append((ident[:], Md[:], c0, c0 + 128, [t_ident, t_cst]))
                        specs.append(dict(c0=c0, c1=512, lhsT=KT[:, kt * 128:(kt + 1) * 128], rhs=QT[:, Q * 512 + c0:(Q + 1) * 512],
                                          rd=[tq, tk], extra=ex, v=Vm[:, kt, h, :], vrd=[t_Vm[kt]]))
                    sidx = run_tiles(specs, 3, E, t_E, sidx)
                    rl = fw_[3]
                    act_recip(rl[64:128, :], ps[64:128, 3, :], [ps_t[3]], [t_fw[3]])
                    hp = (h % 2) * 64
                    P.op("dve", lambda e, rl=rl, hp=hp, h=h, Q=Q: e.tensor_tensor(
                        out=oM[hp:hp + 64, h // 2, Q * 512:(Q + 1) * 512], in0=ps[0:64, 3, :], in1=rl[64:128, :], op=ALU.mult),
                        reads=[ps_t[3], t_fw[3]], writes=[t_oM[h // 2][Q]])
            self.dbg_out("dbg_oM", oM, [128, 4, S], BF16, [t for r in t_oM for t in r])

            P.barrier()
            off[0] = base_off
            oN = self.view(alloc(4 * S), [128, 4, S], BF16)
            NQ = self.view(alloc(4 * S), [128, 4, S], BF16)
            t_NQ = [[T() for _ in range(4)] for _ in range(4)]
            fm = [self.view(alloc(S), [128, S], BF16) for _ in range(4)]
            t_fm = [[T() for _ in range(4)] for _ in range(4)]
            srcK, srcV, kselT, kwinT = fm
            ghl = self.view(alloc(2 * S), [128, 2, S], BF16)
            t_ghl = [T() for _ in range(4)]
            Vn = self.view(alloc(NT * 512), [128, NT, 4, 128], BF16)
            t_Vn = [T() for _ in range(NT)]
            wvn = self.view(alloc(2048), [128, 8, 256], BF16)
            t_wvn = T()
            maskT = self.view(alloc(S), [128, S], BF16)
            t_mask = [T() for _ in range(NT)]
            P.op("pool", lambda e: e.memset(Vn.rearrange("p t h d -> p (t h) d")[:, :, 64:128], 1.0), writes=t_Vn)
            self.dma("sp", wvn, evb["ev_wvn"].rearrange("p (k n) -> p k n", k=8), reads=rd_w, writes=[t_wvn])
            for t in range(NT):
                for kc in range(8):
                    P.op("pe", lambda e, t=t, kc=kc: e.matmul(ps[:, 6, 0:256], hT[:, kc, t * 128:(t + 1) * 128], wvn[:, kc, :],
                                                              start=(kc == 0), stop=(kc == 7)),
                         reads=[t_hT[t], t_wvn], writes=[ps_t[6]])
                P.op("act", lambda e, t=t: e.copy(out=Vn[:, t, :, 0:64], in_=ps[:, 6, 0:256].rearrange("p (h d) -> p h d", h=4)),
                     reads=[ps_t[6]], writes=[t_Vn[t]])
            for ch in range(6, 15):
                self.dma("sp", wc[:, 0], evb["ev_wc"][ch].rearrange("p (k j) -> p k j", k=8), reads=rd_w, writes=[t_wcb[0]])
                for tb in range(4):
                    cols = slice(tb * 512, (tb + 1) * 512)
                    bank = 6 + (tb % 2)
                    for kc in range(8):
                        P.op("pe", lambda e, kc=kc, cols=cols, bank=bank: e.matmul(ps[:, bank, :], wc[:, 0, kc, :], hT[:, kc, cols],
                                                                                   start=(kc == 0), stop=(kc == 7)),
                             reads=[t_wcb[0]] + t_hT[tb * 4:tb * 4 + 4], writes=[ps_t[bank]])
                    if ch <= 9:
                        P.op("act", lambda e, ch=ch, cols=cols, bank=bank: e.activation(out=NQ[:, ch - 6, cols], in_=ps[:, bank, :],
                                                                                       func=AF.Identity, scale=0.125),
                             reads=[ps_t[bank]], writes=[t_NQ[ch - 6][tb]])
                    elif ch <= 13:
                        P.op("act", lambda e, ch=ch, cols=cols, bank=bank: e.copy(out=fm[ch - 10][:, cols], in_=ps[:, bank, :]),
                             reads=[ps_t[bank]], writes=[t_fm[ch - 10][tb]])
                    else:
                        gt = fw_[0]
                        P.op("act", lambda e, gt=gt, bank=bank: e.activation(out=gt[0:24, :], in_=ps[0:24, bank, :], func=AF.Sigmoid),
                             reads=[ps_t[bank]], writes=[t_fw[0]])
                        P.op("dve", lambda e, gt=gt, cols=cols: e.tensor_copy(out=ghl[0:24, 0, cols], in_=gt[0:24, :]),
                             reads=[t_fw[0]], writes=[t_ghl[tb]])
                        P.op("dve", lambda e, gt=gt, cols=cols: e.tensor_tensor(out=ghl[0:24, 1, cols], in0=gt[0:24, :], in1=ghl[0:24, 0, cols],
                                                                                op=ALU.subtract),
                             reads=[t_fw[0], t_ghl[tb]], writes=[t_ghl[tb]])

            P.barrier()
            hx = hT.rearrange("p k s -> p (k s)")
            hoff = [0]

            def hview(n, shape, dt):
                o = hoff[0]
                hoff[0] = o + n + (n % 2)
                assert hoff[0] <= 8 * S
                if dt == BF16:
                    ap = hx[:, o:o + n]
                else:
                    ap = hx[:, o:o + n].bitcast(F32)
                if len(shape) == 3:
                    ap = ap.rearrange("p (a b) -> p a b", a=shape[1])
                elif len(shape) == 4:
                    ap = ap.rearrange("p (a b c) -> p a b c", a=shape[1], b=shape[2])
                return ap
            t_h = lambda: T()
            w1sb = hview(32 * 256, [128, 32, 256], BF16); t_w1 = T()
            after_w1 = hoff[0]
            hoff[0] = 0
            efp = hview(2 * 8 * 128, [128, 8, 128], F32); t_e = T()
            pbf = hview(8 * 128, [128, 8, 128], BF16); t_p = T()
            pT = hview(4 * 8 * 128, [128, 4, 8, 128], BF16); t_pT = [T() for _ in range(4)]
            assert hoff[0] <= after_w1
            hoff[0] = after_w1
            peT = hview(32, [128, 32], BF16); t_pe = T()
            w2k = hview(512, [128, 2, 2, 128], BF16); t_w2 = T()
            w2v = hview(128, [128, 2, 64], BF16)
            hid = hview(256, [128, 2, 128], BF16); t_hid = [T(), T()]
            xa = hview(256, [128, 128], F32); t_xa = T()
            xb = hview(256, [128, 128], F32); t_xb = T()
            kcT = hview(128, [128, 128], BF16); t_kc = T()
            vc = hview(128, [128, 2, 64], BF16); t_vc = T()
            pebias = hview(4, [128, 2], F32); t_peb = T()
            ssum = hview(32, [128, 16], F32); t_ss = T()
            scr = hview(2 * 64, [128, 64], F32); t_scr = T()
            m8 = hview(2 * 16, [128, 16], F32); t_m8 = T()
            mbb = hview(64, [128, 64], BF16); t_mb = T()
            EXP = hview(16 * 128, [128, 16, 128], BF16)
            selb = hview(24 * 128, [128, 24, 128], BF16)
            scab = hview(2 * NT * 32, [128, 2, NT, 32], BF16)
            t_c3 = T()
            self.dma("sp", EXP[0:64], cst_b["expmat"].rearrange("p (a b) -> p a b", a=16), reads=[t_cstb], writes=[t_c3])
            self.dma("sp", selb[0:24], cst_b["selb"].rearrange("p (a b) -> p a b", a=24), reads=[t_cstb], writes=[t_c3])
            for a_ in range(2):
                self.dma("sp", scab[:, a_], cst_b["scab"][a_].rearrange("p (a b) -> p a b", a=NT), reads=[t_cstb], writes=[t_c3])

            self.dma("sp", w2k, evb["ev_w2k"].rearrange("p (g c j) -> p g c j", g=2, c=2), reads=rd_w, writes=[t_w2])
            self.dma("sp", w2v, evb["ev_w2v"].rearrange("p (c j) -> p c j", c=2), reads=rd_w, writes=[t_w2])
            for kv in range(2):
                self.dma("sp", w1sb, evb["ev_w1"][kv].rearrange("p (l c) -> p l c", l=32), reads=rd_w, writes=[t_w1])
                self.dma("sp", peT[0:64, :], evb["ev_peT"][kv], reads=rd_w, writes=[t_pe])
                for cc in range(2):
                    for l_ in range(32):
                        P.op("pe", lambda e, cc=cc, l_=l_: e.matmul(ps[:, 7, 0:1], w1sb[0:64, l_, cc * 128:(cc + 1) * 128], peT[0:64, l_:l_ + 1],
                                                                    start=(l_ == 0), stop=(l_ == 31)),
                             reads=[t_w1, t_pe], writes=[ps_t[7]])
                    P.op("dve", lambda e, cc=cc: e.tensor_copy(out=pebias[:, cc:cc + 1], in_=ps[:, 7, 0:1]), reads=[ps_t[7]], writes=[t_peb])
                src = srcK if kv == 0 else srcV
                tsrc = t_fm[kv]
                for g in range(2):
                    gp = slice(g * 64, (g + 1) * 64)
                    for cc in range(2):
                        for l_ in range(32):
                            P.op("pe", lambda e, gp=gp, cc=cc, l_=l_, src=src: e.matmul(
                                ps[:, 6, 0:127], w1sb[gp, l_, cc * 128:(cc + 1) * 128], src[gp, l_:l_ + 2017:16],
                                start=(l_ == 0), stop=(l_ == 31)), reads=[t_w1] + tsrc, writes=[ps_t[6]])
                        P.op("act", lambda e, cc=cc: e.activation(out=xa[:, 0:127], in_=ps[:, 6, 0:127], func=AF.Identity,
                                                                  bias=pebias[:, cc:cc + 1], scale=1.0),
                             reads=[ps_t[6], t_peb], writes=[t_xa])
                        P.op("act", lambda e: e.activation(out=xb[:, 0:127], in_=xa[:, 0:127], func=AF.Square), reads=[t_xa], writes=[t_xb])
                        P.op("dve", lambda e: e.tensor_scalar(out=xb[:, 0:127], in0=xb[:, 0:127], scalar1=0.044715, scalar2=1.0,
                                                              op0=ALU.mult, op1=ALU.add), reads=[t_xb], writes=[t_xb])
                        P.op("dve", lambda e: e.tensor_tensor(out=xb[:, 0:127], in0=xb[:, 0:127], in1=xa[:, 0:127], op=ALU.mult),
                             reads=[t_xb, t_xa], writes=[t_xb])
                        P.op("act", lambda e: e.activation(out=xb[:, 0:127], in_=xb[:, 0:127], func=AF.Sigmoid, scale=1.5957691216057308),
                             reads=[t_xb], writes=[t_xb])
                        P.op("dve", lambda e, cc=cc: e.tensor_tensor(out=hid[:, cc, 0:127], in0=xa[:, 0:127], in1=xb[:, 0:127], op=ALU.mult),
                             reads=[t_xa, t_xb], writes=[t_hid[cc]])
                    if kv == 0:
                        for cc in range(2):
                            P.op("pe", lambda e, g=g, cc=cc: e.matmul(ps[:, 7, 0:127], w2k[:, g, cc, :], hid[:, cc, 0:127],
                                                                      start=(cc == 0), stop=(cc == 1)),
                                 reads=[t_w2] + t_hid, writes=[ps_t[7]])
                        P.op("dve", lambda e, gp=gp: e.tensor_copy(out=kcT[gp, 0:127], in_=ps[gp, 7, 0:127]), reads=[ps_t[7]], writes=[t_kc])
                    else:
                        for cc in range(2):
                            P.op("pe", lambda e, cc=cc: e.matmul(ps[0:127, 7, 0:64], hid[:, cc, 0:127], w2v[:, cc, :],
                                                                 start=(cc == 0), stop=(cc == 1)),
                                 reads=[t_w2] + t_hid, writes=[ps_t[7]])
                        P.op("dve", lambda e, g=g: e.tensor_copy(out=vc[0:127, g, :], in_=ps[0:127, 7, 0:64]), reads=[ps_t[7]], writes=[t_vc])
            P.barrier()
            self.dbg_out("dbg_kcT", kcT, [128, 128], BF16, [t_kc])
            self.dbg_out("dbg_vc", vc, [128, 2, 64], BF16, [t_vc])

            for Q in range(4):
                for tt in range(4):
                    qt = 4 * Q + tt
                    qc = slice(qt * 128, (qt + 1) * 128)
                    for h in range(8):
                        g, jn = h // 4, h % 4
                        gp = slice(g * 64, (g + 1) * 64)
                        bank = h // 4
                        cs = (h % 4) * 127
                        P.op("pe", lambda e, gp=gp, jn=jn, qc=qc, bank=bank, cs=cs: e.matmul(
                            ps[:, bank, cs:cs + 127], NQ[gp, jn, qc], kcT[gp, 0:127], start=True, stop=False),
                            reads=[t_NQ[jn][Q], t_kc], writes=[ps_t[bank]])
                        P.op("pe", lambda e, h=h, qt=qt, bank=bank, cs=cs: e.matmul(
                            ps[:, bank, cs:cs + 127], ident[:], Bc[:, h, 120 - 8 * qt:247 - 8 * qt], start=False, stop=True),
                            reads=[t_ident, t_B], writes=[ps_t[bank]])
                    for h in range(8):
                        bank = h // 4
                        cs = (h % 4) * 127
                        P.op("act", lambda e, h=h, bank=bank, cs=cs: e.activation(out=efp[:, h, 0:127], in_=ps[:, bank, cs:cs + 127], func=AF.Exp,
                                                                                   accum_out=ssum[:, h:h + 1]),
                             reads=[ps_t[bank]], writes=[t_e, t_ss])
                    P.op("dve", lambda e: e.tensor_scalar_add(out=ssum[:, 8:16], in0=ssum[:, 0:8], scalar1=1e-30), reads=[t_ss], writes=[t_ss])
                    P.op("dve", lambda e: e.reciprocal(out=ssum[:, 8:16], in_=ssum[:, 8:16]), reads=[t_ss], writes=[t_ss])
                    for h in range(8):
                        P.op("dve" if h % 2 == 0 else "pool", lambda e, h=h: e.tensor_scalar_mul(out=pbf[:, h, 0:127], in0=efp[:, h, 0:127],
                                                                                               scalar1=ssum[:, 8 + h:9 + h]),
                             reads=[t_e, t_ss], writes=[t_p])
                    psT = ps[:, 5, :].bitcast(BF16)
                    for h in range(8):
                        P.op("pe", lambda e, h=h, psT=psT: e.transpose(psT[0:127, h * 128:(h + 1) * 128], pbf[:, h, 0:127], ident[:]),
                             reads=[t_p, t_ident], writes=[ps_t[5]])
                    P.op("act", lambda e, tt=tt, psT=psT: e.copy(out=pT[0:127, tt, :, :], in_=psT[0:127, :].rearrange("p (h q) -> p h q", h=8)),
                         reads=[ps_t[5]], writes=[t_pT[tt]])
                    for h in range(8):
                        g = h // 4
                        P.op("pe", lambda e, h=h, g=g, tt=tt: e.matmul(ps[:, 7, g * 32:(g + 1) * 32], pT[0:127, tt, h, :], ovl[:, :],
                                                                       start=(h % 4 == 0), stop=(h % 4 == 3)),
                             reads=[t_pT[tt], t_cst], writes=[ps_t[7]])
                    for g in range(2):
                        P.op("dve", lambda e, g=g, qt=qt: e.tensor_tensor(out=scr[:, g * 32:(g + 1) * 32], in0=ps[:, 7, g * 32:(g + 1) * 32],
                                                                          in1=scab[:, 0, qt, :], op=ALU.mult),
                             reads=[ps_t[7], t_c3], writes=[t_scr])
                        P.op("dve", lambda e, g=g, qt=qt: e.tensor_tensor(out=scr[:, g * 32:(g + 1) * 32], in0=scr[:, g * 32:(g + 1) * 32],
                                                                          in1=scab[:, 1, qt, :], op=ALU.add),
                             reads=[t_scr, t_c3], writes=[t_scr])
                        P.op("dve", lambda e, g=g: e.max(out=m8[:, g * 8:(g + 1) * 8], in_=scr[:, g * 32:(g + 1) * 32]),
                             reads=[t_scr], writes=[t_m8])
                        P.op("dve", lambda e, g=g: e.tensor_scalar(out=scr[:, g * 32:(g + 1) * 32], in0=scr[:, g * 32:(g + 1) * 32],
                                                                   scalar1=m8[:, g * 8 + 7:g * 8 + 8], scalar2=1.0,
                                                                   op0=ALU.is_ge, op1=ALU.subtract),
                             reads=[t_scr, t_m8], writes=[t_scr])
                    P.op("dve", lambda e: e.tensor_scalar_mul(out=mbb[:, :], in0=scr[:, :], scalar1=-NEG), reads=[t_scr], writes=[t_mb])
                    psM = ps[:, 7, 256:512].bitcast(BF16)
                    P.op("pe", lambda e, psM=psM: e.transpose(psM[0:64, 0:128], mbb[:, :], ident[:]), reads=[t_mb, t_ident], writes=[ps_t[7]])
                    P.op("act", lambda e, qc=qc, psM=psM: e.copy(out=maskT[0:64, qc], in_=psM[0:64, 0:128]), reads=[ps_t[7]], writes=[t_mask[qt]])
                for h in range(8):
                    g, jn = h // 4, h % 4
                    gp = slice(g * 64, (g + 1) * 64)
                    mp = slice(g * 32, (g + 1) * 32)
                    qrd = [t_NQ[jn][Q]]
                    specs = []
                    for kt in range(4 * Q + 4):
                        jj = kt - 4 * Q
                        c0 = max(0, jj) * 128
                        ex = [(EXP[mp, kt, :], maskT[mp, Q * 512 + c0:(Q + 1) * 512], c0, 512, [t_c3] + t_mask[4 * Q:4 * Q + 4])]
                        if jj == -1:
                            ex.append((ident[:], Bsb[:, h, 1, :], 0, 128, [t_ident, t_B]))
                        elif jj == 3:
                            ex.append((ident[:], Bsb[:, h, 0, :], 384, 512, [t_ident, t_B]))
                        elif jj >= 0:
                            ex.append((ident[:], Bsb[:, h, 0:2, :].rearrange("p t q -> p (t q)"), c0, c0 + 256, [t_ident, t_B]))
                        specs.append(dict(c0=c0, c1=512, lhsT=kselT[gp, kt * 128:(kt + 1) * 128], rhs=NQ[gp, jn, Q * 512 + c0:(Q + 1) * 512],
                                          rd=qrd + t_fm[2], extra=ex, v=Vn[:, kt, 0 + g, :], vrd=[t_Vn[kt]]))
                    sidx = run_tiles(specs, 3, E, t_E, sidx)
                    specs = []
                    for kt in range(max(0, 4 * Q - 2), 4 * Q + 4):
                        jj = kt - 4 * Q
                        lo = max(0, jj)
                        hi = min(3, jj + 2)
                        c0, c1 = lo * 128, (hi + 1) * 128
                        t0, t1 = lo - jj, hi - jj
                        ex = [(ident[:], Bsb[:, h, t0:t1 + 1, :].rearrange("p t q -> p (t q)"), c0, c1, [t_ident, t_B])]
                        specs.append(dict(c0=c0, c1=c1, lhsT=kwinT[gp, kt * 128:(kt + 1) * 128], rhs=NQ[gp, jn, Q * 512 + c0:Q * 512 + c1],
                                          rd=qrd + t_fm[3], extra=ex, v=Vn[:, kt, 2 + g, :], vrd=[t_Vn[kt]]))
                    sidx = run_tiles(specs, 4, E, t_E, sidx)
                    for tt in range(4):
                        P.op("pe", lambda e, g=g, tt=tt, h=h: e.matmul(ps[0:64, 6, tt * 128:(tt + 1) * 128], vc[0:127, g, :], pT[0:127, tt, h, :],
                                                                       start=True, stop=True),
                             reads=[t_vc, t_pT[tt]], writes=[ps_t[6]])
                    acc = fw_[0]
                    hp = (h % 2) * 64
                    brs = [(0, 6, False), (1, 3, True), (2, 4, True)]
                    if self.nsa_only is not None:
                        brs = [brs[self.nsa_only]]
                    for bi_, (br, src_bank, norm) in enumerate(brs):
                        r = 3 * h + br
                        first_, last_ = (bi_ == 0), (bi_ == len(brs) - 1)
                        for part in range(2):
                            P.op("pe", lambda e, r=r, part=part, Q=Q: e.matmul(ps[:, 5, :], selb[0:24, r, :], ghl[0:24, part, Q * 512:(Q + 1) * 512],
                                                                          start=(part == 0), stop=(part == 1)),
                                 reads=[t_c3, t_ghl[Q]], writes=[ps_t[5]])
                        f = fw_[1]
                        if norm:
                            act_recip(f[64:128, :], ps[64:128, src_bank, :], [ps_t[src_bank]], [t_fw[1]])
                            P.op("dve", lambda e, f=f: e.tensor_tensor(out=f[64:128, :], in0=ps[64:128, 5, :], in1=f[64:128, :], op=ALU.mult),
                                 reads=[ps_t[5], t_fw[1]], writes=[t_fw[1]])
                        else:
                            P.op("dve", lambda e, f=f: e.tensor_copy(out=f[64:128, :], in_=ps[64:128, 5, :]), reads=[ps_t[5]], writes=[t_fw[1]])
                        dst = oN[hp:hp + 64, h // 2, Q * 512:(Q + 1) * 512] if last_ else acc[0:64, :]
                        tdst = t_oN[h // 2][Q] if last_ else t_fw[0]
                        if first_:
                            P.op("dve", lambda e, f=f, src_bank=src_bank, dst=dst: e.tensor_tensor(out=dst, in0=ps[0:64, src_bank, :], in1=f[64:128, :],
                                                                                                  op=ALU.mult),
                                 reads=[ps_t[src_bank], t_fw[1]], writes=[tdst])
                        else:
                            tmp_ = fw_[2]
                            P.op("dve", lambda e, f=f, src_bank=src_bank, tmp_=tmp_: e.tensor_tensor(out=tmp_[0:64, :], in0=ps[0:64, src_bank, :],
                                                                                                    in1=f[64:128, :], op=ALU.mult),
                                 reads=[ps_t[src_bank], t_fw[1]], writes=[t_fw[2]])
                            P.op("pool", lambda e, tmp_=tmp_, dst=dst: e.tensor_tensor(out=dst, in0=acc[0:64, :], in1=tmp_[0:64, :], op=ALU.add),
                                 reads=[t_fw[0], t_fw[2]], writes=[tdst])
            self.dbg_out("dbg_oN", oN, [128, 4, S], BF16, [t for r in t_oN for t in r])
            self.dbg_out("dbg_maskT", maskT, [128, S], BF16, t_mask)

            P.barrier()
            off[0] = base_off + 4 * S
            wo = self.view(alloc(8 * D), [128, 8, D], BF16)
            t_wo = T()
            self.dma("sp", wo, evb["ev_wo"].rearrange("p (k n) -> p k n", k=8), reads=rd_w, writes=[t_wo])
            def mm_wo(t):
                yb = 3 if t % 2 == 0 else 5
                for dh in range(2):
                    for c in range(8):
                        src_ = oM if c < 4 else oN
                        tsrc_ = (t_oM if c < 4 else t_oN)[c % 4][t // 4]
                        P.op("pe", lambda e, t=t, dh=dh, c=c, src_=src_, yb=yb: e.matmul(
                            ps[:, yb + dh, :], src_[:, c % 4, t * 128:(t + 1) * 128], wo[:, c, dh * 512:(dh + 1) * 512],
                            start=(c == 0), stop=(c == 7)), reads=[tsrc_, t_wo], writes=[ps_t[yb + dh]])
            ep_pipeline(list(range(NT)), mm_wo, lambda t: 3 if t % 2 == 0 else 5, b, l, 0, xsrc, 7)

        for b in range(nseq):
            self.have_hT = False
            for pi, (l, sub) in enumerate(plan):
                xsrc = x_in if pi == 0 else out
                self.nxt = plan[pi + 1] if (pi + 1 < len(plan) and self.fuse) else None
                if sub == 1:
                    ffn(b, l, xsrc)
                elif l % 2 == 1:
                    diff_attn(b, l, xsrc)
                else:
                    even_attn(b, l, xsrc)
                self.have_hT = self.nxt is not None
        P.emit()


def host_prep(inputs, core, nseq, seq0=None):
    b0 = core * nseq if seq0 is None else seq0
    m = {}
    m["x"] = np.ascontiguousarray(inputs["x"][b0:b0 + nseq])
    m["cT"] = np.ascontiguousarray(inputs["c"][b0:b0 + nseq].reshape(nseq, 8, 128).transpose(2, 1, 0))
    for k in ["ada_w", "ada_b", "ln_g", "ln_b", "rel_bias"]:
        m[k] = np.ascontiguousarray(inputs[k])
    up = inputs["ffn_w_up"].reshape(DEPTH, 8, 128, NFC, 128)
    m["wu"] = np.ascontiguousarray(up.transpose(0, 3, 2, 1, 4))
    gt = inputs["ffn_w_gate"].reshape(DEPTH, 8, 128, NFC, 128)
    m["wg"] = np.ascontiguousarray(gt.transpose(0, 3, 2, 1, 4))
    dn = inputs["ffn_w_down"].reshape(DEPTH, NFC, 128, D)
    m["wd"] = np.ascontiguousarray(dn.transpose(0, 2, 1, 3))
    cw = np.concatenate([inputs["ffn_conv_w"].transpose(0, 2, 1), inputs["ffn_conv_b"][:, :, None]], axis=2)
    m["convp"] = np.ascontiguousarray(cw.astype(np.float32))
    w = inputs["od_w_in"]
    qk = w[:, :, :2048].reshape(2, 8, 128, 16, 128)
    m["od_wqk"] = np.ascontiguousarray(qk.transpose(0, 3, 2, 1, 4))
    v = w[:, :, 2048:].reshape(2, 8, 128, D)
    m["od_wv"] = np.ascontiguousarray(v.transpose(0, 2, 1, 3))
    wo = inputs["od_w_o"].reshape(2, 8, 128, D)
    m["od_wo"] = np.ascontiguousarray(wo.transpose(0, 2, 1, 3))
    m["diff_lambda"] = np.ascontiguousarray(inputs["diff_lambda"].reshape(2, 256))
    m["diff_subln"] = np.ascontiguousarray(inputs["diff_subln"].reshape(2, 128, 1))
    ew = inputs["ev_w_in"]
    def fm_chunk(W, cols, nk):
        Z = np.zeros((W.shape[0], W.shape[1], 128), np.float32)
        Z[:, :, :len(cols)] = W[:, :, cols]
        return Z.reshape(W.shape[0], nk, 128, 128).transpose(0, 2, 1, 3).reshape(W.shape[0], 128, nk * 128)
    sw = [(r + 16) % 32 for r in range(32)]
    lists = [list(range(c * 128, (c + 1) * 128)) for c in range(3)]
    lists += [list(range(384 + c * 128, 384 + (c + 1) * 128)) for c in range(2)]
    lists += [[640 + r for r in range(32)] + [640 + r for r in sw]]
    for c in range(4):
        lists += [[672 + c * 64 + d for d in range(64)] + [672 + (c + 4) * 64 + d for d in range(64)]]
    lists += [[1184 + d for d in range(0, 128)], [1184 + d for d in range(128, 256)],
              [1184 + d for d in range(256, 384)], [1184 + d for d in range(512, 640)]]
    lists += [list(range(1952, 1976))]
    m["ev_wc"] = np.ascontiguousarray(np.stack([fm_chunk(ew, L, 8) for L in lists], axis=1))
    vcols = [1184 + d for d in range(384, 512)] + [1184 + d for d in range(640, 768)]
    m["ev_wvn"] = np.ascontiguousarray(ew[:, :, vcols].reshape(2, 8, 128, 256).transpose(0, 2, 1, 3).reshape(2, 128, 8 * 256))
    uq = inputs["mla_w_uq"]
    m["ev_uq"] = np.ascontiguousarray(np.stack([fm_chunk(uq, [h * 96 + 64 + r for r in range(32)] + [h * 96 + 64 + r for r in sw]
                                                          + [h * 96 + d for d in range(64)], 3) for h in range(8)], axis=1))
    ukv = inputs["mla_w_ukv"]
    def kchunk(h):
        Z = np.zeros((2, 256, 128), np.float32)
        Z[:, :, 64:] = ukv[:, :, h * 128:h * 128 + 64]
        return Z.reshape(2, 2, 128, 128).transpose(0, 2, 1, 3).reshape(2, 128, 256)
    m["ev_ukvk"] = np.ascontiguousarray(np.stack([kchunk(h) for h in range(8)], axis=1))
    vc_ = [h * 128 + 64 + d for h in range(8) for d in range(64)]
    m["ev_ukvv"] = np.ascontiguousarray(ukv[:, :, vc_].reshape(2, 2, 128, 512).transpose(0, 2, 1, 3).reshape(2, 128, 1024))
    m["ev_qn"] = np.ascontiguousarray(inputs["mla_q_norm"].reshape(2, 3, 128).transpose(0, 2, 1))
    m["ev_kvn"] = np.ascontiguousarray(inputs["mla_kv_norm"].reshape(2, 2, 128).transpose(0, 2, 1))
    w1 = inputs["nsa_cmp_w1"].reshape(2, 2, 32, 64, 256).transpose(0, 1, 3, 2, 4)
    m["ev_w1"] = np.ascontiguousarray(np.concatenate([w1, w1], axis=2).reshape(2, 2, 128, 32 * 256))
    w2 = inputs["nsa_cmp_w2"]
    w2k = np.zeros((2, 128, 2, 2, 128), np.float32)
    for g in range(2):
        w2k[:, :, g, :, g * 64:(g + 1) * 64] = w2[:, 0].reshape(2, 2, 128, 64).transpose(0, 2, 1, 3)
    m["ev_w2k"] = np.ascontiguousarray(w2k.reshape(2, 128, 512))
    m["ev_w2v"] = np.ascontiguousarray(w2[:, 1].reshape(2, 2, 128, 64).transpose(0, 2, 1, 3).reshape(2, 128, 128))
    m["ev_peT"] = np.ascontiguousarray(inputs["nsa_cmp_pe"].transpose(0, 1, 3, 2))
    m["ev_wo"] = np.ascontiguousarray(inputs["ev_w_o"].reshape(2, 8, 128, D).transpose(0, 2, 1, 3).reshape(2, 128, 8 * D))
    m.update(static_consts())
    return m


_SC = {}


def static_consts():
    if _SC:
        return _SC
    inv = (1.0 / (np.float32(10000.0) ** (np.arange(0, 32, 2, dtype=np.float32) / np.float32(32)))).astype(np.float32)
    ang = (np.arange(S, dtype=np.float32)[:, None] * inv[None, :]).astype(np.float32)
    cos, sin = np.cos(ang).astype(np.float32), np.sin(ang).astype(np.float32)
    rt = np.zeros((64, S), np.float32)
    for r in range(32):
        rt[r] = cos[:, r % 16]
        rt[32 + r] = -sin[:, r] if r < 16 else sin[:, r - 16]
    _SC["ropeT"] = rt
    kk, qq = np.meshgrid(np.arange(128), np.arange(128), indexing="ij")
    _SC["cmask"] = np.stack([np.where(qq >= kk, 0.0, NEG), np.where(qq < kk, 0.0, NEG)]).astype(np.float32)
    ex = np.zeros((2, 32, 16, 128), np.float32)
    for kt in range(16):
        ex[:, 2 * kt, kt, 0:64] = 1.0
        ex[:, 2 * kt + 1, kt, 64:128] = 1.0
    _SC["expmat"] = ex.reshape(64, 16 * 128)
    starts = np.arange(127) * 16
    jb = np.arange(32)
    _SC["ovl"] = ((starts[:, None] < (jb[None, :] + 1) * 64) & (starts[:, None] + 32 > jb[None, :] * 64)).astype(np.float32)
    t = np.arange(S)
    cur = t // 64
    forced = (jb[None, :] == 0) | (jb[None, :] == cur[:, None]) | (jb[None, :] == cur[:, None] - 1)
    causal = jb[None, :] * 64 <= t[:, None]
    A = (causal & ~forced).astype(np.float32)
    Bf = np.where(~causal, -1.0, np.where(forced, 1e4, 0.0)).astype(np.float32)
    sc = np.stack([A, Bf]).reshape(2, NT, 128, 32).transpose(0, 2, 1, 3).reshape(2, 128, NT * 32)
    _SC["scab"] = np.ascontiguousarray(sc)
    sb = np.zeros((24, 24, 128), np.float32)
    for r in range(24):
        sb[r, r, :] = 1.0
    _SC["selb"] = sb.reshape(24, 24 * 128)
    _SC["onehot"] = onehot_consts()
    _SC["ident"] = np.eye(128, dtype=np.float32)
    return _SC


FULL_PLAN = [(l, s) for l in range(DEPTH) for s in range(2)]
NCORES = 8
_CACHE = {}


def kernel(**inputs):
    inputs = {k: np.asarray(v) for k, v in inputs.items()}
    B = inputs["x"].shape[0]
    nseq = B // NCORES
    kb = K(nseq, FULL_PLAN)
    in_maps = []
    shared = host_prep(inputs, 0, nseq)
    shared = {k: v for k, v in shared.items() if k in kb.dram_in}
    for core in range(NCORES):
        m = dict(shared)
        b0 = core * nseq
        m["x"] = np.ascontiguousarray(inputs["x"][b0:b0 + nseq])
        m["cT"] = np.ascontiguousarray(inputs["c"][b0:b0 + nseq].reshape(nseq, 8, 128).transpose(2, 1, 0))
        in_maps.append(m)
    res = run_bass_kernel_spmd(kb.nc, in_maps, core_ids=list(range(NCORES)))
    outs = [np.asarray(r["out"]) for r in res.results]
    return np.concatenate(outs, axis=0).astype(np.float32)
```

```python
import numpy as np
import math
import contextlib
from concourse.bass_utils import run_bass_kernel_spmd
import concourse.bass as bass
import concourse.mybir as mybir

F32 = mybir.dt.float32
BF16 = mybir.dt.bfloat16
ALU = mybir.AluOpType
AF = mybir.ActivationFunctionType
AX = mybir.AxisListType

SEM_LIM = 30000
NSLOT = 12


class Tok:
    __slots__ = ("w", "r", "const")

    def __init__(self, const=False):
        self.w = None
        self.r = []
        self.const = const


class Ins:
    __slots__ = ("eng", "fn", "deps", "dma", "pos", "sig", "slot", "slotcnt", "signo")


class Prog:
    ENGS = ["pe", "act", "dve", "pool", "sp"]

    def __init__(self, nc):
        self.nc = nc
        self.instrs = []
        self.ndma = {e: 0 for e in self.ENGS}
        self.slot_last = {}
        self.stack = contextlib.ExitStack()
        self.last = {}
        self.pending_bar = {}
        self.bar_skip = ()

    def sbuf(self, name, shape, dt):
        return self.stack.enter_context(self.nc.sbuf_tensor(name, list(shape), dt))

    def psum(self, name, shape, dt):
        return self.stack.enter_context(self.nc.psum_tensor(name, list(shape), dt))

    def op(self, eng, fn, reads=(), writes=(), dma=False):
        ins = Ins()
        ins.eng = eng
        ins.fn = fn
        ins.dma = dma
        ins.sig = False
        ins.signo = 0
        deps = set()
        for t in reads:
            if t.w is not None:
                deps.add(t.w)
        for t in writes:
            if t.w is not None:
                deps.add(t.w)
            deps.update(t.r)
        for t in reads:
            if not t.const:
                t.r.append(ins)
        for t in writes:
            t.w = ins
            t.r = []
        if dma:
            n = self.ndma[eng]
            self.ndma[eng] = n + 1
            ins.slot = n % NSLOT
            ins.slotcnt = n // NSLOT + 1
            prev = self.slot_last.get((eng, ins.slot))
            if prev is not None:
                deps.add(prev)
            self.slot_last[(eng, ins.slot)] = ins
        pb = self.pending_bar.pop(eng, None)
        if pb:
            deps.update(pb)
        deps.discard(ins)
        ins.deps = deps
        self.instrs.append(ins)
        self.last[eng] = ins
        return ins

    def barrier(self):
        bar = list(self.last.values()) + [v for (q_, s_), v in self.slot_last.items() if q_ not in self.bar_skip]
        for e in self.ENGS:
            self.pending_bar[e] = list(bar)

    def emit(self, final_waits=()):
        nc = self.nc
        per = {e: [] for e in self.ENGS}
        for ins in self.instrs:
            ins.pos = len(per[ins.eng])
            per[ins.eng].append(ins)

        def needs_wait(ins, d):
            if d.dma:
                return True
            if d.eng == ins.eng:
                if d.eng == "pe":
                    return False
                if ins.dma:
                    return True
                return (ins.pos - d.pos) <= 3
            return True

        for ins in self.instrs:
            best = {}
            for d in ins.deps:
                if not d.dma and needs_wait(ins, d):
                    if d.eng not in best or best[d.eng].pos < d.pos:
                        best[d.eng] = d
            nd = set(d for d in ins.deps if d.dma)
            for d in best.values():
                d.sig = True
                nd.add(d)
            ins.deps = nd
        for e in self.ENGS:
            n = 0
            for ins in per[e]:
                if ins.sig and not ins.dma:
                    n += 1
                    ins.signo = n
        nsig = {e: max([i.signo for i in per[e]] + [0]) for e in self.ENGS}
        esems = {}
        for e in self.ENGS:
            nep = (nsig[e] + SEM_LIM - 1) // SEM_LIM
            esems[e] = [self.stack.enter_context(nc.semaphore(f"s_{e}_{k}")) for k in range(max(nep, 1))]
        ssems = {}
        for e in self.ENGS:
            if self.ndma[e]:
                ssems[e] = [self.stack.enter_context(nc.semaphore(f"d_{e}_{k}")) for k in range(NSLOT)]
                assert (self.ndma[e] // NSLOT + 1) * 16 < 65000, "too many dmas on queue"
        self.stats = {e: [len(per[e]), nsig[e], self.ndma[e]] for e in self.ENGS}
        nwaits = {e: 0 for e in self.ENGS}

        def run(ename, handle):
            waited = {}
            for ins in per[ename]:
                need = {}
                for d in ins.deps:
                    if not needs_wait(ins, d):
                        continue
                    if d.dma:
                        key = ("s", d.eng, d.slot)
                        v = d.slotcnt * 16
                    else:
                        key = ("e", d.eng)
                        v = d.signo
                    if waited.get(key, 0) >= v:
                        continue
                    if need.get(key, 0) < v:
                        need[key] = v
                for key, v in need.items():
                    waited[key] = v
                    nwaits[ename] += 1
                    if key[0] == "s":
                        handle.wait_ge(ssems[key[1]][key[2]], v)
                    else:
                        ep = (v - 1) // SEM_LIM
                        handle.wait_ge(esems[key[1]][ep], (v - 1) % SEM_LIM + 1)
                bi = ins.fn(handle)
                if ins.dma:
                    bi.then_inc(ssems[ename][ins.slot], 16)
                elif ins.sig:
                    ep = (ins.signo - 1) // SEM_LIM
                    bi.then_inc(esems[ename][ep], 1)
            if ename == "sp":
                for e in self.ENGS:
                    if self.ndma[e]:
                        for s in range(NSLOT):
                            last = self.slot_last.get((e, s))
                            if last is not None:
                                handle.wait_ge(ssems[e][s], last.slotcnt * 16)

        with nc.Block() as block:
            @block.tensor
            def _(e):
                run("pe", e)

            @block.scalar
            def _(e):
                run("act", e)

            @block.vector
            def _(e):
                run("dve", e)

            @block.gpsimd
            def _(e):
                run("pool", e)

            @block.sync
            def _(e):
                run("sp", e)
        self.stats["waits"] = nwaits
        self.stack.close()


S = 2048
D = 1024
DEPTH = 4
DFF = 2816
NFC = DFF // 128
NT = S // 128
ALPHA = (2.0 * DEPTH) ** 0.25
LN_EPS = 1e-5
RMS_EPS = 1e-6
NEG = -30000.0
N_ATT = 2 * 128 * 128
N_CMP = 128 * 247
N_OH = N_ATT + N_CMP
AR_EL = 60 * 1024


def T(const=False):
    return Tok(const)


def t5_bucket_np(dist):
    n = np.maximum(dist, 0)
    nf = np.maximum(n, 1).astype(np.float32)
    large = 16 + (np.log(nf / np.float32(16)) / np.float32(math.log(128 / 16)) * np.float32(16)).astype(np.int32)
    large = np.minimum(large, 31)
    return np.where(n < 16, n, large)


def onehot_consts():
    oh = np.zeros((33, N_OH), np.float32)
    tau, k, q = np.meshgrid(np.arange(2), np.arange(128), np.arange(128), indexing="ij")
    dist = (q - k + 128 * tau).reshape(-1)
    col = np.arange(N_ATT)
    b = t5_bucket_np(dist)
    pos = dist >= 0
    np.add.at(oh, (b[pos], col[pos]), 1.0)
    np.add.at(oh, (np.full(pos.sum(), 31), col[pos]), -1.0)
    oh[32, col[~pos]] = 1.0
    p, m = np.meshgrid(np.arange(128), np.arange(247), indexing="ij")
    dist = (p - 16 * (m - 120) - 31).reshape(-1)
    col = N_ATT + np.arange(N_CMP)
    b = t5_bucket_np(dist)
    pos = dist >= 0
    np.add.at(oh, (b[pos], col[pos]), 1.0)
    np.add.at(oh, (np.full(pos.sum(), 31), col[pos]), -1.0)
    oh[32, col[~pos]] = 1.0
    return oh


class K:
    def __init__(self, nseq, plan, dbg=False, nsa_only=None):
        self.nsa_only = nsa_only
        self.fuse = True
        self.have_hT = False
        self.nxt = None
        self.nseq = nseq
        self.plan = plan
        self.dbg = dbg
        nc = self.nc = bass.Bass("TRN2", target_bir_lowering=False)
        self.P = Prog(nc)
        self.P.bar_skip = ("pool",)
        self.dram_in = {}
        self.dbg_names = set()
        self.build()

    def din(self, name, shape, dt=F32):
        t = self.nc.dram_tensor(name, list(shape), dt, kind="ExternalInput")
        self.dram_in[name] = (tuple(shape), dt)
        return t.ap()

    def dscr(self, name, shape, dt):
        return self.nc.dram_tensor(name, list(shape), dt, kind="Internal").ap()

    def dbg_out(self, name, src_ap, shape, dt, reads):
        if not self.dbg or name in self.dbg_names:
            return
        self.dbg_names.add(name)
        t = self.nc.dram_tensor(name, list(shape), dt, kind="ExternalOutput").ap()
        self.dma("sp", t, src_ap, reads=reads, writes=[Tok()])

    def dma(self, q, out, in_, reads=(), writes=()):
        return self.P.op(q, lambda e, o=out, i=in_: e.dma_start(out=o, in_=i), reads, writes, dma=True)

    def view(self, off, shape, dt):
        n = int(np.prod(shape[1:]))
        if dt == BF16:
            ap = self.ar[:, off:off + n]
        else:
            assert off % 2 == 0
            ap = self.ar[:, off:off + 2 * n].bitcast(F32)
        if len(shape) == 3:
            ap = ap.rearrange("p (a b) -> p a b", a=shape[1])
        elif len(shape) == 4:
            ap = ap.rearrange("p (a b c) -> p a b c", a=shape[1], b=shape[2])
        return ap

    def build(self):
        nc, P = self.nc, self.P
        nseq = self.nseq
        plan = self.plan
        layers = sorted(set(l for (l, s) in plan))
        has = lambda l, s: (l, s) in plan
        x_in = self.din("x", [nseq, S, D])
        cT = self.din("cT", [128, 8, nseq])
        out = nc.dram_tensor("out", [nseq, S, D], F32, kind="ExternalOutput").ap()
        ident_d = self.din("ident", [128, 128])
        ada_w = self.din("ada_w", [DEPTH, D, 6 * D])
        ada_b = self.din("ada_b", [DEPTH, 6 * D])
        ln_g = self.din("ln_g", [DEPTH, 2, D])
        ln_b = self.din("ln_b", [DEPTH, 2, D])
        wu_f = self.din("wu", [DEPTH, NFC, 128, 8, 128])
        wg_f = self.din("wg", [DEPTH, NFC, 128, 8, 128])
        wd_f = self.din("wd", [DEPTH, 128, NFC, D])
        cw_d = self.din("convp", [DEPTH, DFF, 4])
        relb = self.din("rel_bias", [32, 8])
        oh_d = self.din("onehot", [33, N_OH])
        odqk_f = self.din("od_wqk", [2, 16, 128, 8, 128])
        odv_f = self.din("od_wv", [2, 128, 8, D])
        odo_f = self.din("od_wo", [2, 128, 8, D])
        dlam = self.din("diff_lambda", [2, 256])
        dsub = self.din("diff_subln", [2, 128, 1])
        ev_specs = {"ev_wc": [15, 128, 8 * 128], "ev_wvn": [128, 8 * 256], "ev_uq": [8, 128, 3 * 128], "ev_ukvk": [8, 128, 2 * 128],
                    "ev_ukvv": [128, 2 * 512], "ev_w1": [2, 128, 32 * 256], "ev_w2k": [128, 2 * 2 * 128], "ev_w2v": [128, 2 * 64],
                    "ev_peT": [2, 64, 32], "ev_wo": [128, 8 * D]}
        ev_f = {k: self.din(k, [2] + v) for k, v in ev_specs.items()}
        ev_b = {k: self.dscr(k + "_b", [2] + v, BF16) for k, v in ev_specs.items()}
        ev_qn = self.din("ev_qn", [2, 128, 3])
        ev_kvn = self.din("ev_kvn", [2, 128, 2])
        rope_d = self.din("ropeT", [64, S])
        cmask_d = self.din("cmask", [2, 128, 128])
        exp_d = self.din("expmat", [64, 16 * 128])
        ovl_d = self.din("ovl", [127, 32])
        scab_d = self.din("scab", [2, 128, NT * 32])
        selb_d = self.din("selb", [24, 24 * 128])
        cst_b = {"expmat": self.dscr("expmat_b", [64, 16 * 128], BF16), "selb": self.dscr("selb_b", [24, 24 * 128], BF16),
                 "scab": self.dscr("scab_b", [2, 128, NT * 32], BF16)}
        t_cstb = T()
        wu_b = self.dscr("wu_b", [DEPTH, NFC, 128, 8 * 128], BF16)
        wg_b = self.dscr("wg_b", [DEPTH, NFC, 128, 8 * 128], BF16)
        wd_b = self.dscr("wd_b", [DEPTH, 128, NFC * D], BF16)
        odqk_b = self.dscr("odqk_b", [2, 16, 128, 8 * 128], BF16)
        odv_b = self.dscr("odv_b", [2, 128, 8 * D], BF16)
        odo_b = self.dscr("odo_b", [2, 128, 8 * D], BF16)
        mod_d = self.dscr("mod", [DEPTH, nseq, 6 * D], F32)
        G_d = self.dscr("Gd", [8, N_OH], F32)
        self.out = out

        ps = P.psum("ps", [128, 8, 512], F32)
        ps_t = [T() for _ in range(8)]
        ident = P.sbuf("identb", [128, 128], BF16)
        ident_f = P.sbuf("identf", [128, 128], F32)
        ones_b = P.sbuf("onesb", [128, 128], BF16)
        ones_f = P.sbuf("onesf", [128, 128], F32)
        t_ident = T(const=True)
        t_ones = T(const=True)
        hT = P.sbuf("hT", [128, 8, S], BF16)
        t_hT = [T() for _ in range(NT)]
        NB = 5
        bc = P.sbuf("bc", [128, NB, D], F32)
        t_bc = [T() for _ in range(NB)]
        xt = [P.sbuf(f"xt{i}", [128, D], F32) for i in range(2)]
        t_xt = [T(), T()]
        wk = [P.sbuf(f"wk{i}", [128, D], F32) for i in range(2)]
        t_wk = [T() for _ in range(2)]
        hb = [P.sbuf(f"hb{i}", [128, D], BF16) for i in range(2)]
        t_hb = [T(), T()]
        st6 = P.sbuf("st6", [128, 2, 2, 6], F32)
        mv = P.sbuf("mv", [128, 2, 4], F32)
        t_st = [T(), T()]
        t_mv = [T(), T()]
        halo = P.sbuf("halo", [128, NFC, 2], F32)
        t_halo = [T() for _ in range(NFC)]
        cwt = P.sbuf("cwt", [128, NFC, 4], F32)
        t_cw = T()
        cact = P.sbuf("cact", [128, 8, nseq], F32)
        t_cact = T()
        t_adb = [T(), T()]
        t_modsb = [T(), T()]
        tbl = P.sbuf("tbl", [33, 8], F32)
        t_tbl = T()
        Bsb = P.sbuf("Bsb", [128, 8, 3, 128], BF16)
        Bc = P.sbuf("Bc", [128, 8, 247], BF16)
        t_B = T()
        lamt = P.sbuf("lamt", [128, 264], F32)
        t_lam = T()
        subl = P.sbuf("subl", [128, 2], F32)
        t_subl = T()
        eps_t = P.sbuf("eps_t", [128, 2], F32)
        t_eps = T(const=True)
        Md = P.sbuf("Md", [128, 128], BF16)
        ovl = P.sbuf("ovl_s", [127, 32], BF16)
        Zt = P.sbuf("Zt", [128, 512], BF16)
        qkn = P.sbuf("qkn", [128, 8], F32)
        t_qkn = T()
        t_cst = T(const=True)
        self.ar = P.sbuf("arena", [128, AR_EL], BF16)

        t_mod = [[T() for _ in range(nseq)] for _ in range(DEPTH)]
        t_x = [[T() for _ in range(NT)] for _ in range(nseq)]
        t_wub = [T() for _ in range(DEPTH)]
        t_wdb = [T() for _ in range(DEPTH)]
        t_odb = [T() for _ in range(2)]
        t_evb = [T() for _ in range(2)]
        t_G = T()

        def conv(dst, src, tok):
            P.op("pool", lambda e, d=dst, s=src: e.dma_start(out=d, in_=s), writes=[tok], dma=True)
        self.dma("sp", ident_f[:], ident_d[:, :], writes=[t_ident])
        P.op("pool", lambda e: e.dma_start(out=ident[:], in_=ident_d[:, :]), writes=[t_ident], dma=True)
        P.op("pool", lambda e: e.memset(ones_b[:], 1.0), writes=[t_ones])
        P.op("pool", lambda e: e.memset(ones_f[:], 1.0), writes=[t_ones])
        P.op("pool", lambda e: e.memset(eps_t[:, 0:1], RMS_EPS), writes=[t_eps])
        P.op("pool", lambda e: e.memset(eps_t[:, 1:2], LN_EPS), writes=[t_eps])
        for l in layers:
            if has(l, 1):
                for fc0 in range(0, NFC, 11):
                    conv(wu_b[l, fc0:fc0 + 11], wu_f[l, fc0:fc0 + 11].rearrange("f p k j -> f p (k j)"), t_wub[l])
                    conv(wg_b[l, fc0:fc0 + 11], wg_f[l, fc0:fc0 + 11].rearrange("f p k j -> f p (k j)"), t_wub[l])
                conv(wd_b[l], wd_f[l].rearrange("p f d -> p (f d)"), t_wdb[l])
            if has(l, 0) and l % 2 == 0:
                i = l // 2
                for k in ev_specs:
                    if k == "ev_wc":
                        for c0_ in range(0, 15, 5):
                            conv(ev_b[k][i, c0_:c0_ + 5], ev_f[k][i, c0_:c0_ + 5], t_evb[i])
                    else:
                        conv(ev_b[k][i], ev_f[k][i], t_evb[i])
            if has(l, 0) and l % 2 == 1:
                i = l // 2
                conv(odqk_b[i], odqk_f[i].rearrange("f p k j -> f p (k j)"), t_odb[i])
                conv(odv_b[i], odv_f[i].rearrange("p k n -> p (k n)"), t_odb[i])
                conv(odo_b[i], odo_f[i].rearrange("p k n -> p (k n)"), t_odb[i])

        if any(s_ == 0 and l_ % 2 == 0 for (l_, s_) in plan):
            P.op("pool", lambda e: e.dma_start(out=Md[:], in_=cmask_d[0]), writes=[t_cst], dma=True)
            conv(cst_b["expmat"], exp_d, t_cstb)
            conv(cst_b["selb"], selb_d, t_cstb)
            conv(cst_b["scab"], scab_d, t_cstb)
            P.op("pool", lambda e: e.dma_start(out=ovl[:], in_=ovl_d[:, :]), writes=[t_cst], dma=True)
        P.op("pool", lambda e: e.memset(Zt[:], 0.0), writes=[t_cst])
        adw = [self.view(i * 8192, [128, 8, 512], F32) for i in range(2)]
        adb = [self.view(32768 + i * 1024, [128, 512], F32)[0:nseq] for i in range(2)]
        modsb = [self.view(36864 + i * 1024, [128, 512], F32)[0:nseq] for i in range(2)]
        t_adw = [T(), T()]
        self.dma("sp", cact[:], cT[:, :, :], writes=[t_cact])
        P.op("act", lambda e: e.activation(out=cact[:], in_=cact[:], func=AF.Silu), reads=[t_cact], writes=[t_cact])
        ib = 0
        for l in layers:
            for cb in range(12):
                a = adw[ib % 2]
                ta = t_adw[ib % 2]
                ab, tab = adb[ib % 2], t_adb[ib % 2]
                mb, tmb = modsb[ib % 2], t_modsb[ib % 2]
                ib += 1
                self.dma("sp", ab, ada_b[l:l + 1, cb * 512:(cb + 1) * 512].broadcast_to([nseq, 512]), writes=[tab])
                self.dma("sp", a, ada_w[l, :, cb * 512:(cb + 1) * 512].rearrange("(k p) n -> p k n", p=128), writes=[ta])
                bank = 6 + (cb % 2)
                for kc in range(8):
                    P.op("pe", lambda e, a=a, kc=kc, bank=bank: e.matmul(
                        ps[0:nseq, bank, :], cact[:, kc, :], a[:, kc, :], start=(kc == 0), stop=(kc == 7)),
                        reads=[t_cact, ta], writes=[ps_t[bank]])
                P.op("dve", lambda e, mb=mb, ab=ab, bank=bank: e.tensor_tensor(
                    out=mb, in0=ps[0:nseq, bank, :], in1=ab, op=ALU.add),
                    reads=[ps_t[bank], tab], writes=[tmb])
                self.dma("sp", mod_d[l, :, cb * 512:(cb + 1) * 512], mb, reads=[tmb], writes=t_mod[l])
        if self.dbg:
            self.dbg_out("dbg_mod", mod_d[layers[0]], [nseq, 6 * D], F32, t_mod[layers[0]])

        need_bias = any(s == 0 for (l, s) in plan)
        if need_bias:
            P.barrier()
            self.dma("sp", tbl[0:32, :], relb[:, :], writes=[t_tbl])
            P.op("pool", lambda e: e.memset(tbl[32:33, :], NEG), writes=[t_tbl])
            ohb = [self.view(i * 8192, [33, 4096], F32) for i in range(2)]
            gsb_ = [self.view(16384 + i * 8192, [8, 4096], F32) for i in range(2)]
            t_oh = [T(), T()]
            t_g = [T(), T()]
            nch = (N_OH + 4095) // 4096
            for c in range(nch):
                c0 = c * 4096
                w = min(4096, N_OH - c0)
                o_, to = ohb[c % 2], t_oh[c % 2]
                g_, tg = gsb_[c % 2], t_g[c % 2]
                self.dma("sp", o_[0:33, 0:w], oh_d[:, c0:c0 + w], writes=[to])
                for s0 in range(0, w, 512):
                    sw = min(512, w - s0)
                    bank = (s0 // 512) % 2
                    P.op("pe", lambda e, o_=o_, s0=s0, sw=sw, bank=bank: e.matmul(
                        ps[0:8, bank, 0:sw], tbl[0:33, :], o_[0:33, s0:s0 + sw], start=True, stop=True),
                        reads=[t_tbl, to], writes=[ps_t[bank]])
                    P.op("act", lambda e, g_=g_, s0=s0, sw=sw, bank=bank: e.copy(out=g_[0:8, s0:s0 + sw], in_=ps[0:8, bank, 0:sw]),
                         reads=[ps_t[bank]], writes=[tg])
                self.dma("sp", G_d[:, c0:c0 + w], g_[0:8, 0:w], reads=[tg], writes=[t_G])
            for tau in range(2):
                P.op("pool", lambda e, tau=tau: e.dma_start(
                    out=Bsb[:, :, tau, :], in_=G_d[:, tau * 16384:(tau + 1) * 16384].rearrange("h (k q) -> k h q", k=128)),
                    reads=[t_G], writes=[t_B], dma=True)
            P.op("pool", lambda e: e.dma_start(out=Bc[:], in_=G_d[:, N_ATT:N_OH].rearrange("h (p m) -> p h m", p=128)),
                 reads=[t_G], writes=[t_B], dma=True)
            for h_ in range(8):
                P.op("pool", lambda e, h_=h_: e.dma_start(out=Bsb[:, h_, 2, :], in_=cmask_d[1]), writes=[t_B], dma=True)
            self.dbg_out("dbg_Bsb", Bsb[:], [128, 8, 3, 128], BF16, [t_B])
            self.dbg_out("dbg_Bc", Bc[:], [128, 8, 247], BF16, [t_B])
        P.barrier()

        def act_recip(out_ap, in_ap, rd, wr):
            P.op("act", lambda e: e.activation(out=out_ap, in_=in_ap, func=AF.Ln), reads=rd, writes=wr)
            P.op("act", lambda e: e.activation(out=out_ap, in_=out_ap, func=AF.Exp, scale=-1.0), reads=wr, writes=wr)

        def act_rstd(out_ap, in_ap, scale, eps_ap, rd, wr):
            P.op("act", lambda e: e.activation(out=out_ap, in_=in_ap, func=AF.Ln, bias=eps_ap, scale=scale), reads=rd + [t_eps], writes=wr)
            P.op("act", lambda e: e.activation(out=out_ap, in_=out_ap, func=AF.Exp, scale=-0.5), reads=wr, writes=wr)

        def bc_load(slot, src, rd, plus_one):
            self.dma("sp", bc[:, slot, :], src.broadcast_to([128, D]), reads=rd, writes=[t_bc[slot]])
            if plus_one:
                P.op("pool", lambda e, slot=slot: e.tensor_scalar_add(out=bc[:, slot, :], in0=bc[:, slot, :], scalar1=1.0),
                     reads=[t_bc[slot]], writes=[t_bc[slot]])

        def emit_hT(b, l, sub, xsrc):
            load_mod_h(b, l, sub)
            for t in range(NT):
                xi = t % 2
                self.dma("sp", xt[xi][:], xsrc[b, t * 128:(t + 1) * 128, :], reads=[t_x[b][t]], writes=[t_xt[xi]])
                h_from_x(xi, t)()

        def h_from_x(xi, t, tbank=7, src=None, tsrc=None):
            if src is None:
                src, tsrc = xt[xi], t_xt[xi]
            hps = ps[:, 0:2, :].rearrange("p a n -> p (a n)")
            P.op("dve", lambda e: e.tensor_tensor(out=hps, in0=src[:], in1=bc[:, 0, :], op=ALU.mult),
                 reads=[tsrc, t_bc[0]], writes=[ps_t[0], ps_t[1]])
            P.op("dve", lambda e: e.tensor_tensor(out=hb[xi][:], in0=hps, in1=bc[:, 1, :], op=ALU.add),
                 reads=[ps_t[0], ps_t[1], t_bc[1]], writes=[t_hb[xi]])

            def fin():
                psb = ps[:, tbank, :].bitcast(BF16)
                for kc in range(8):
                    P.op("pe", lambda e, kc=kc: e.transpose(
                        psb[:, kc * 128:(kc + 1) * 128], hb[xi][:, kc * 128:(kc + 1) * 128], ident[:]),
                        reads=[t_hb[xi], t_ident], writes=[ps_t[tbank]])
                P.op("act", lambda e: e.copy(
                    out=hT[:, :, t * 128:(t + 1) * 128], in_=psb.rearrange("p (k j) -> p k j", k=8)),
                    reads=[ps_t[tbank]], writes=[t_hT[t]])
            return fin

        def load_mod_h(b, l, sub):
            o = 0 if sub == 0 else 3
            bc_load(0, mod_d[l, b:b + 1, (o + 1) * D:(o + 2) * D], [t_mod[l][b]], True)
            bc_load(1, mod_d[l, b:b + 1, (o + 0) * D:(o + 1) * D], [t_mod[l][b]], False)

        def begin_sublayer(b, l, sub, xsrc):
            if not self.have_hT:
                emit_hT(b, l, sub, xsrc)
            epi_setup(b, l, sub)
            if self.nxt is not None:
                load_mod_h(b, self.nxt[0], self.nxt[1])

        def epilogue(b, l, sub, t, ybank, xsrc, tbank=7):
            xi = t % 2
            w_ = wk[xi]
            tw_ = t_wk[xi]
            st_ = st6[:, xi]
            mv_ = mv[:, xi]
            nxt = self.nxt
            self.dma("sp", xt[xi][:], xsrc[b, t * 128:(t + 1) * 128, :], reads=[t_x[b][t]], writes=[t_xt[xi]])
            yv = ps[:, ybank:ybank + 2, :].rearrange("p a n -> p (a n)")
            yt = [ps_t[ybank], ps_t[ybank + 1]]
            P.op("dve", lambda e: e.tensor_tensor(out=yv, in0=yv, in1=bc[:, 2, :], op=ALU.mult), reads=yt + [t_bc[2]], writes=yt)
            P.op("dve", lambda e: e.scalar_tensor_tensor(out=w_[:], in0=xt[xi][:], scalar=ALPHA, in1=yv, op0=ALU.mult, op1=ALU.add),
                 reads=[t_xt[xi]] + yt, writes=[tw_])
            for hh in range(2):
                P.op("dve", lambda e, hh=hh: e.bn_stats(out=st_[:, hh, :], in_=w_[:, hh * 512:(hh + 1) * 512]),
                     reads=[tw_], writes=[t_st[xi]])
            P.op("dve", lambda e: e.bn_aggr(out=mv_[:, 0:2], in_=st_), reads=[t_st[xi]], writes=[t_mv[xi]])
            act_rstd(mv_[:, 2:3], mv_[:, 1:2], 1.0, eps_t[:, 1:2], [t_mv[xi]], [t_mv[xi]])
            P.op("dve", lambda e: e.scalar_tensor_tensor(out=mv_[:, 3:4], in0=mv_[:, 0:1], scalar=-1.0, in1=mv_[:, 2:3],
                                                         op0=ALU.mult, op1=ALU.mult), reads=[t_mv[xi]], writes=[t_mv[xi]])
            P.op("act", lambda e: e.activation(out=w_[:], in_=w_[:], func=AF.Identity, bias=mv_[:, 3:4], scale=mv_[:, 2:3]),
                 reads=[tw_, t_mv[xi]], writes=[tw_])

            def stage2():
                P.op("pool", lambda e: e.tensor_tensor(out=w_[:], in0=w_[:], in1=bc[:, 3, :], op=ALU.mult),
                     reads=[tw_, t_bc[3]], writes=[tw_])
                P.op("pool", lambda e: e.tensor_tensor(out=w_[:], in0=w_[:], in1=bc[:, 4, :], op=ALU.add),
                     reads=[tw_, t_bc[4]], writes=[tw_])
                self.dma("sp", out[b, t * 128:(t + 1) * 128, :], w_[:], reads=[tw_], writes=[t_x[b][t]])
                if nxt is not None:
                    return h_from_x(xi, t, tbank, w_, tw_)
                return None
            return stage2

        def ep_pipeline(tiles, mm, ybank_of, b, l, sub, xsrc, tbank):
            s2_prev = None
            fins = []
            for t in tiles:
                mm(t)
                while len(fins) > 1:
                    f_ = fins.pop(0)
                    if f_ is not None:
                        f_()
                s2_cur = epilogue(b, l, sub, t, ybank_of(t), xsrc, tbank)
                if s2_prev is not None:
                    fins.append(s2_prev())
                s2_prev = s2_cur
            while len(fins) > 1:
                f_ = fins.pop(0)
                if f_ is not None:
                    f_()
            if s2_prev is not None:
                fins.append(s2_prev())
            for f_ in fins:
                if f_ is not None:
                    f_()

        def epi_setup(b, l, sub):
            o = 2 if sub == 0 else 5
            bc_load(2, mod_d[l, b:b + 1, o * D:(o + 1) * D], [t_mod[l][b]], True)
            bc_load(3, ln_g[l, sub:sub + 1, :], [], False)
            bc_load(4, ln_b[l, sub:sub + 1, :], [], False)

        def ffn(b, l, xsrc):
            P.barrier()
            wd = self.view(0, [128, NFC, D], BF16)
            t_wd = T()
            actT = self.view(22528, [128, NFC, 1024], BF16)
            t_act = [T() for _ in range(NFC)]
            NW = 4
            wug = [self.view(45056 + i * 2048, [128, 2, 8, 128], BF16) for i in range(NW)]
            t_wug = [T() for _ in range(NW)]
            gsb = [self.view(53248 + i * 1032, [128, 514], F32) for i in range(2)]
            t_gsb = [T(), T()]
            tmp = [self.view(55312 + i * 1024, [128, 512], F32) for i in range(3)]
            t_tmp = [T() for _ in range(3)]
            begin_sublayer(b, l, 1, xsrc)
            self.dbg_out("dbg_hT", hT[:], [128, 8, S], BF16, t_hT)
            self.dma("sp", wd, wd_b[l].rearrange("p (f d) -> p f d", f=NFC), reads=[t_wdb[l]], writes=[t_wd])
            self.dma("sp", cwt[:], cw_d[l].rearrange("(f p) k -> p f k", p=128), writes=[t_cw])
            iw = 0
            ih = 0
            for tb in range(2):
                for fc in range(NFC):
                    w = wug[iw % NW]
                    tw = t_wug[iw % NW]
                    iw += 1
                    self.dma("sp", w[:, 0], wu_b[l, fc].rearrange("p (k j) -> p k j", k=8), reads=[t_wub[l]], writes=[tw])
                    self.dma("sp", w[:, 1], wg_b[l, fc].rearrange("p (k j) -> p k j", k=8), reads=[t_wub[l]], writes=[tw])
                    for half in range(2):
                        c0 = tb * 1024 + half * 512
                        a0 = half * 512
                        hts = t_hT[c0 // 128:c0 // 128 + 4]
                        bu = ih % 2
                        bg = 2 + ih % 2
                        g = gsb[ih % 2]
                        tg = t_gsb[ih % 2]
                        ih += 1
                        for kc in range(8):
                            P.op("pe", lambda e, w=w, kc=kc, bu=bu, c0=c0: e.matmul(ps[:, bu, :], w[:, 0, kc, :], hT[:, kc, c0:c0 + 512],
                                                                                   start=(kc == 0), stop=(kc == 7)),
                                 reads=[tw] + hts, writes=[ps_t[bu]])
                        for kc in range(8):
                            P.op("pe", lambda e, w=w, kc=kc, bg=bg, c0=c0: e.matmul(ps[:, bg, :], w[:, 1, kc, :], hT[:, kc, c0:c0 + 512],
                                                                                   start=(kc == 0), stop=(kc == 7)),
                                 reads=[tw] + hts, writes=[ps_t[bg]])
                        if c0 == 0:
                            P.op("pool", lambda e, g=g: e.memset(g[:, 0:2], 0.0), writes=[tg])
                        else:
                            P.op("pool", lambda e, g=g, fc=fc: e.tensor_copy(out=g[:, 0:2], in_=halo[:, fc, :]),
                                 reads=[t_halo[fc]], writes=[tg])
                        P.op("act", lambda e, g=g, bg=bg: e.copy(out=g[:, 2:514], in_=ps[:, bg, :]), reads=[ps_t[bg]], writes=[tg])
                        P.op("pool", lambda e, g=g, fc=fc: e.tensor_copy(out=halo[:, fc, :], in_=g[:, 512:514]),
                             reads=[tg], writes=[t_halo[fc]])
                        P.op("dve", lambda e, g=g, fc=fc: e.tensor_scalar(out=tmp[0], in0=g[:, 2:514], scalar1=cwt[:, fc, 2:3],
                                                                          scalar2=cwt[:, fc, 3:4], op0=ALU.mult, op1=ALU.add),
                             reads=[tg, t_cw], writes=[t_tmp[0]])
                        P.op("dve", lambda e, g=g, fc=fc: e.scalar_tensor_tensor(out=tmp[1], in0=g[:, 1:513], scalar=cwt[:, fc, 1:2],
                                                                                 in1=tmp[0], op0=ALU.mult, op1=ALU.add),
                             reads=[tg, t_cw, t_tmp[0]], writes=[t_tmp[1]])
                        P.op("dve", lambda e, g=g, fc=fc: e.scalar_tensor_tensor(out=tmp[2], in0=g[:, 0:512], scalar=cwt[:, fc, 0:1],
                                                                                 in1=tmp[1], op0=ALU.mult, op1=ALU.add),
                             reads=[tg, t_cw, t_tmp[1]], writes=[t_tmp[2]])
                        P.op("act", lambda e: e.activation(out=tmp[0], in_=tmp[2], func=AF.Silu),
                             reads=[t_tmp[2]], writes=[t_tmp[0]])
                        P.op("dve", lambda e, fc=fc, bu=bu, a0=a0: e.tensor_tensor(out=actT[:, fc, a0:a0 + 512], in0=tmp[0], in1=ps[:, bu, :],
                                                                                  op=ALU.mult),
                             reads=[t_tmp[0], ps_t[bu]], writes=[t_act[fc]])
                def mm_down(t):
                    tt = t % 8
                    yb = 4 if tt % 2 == 0 else 6
                    for dh in range(2):
                        for fc in range(NFC):
                            P.op("pe", lambda e, fc=fc, tt=tt, dh=dh, yb=yb: e.matmul(
                                ps[:, yb + dh, :], actT[:, fc, tt * 128:(tt + 1) * 128], wd[:, fc, dh * 512:(dh + 1) * 512],
                                start=(fc == 0), stop=(fc == NFC - 1)),
                                reads=[t_act[fc], t_wd], writes=[ps_t[yb + dh]])
                ep_pipeline([tb * 8 + tt for tt in range(8)], mm_down, lambda t: 4 if t % 2 == 0 else 6, b, l, 1, xsrc, 2)

        def diff_attn(b, l, xsrc):
            i = l // 2
            lam_init = 0.8 - 0.6 * math.exp(-0.3 * l)
            P.barrier()
            V = self.view(0, [128, NT, D], BF16)
            t_V = [T() for _ in range(NT)]
            oT = self.view(16384, [128, 8, S], BF16)
            t_oT = [[T() for _ in range(4)] for _ in range(8)]
            qk = [[self.view(32768 + (2 * j + c) * 2048, [128, S], BF16) for c in range(2)] for j in range(2)]
            t_qk = [[T(), T()], [T(), T()]]
            E = [self.view(40960 + j * 512, [128, 512], BF16) for j in range(4)]
            t_E = [T() for _ in range(4)]
            wo = self.view(43008, [128, 8, D], BF16)
            t_wo = T()
            wqk = [self.view(51200 + j * 2048, [128, 2, 8, 128], BF16) for j in range(2)]
            t_wqk = [T(), T()]
            f32w = [self.view(55296 + j * 1024, [128, 512], F32) for j in range(4)]
            t_f = [T() for _ in range(4)]

            begin_sublayer(b, l, 0, xsrc)
            self.dma("sp", lamt[:, 0:256], dlam[i:i + 1, :].broadcast_to([128, 256]), writes=[t_lam])
            P.op("dve", lambda e: e.tensor_tensor(out=lamt[:, 0:64], in0=lamt[:, 0:64], in1=lamt[:, 64:128], op=ALU.mult),
                 reads=[t_lam], writes=[t_lam])
            P.op("dve", lambda e: e.tensor_tensor(out=lamt[:, 128:192], in0=lamt[:, 128:192], in1=lamt[:, 192:256], op=ALU.mult),
                 reads=[t_lam], writes=[t_lam])
            P.op("dve", lambda e: e.reduce_sum(out=lamt[:, 256:257], in_=lamt[:, 0:64], axis=AX.X), reads=[t_lam], writes=[t_lam])
            P.op("dve", lambda e: e.reduce_sum(out=lamt[:, 257:258], in_=lamt[:, 128:192], axis=AX.X), reads=[t_lam], writes=[t_lam])
            P.op("act", lambda e: e.activation(out=lamt[:, 258:260], in_=lamt[:, 256:258], func=AF.Exp), reads=[t_lam], writes=[t_lam])
            P.op("dve", lambda e: e.scalar_tensor_tensor(out=lamt[:, 260:261], in0=lamt[:, 259:260], scalar=-lam_init,
                                                         in1=lamt[:, 258:259], op0=ALU.add, op1=ALU.subtract),
                 reads=[t_lam], writes=[t_lam])
            self.dma("sp", subl[:, 0:1], dsub[i], writes=[t_subl])
            P.op("pool", lambda e: e.tensor_scalar_mul(out=subl[:, 0:1], in0=subl[:, 0:1], scalar1=1.0 - lam_init),
                 reads=[t_subl], writes=[t_subl])
            self.dma("sp", wo, odv_b[i].rearrange("p (k n) -> p k n", k=8), reads=[t_odb[i]], writes=[t_wo])
            for t in range(NT):
                for dh in range(2):
                    bank = 5 + dh
                    for kc in range(8):
                        P.op("pe", lambda e, t=t, dh=dh, kc=kc, bank=bank: e.matmul(
                            ps[:, bank, :], hT[:, kc, t * 128:(t + 1) * 128], wo[:, kc, dh * 512:(dh + 1) * 512],
                            start=(kc == 0), stop=(kc == 7)), reads=[t_hT[t], t_wo], writes=[ps_t[bank]])
                    if dh == 0:
                        P.op("act", lambda e, t=t, bank=bank: e.copy(out=V[:, t, 0:512], in_=ps[:, bank, :]),
                             reads=[ps_t[bank]], writes=[t_V[t]])
                    else:
                        P.op("dve", lambda e, t=t, bank=bank: e.tensor_copy(out=V[:, t, 512:1024], in_=ps[:, bank, :]),
                             reads=[ps_t[bank]], writes=[t_V[t]])
            Esum = [self.view(59392 + m_ * 1024, [128, 512], F32) for m_ in range(2)]
            t_Es = [T(), T()]

            def hq_block(h, Q, qT, kT, tq, tk, sidx, par, deferred):
                nk = 4 * Q + 4
                tiles = [(kt, m) for kt in range(nk) for m in range(2)]
                ob = (3, 5) if par == 0 else (4, 6)
                LAG = 2
                pend = []

                def do_scores(kt, m, sidx):
                    jj = kt - 4 * Q
                    c0 = max(0, jj) * 128
                    sb = sidx % 3
                    eb = sidx % 4
                    q0 = Q * 512
                    hasb = kt >= 4 * Q - 1
                    P.op("pe", lambda e: e.matmul(
                        ps[:, sb, c0:512], kT[m * 64:(m + 1) * 64, kt * 128:(kt + 1) * 128],
                        qT[m * 64:(m + 1) * 64, q0 + c0:q0 + 512], start=True, stop=not hasb),
                        reads=[tq, tk], writes=[ps_t[sb]])
                    if hasb:
                        if jj < 0:
                            rhs = Bsb[:, h, 1, :]
                            cs, ce = 0, 128
                        elif jj == 3:
                            rhs = Bsb[:, h, 0, :]
                            cs, ce = 384, 512
                        else:
                            rhs = Bsb[:, h, 0:2, :].rearrange("p t q -> p (t q)")
                            cs, ce = c0, c0 + 256
                        P.op("pe", lambda e: e.matmul(ps[:, sb, cs:ce], ident[:], rhs, start=False, stop=True),
                             reads=[t_ident, t_B], writes=[ps_t[sb]])
                    P.op("act", lambda e: e.activation(out=E[eb][:, c0:512], in_=ps[:, sb, c0:512], func=AF.Exp),
                         reads=[ps_t[sb]], writes=[t_E[eb]])
                    if kt == 0:
                        P.op("dve", lambda e: e.tensor_copy(out=Esum[m][:, :], in_=E[eb][:, :]), reads=[t_E[eb]], writes=[t_Es[m]])
                    else:
                        P.op("dve", lambda e: e.tensor_tensor(out=Esum[m][:, c0:512], in0=Esum[m][:, c0:512], in1=E[eb][:, c0:512], op=ALU.add),
                             reads=[t_E[eb], t_Es[m]], writes=[t_Es[m]])
                    return (kt, m, c0, eb)

                def do_pv(kt, m, c0, eb):
                    P.op("pe", lambda e: e.matmul(ps[:, ob[m], c0:512], V[:, kt, h * 128:(h + 1) * 128], E[eb][:, c0:512],
                                                  start=(kt == 0), stop=(kt == nk - 1)),
                         reads=[t_V[kt], t_E[eb]], writes=[ps_t[ob[m]]])

                for ti, (kt, m) in enumerate(tiles):
                    pend.append(do_scores(kt, m, sidx))
                    sidx += 1
                    if len(pend) > LAG:
                        do_pv(*pend.pop(0))
                    if ti == 5 and deferred is not None:
                        deferred()
                        deferred = None
                while pend:
                    do_pv(*pend.pop(0))
                if deferred is not None:
                    deferred()
                r0, r1, u0, u1 = f32w
                P.op("pe", lambda e: e.matmul(ps[:, 7, :], ones_f[:], Esum[0], start=True, stop=True), reads=[t_ones, t_Es[0]], writes=[ps_t[7]])
                act_recip(r0, ps[:, 7, :], [ps_t[7]], [t_f[0]])
                P.op("pe", lambda e: e.matmul(ps[:, 7, :], ones_f[:], Esum[1], start=True, stop=True), reads=[t_ones, t_Es[1]], writes=[ps_t[7]])
                act_recip(r1, ps[:, 7, :], [ps_t[7]], [t_f[1]])
                P.op("dve", lambda e: e.tensor_tensor(out=u0, in0=ps[:, ob[0], :], in1=r0, op=ALU.mult),
                     reads=[ps_t[ob[0]], t_f[0]], writes=[t_f[2]])
                P.op("dve", lambda e: e.tensor_tensor(out=u1, in0=ps[:, ob[1], :], in1=r1, op=ALU.mult),
                     reads=[ps_t[ob[1]], t_f[1]], writes=[t_f[3]])
                P.op("dve", lambda e: e.scalar_tensor_tensor(out=u0, in0=u1, scalar=lamt[:, 260:261], in1=u0,
                                                             op0=ALU.mult, op1=ALU.add),
                     reads=[t_f[3], t_f[2], t_lam], writes=[t_f[2]])
                P.op("dve", lambda e: e.tensor_tensor(out=r0, in0=u0, in1=u0, op=ALU.mult), reads=[t_f[2]], writes=[t_f[0]])

                def tail():
                    P.op("pe", lambda e: e.matmul(ps[:, 7, :], ones_f[:], r0, start=True, stop=True),
                         reads=[t_ones, t_f[0]], writes=[ps_t[7]])
                    act_rstd(r1, ps[:, 7, :], 1.0 / 128.0, eps_t[:, 0:1], [ps_t[7]], [t_f[1]])
                    P.op("dve", lambda e: e.scalar_tensor_tensor(out=oT[:, h, Q * 512:(Q + 1) * 512], in0=u0, scalar=subl[:, 0:1], in1=r1,
                                                                 op0=ALU.mult, op1=ALU.mult),
                         reads=[t_f[2], t_f[1], t_subl], writes=[t_oT[h][Q]])
                return sidx, tail

            sidx = 0
            par = 0
            deferred = None
            for h in range(8):
                j2 = h % 2
                w = wqk[j2]
                tw = t_wqk[j2]
                qT, kT = qk[j2]
                tq, tk = t_qk[j2]
                self.dma("sp", w[:, 0], odqk_b[i, h].rearrange("p (k j) -> p k j", k=8), reads=[t_odb[i]], writes=[tw])
                self.dma("sp", w[:, 1], odqk_b[i, 8 + h].rearrange("p (k j) -> p k j", k=8), reads=[t_odb[i]], writes=[tw])
                for c in range(2):
                    for tb in range(4):
                        for kc in range(8):
                            P.op("pe", lambda e, w=w, c=c, kc=kc, tb=tb: e.matmul(
                                ps[:, 7, :], w[:, c, kc, :], hT[:, kc, tb * 512:(tb + 1) * 512], start=(kc == 0), stop=(kc == 7)),
                                reads=[tw] + t_hT[tb * 4:tb * 4 + 4], writes=[ps_t[7]])
                        dst = qT if c == 0 else kT
                        P.op("act", lambda e, dst=dst, tb=tb, c=c: e.activation(
                            out=dst[:, tb * 512:(tb + 1) * 512], in_=ps[:, 7, :], func=AF.Identity, scale=(0.125 if c == 0 else 1.0)),
                            reads=[ps_t[7]], writes=[tq if c == 0 else tk])
                for Q in range(4):
                    sidx, deferred = hq_block(h, Q, qT, kT, tq, tk, sidx, par, deferred)
                    par ^= 1
            if deferred is not None:
                deferred()
            self.dbg_out("dbg_oT", oT, [128, 8, S], BF16, [t for r in t_oT for t in r])
            self.dma("sp", wo, odo_b[i].rearrange("p (k n) -> p k n", k=8), reads=[t_odb[i]], writes=[t_wo])
            def mm_wo(t):
                yb = 3 if t % 2 == 0 else 5
                for dh in range(2):
                    for h in range(8):
                        P.op("pe", lambda e, t=t, dh=dh, h=h, yb=yb: e.matmul(
                            ps[:, yb + dh, :], oT[:, h, t * 128:(t + 1) * 128], wo[:, h, dh * 512:(dh + 1) * 512],
                            start=(h == 0), stop=(h == 7)), reads=[t_oT[h][t // 4], t_wo], writes=[ps_t[yb + dh]])
            ep_pipeline(list(range(NT)), mm_wo, lambda t: 3 if t % 2 == 0 else 5, b, l, 0, xsrc, 7)

        def run_tiles(specs, acc, E, t_E, sidx, LAG=2):
            P.op("pe", lambda e: e.matmul(ps[:, acc, :], Zt[:, 0:128], Zt[:], start=True, stop=False),
                 reads=[t_cst], writes=[ps_t[acc]])
            pend = []
            n = len(specs)

            def scores(sp, sidx):
                sb = sidx % 3
                eb = sidx % 4
                c0, c1 = sp["c0"], sp["c1"]
                ex = sp["extra"]
                P.op("pe", lambda e: e.matmul(ps[:, sb, c0:c1], sp["lhsT"], sp["rhs"], start=True, stop=(len(ex) == 0)),
                     reads=sp["rd"], writes=[ps_t[sb]])
                for xi_, (xl, xr, cs, ce, xrd) in enumerate(ex):
                    P.op("pe", lambda e, xl=xl, xr=xr, cs=cs, ce=ce, last=(xi_ == len(ex) - 1): e.matmul(
                        ps[:, sb, cs:ce], xl, xr, start=False, stop=last), reads=xrd, writes=[ps_t[sb]])
                P.op("act", lambda e: e.activation(out=E[eb][:, c0:c1], in_=ps[:, sb, c0:c1], func=AF.Exp),
                     reads=[ps_t[sb]], writes=[t_E[eb]])
                return (sp, eb)

            def pv(sp, eb, idx):
                c0, c1 = sp["c0"], sp["c1"]
                P.op("pe", lambda e: e.matmul(ps[:, acc, c0:c1], sp["v"], E[eb][:, c0:c1], start=False, stop=(idx == n - 1)),
                     reads=sp["vrd"] + [t_E[eb]], writes=[ps_t[acc]])

            done = 0
            for sp in specs:
                pend.append(scores(sp, sidx))
                sidx += 1
                if len(pend) > LAG:
                    a_, b_ = pend.pop(0)
                    pv(a_, b_, done)
                    done += 1
            while pend:
                a_, b_ = pend.pop(0)
                pv(a_, b_, done)
                done += 1
            return sidx

        def even_attn(b, l, xsrc):
            i = l // 2
            evb = {k: v[i] for k, v in ev_b.items()}
            rd_w = [t_evb[i]]
            sc_mla = 96 ** -0.5
            P.barrier()
            off = [0]

            def alloc(n):
                o = off[0]
                off[0] = o + n + (n % 2)
                assert off[0] <= AR_EL, off[0]
                return o

            oM = self.view(alloc(4 * S), [128, 4, S], BF16)
            t_oM = [[T() for _ in range(4)] for _ in range(4)]
            t_oN = [[T() for _ in range(4)] for _ in range(4)]
            E = [self.view(alloc(512), [128, 512], BF16) for _ in range(4)]
            t_E = [T() for _ in range(4)]
            fw_ = [self.view(alloc(1024), [128, 512], F32) for _ in range(4)]
            t_fw = [T() for _ in range(4)]
            wcb = [self.view(alloc(3 * 1024), [128, 3, 8, 128], BF16) for _ in range(1)]
            t_wcb = [T()]
            base_off = off[0]
            cqT = self.view(alloc(3 * S), [128, 3, S], BF16)
            ckvT = self.view(alloc(2 * S), [128, 2, S], BF16)
            krT = self.view(alloc(S), [128, S], BF16)
            t_cq = [T() for _ in range(4)]
            t_ckv = [T() for _ in range(4)]
            t_kr = [T() for _ in range(4)]
            rope = self.view(alloc(2 * S), [128, S], F32)
            t_rope = T()
            lat = self.view(alloc(3 * 1024), [128, 3, 512], F32)
            t_lat = [T() for _ in range(3)]
            Vm = self.view(alloc(NT * 1024), [128, NT, 8, 128], BF16)
            t_Vm = [T() for _ in range(NT)]
            QK = [[self.view(alloc(S), [128, S], BF16) for _ in range(2)] for _ in range(2)]
            t_QK = [[T(), T()], [T(), T()]]
            wcoff = base_off - 3 * 1024
            uqb = [self.view(wcoff + j_ * 384, [128, 3, 128], BF16) for j_ in range(2)]
            ukb = [self.view(wcoff + 768 + j_ * 256, [128, 2, 128], BF16) for j_ in range(2)]
            t_ub = [T(), T()]
            ukv = self.view(wcoff + 1280, [128, 2, 512], BF16)
            t_ukv = T()

            begin_sublayer(b, l, 0, xsrc)
            self.dma("sp", rope[0:64, :], rope_d[:, :], writes=[t_rope])
            self.dma("sp", qkn[:, 0:3], ev_qn[i], writes=[t_qkn])
            self.dma("sp", qkn[:, 3:5], ev_kvn[i], writes=[t_qkn])
            wc = wcb[0]
            for (nch, ch0, gcol, dstT, tdst) in [(3, 0, 0, cqT, t_cq), (2, 3, 3, ckvT, t_ckv)]:
                for c in range(nch):
                    self.dma("sp", wc[:, c], evb["ev_wc"][ch0 + c].rearrange("p (k j) -> p k j", k=8), reads=rd_w, writes=[t_wcb[0]])
                for tb in range(4):
                    for c in range(nch):
                        bank = 6 + (c % 2)
                        for kc in range(8):
                            P.op("pe", lambda e, c=c, kc=kc, tb=tb, bank=bank: e.matmul(
                                ps[:, bank, :], wc[:, c, kc, :], hT[:, kc, tb * 512:(tb + 1) * 512], start=(kc == 0), stop=(kc == 7)),
                                reads=[t_wcb[0]] + t_hT[tb * 4:tb * 4 + 4], writes=[ps_t[bank]])
                        P.op("act", lambda e, c=c, bank=bank: e.copy(out=lat[:, c, :], in_=ps[:, bank, :]),
                             reads=[ps_t[bank]], writes=[t_lat[c]])
                        sq = fw_[c % 2]
                        P.op("act", lambda e, sq=sq, bank=bank: e.activation(out=sq, in_=ps[:, bank, :], func=AF.Square),
                             reads=[ps_t[bank]], writes=[t_fw[c % 2]])
                        P.op("pe", lambda e, sq=sq, c=c, nch=nch: e.matmul(ps[:, 5, :], ones_f[:], sq, start=(c == 0), stop=(c == nch - 1)),
                             reads=[t_ones, t_fw[c % 2]], writes=[ps_t[5]])
                    rs = fw_[2]
                    act_rstd(rs, ps[:, 5, :], 1.0 / (128.0 * nch), eps_t[:, 0:1], [ps_t[5]], [t_fw[2]])
                    for c in range(nch):
                        P.op("dve", lambda e, c=c, tb=tb, rs=rs, dstT=dstT, gcol=gcol: e.scalar_tensor_tensor(
                            out=dstT[:, c, tb * 512:(tb + 1) * 512], in0=lat[:, c, :], scalar=qkn[:, gcol + c:gcol + c + 1], in1=rs,
                            op0=ALU.mult, op1=ALU.mult), reads=[t_lat[c], t_fw[2], t_qkn], writes=[tdst[tb]])

            def rope_rot(psb, dst, cols, scale, rd, wr):
                tA, tB = fw_[0], fw_[1]
                P.op("dve", lambda e: e.scalar_tensor_tensor(out=tA[0:32, :], in0=ps[32:64, psb, :], scalar=scale, in1=rope[32:64, cols],
                                                             op0=ALU.mult, op1=ALU.mult),
                     reads=[ps_t[psb], t_rope] + rd, writes=[t_fw[0]])
                P.op("dve", lambda e: e.scalar_tensor_tensor(out=tB[0:32, :], in0=ps[0:32, psb, :], scalar=scale, in1=rope[0:32, cols],
                                                             op0=ALU.mult, op1=ALU.mult),
                     reads=[ps_t[psb], t_rope] + rd, writes=[t_fw[1]])
                P.op("pool", lambda e: e.tensor_tensor(out=dst[0:32, cols], in0=tA[0:32, :], in1=tB[0:32, :], op=ALU.add),
                     reads=[t_fw[0], t_fw[1]], writes=wr)

            self.dma("sp", wc[:, 0], evb["ev_wc"][5].rearrange("p (k j) -> p k j", k=8), reads=rd_w, writes=[t_wcb[0]])
            for tb in range(4):
                for kc in range(8):
                    P.op("pe", lambda e, kc=kc, tb=tb: e.matmul(ps[:, 7, :], wc[:, 0, kc, :], hT[:, kc, tb * 512:(tb + 1) * 512],
                                                                start=(kc == 0), stop=(kc == 7)),
                         reads=[t_wcb[0]] + t_hT[tb * 4:tb * 4 + 4], writes=[ps_t[7]])
                rope_rot(7, krT, slice(tb * 512, (tb + 1) * 512), 1.0, [], [t_kr[tb]])

            P.barrier()
            P.op("pool", lambda e: e.memset(Vm.rearrange("p t h d -> p (t h) d")[:, :, 64:128], 1.0), writes=t_Vm)
            self.dma("sp", ukv, evb["ev_ukvv"].rearrange("p (k n) -> p k n", k=2), reads=rd_w, writes=[t_ukv])
            for t in range(NT):
                for kc in range(2):
                    P.op("pe", lambda e, t=t, kc=kc: e.matmul(ps[:, 6, :], ckvT[:, kc, t * 128:(t + 1) * 128], ukv[:, kc, :],
                                                              start=(kc == 0), stop=(kc == 1)),
                         reads=[t_ckv[t // 4], t_ukv], writes=[ps_t[6]])
                P.op("act", lambda e, t=t: e.copy(out=Vm[:, t, :, 0:64], in_=ps[:, 6, :].rearrange("p (h d) -> p h d", h=8)),
                     reads=[ps_t[6]], writes=[t_Vm[t]])
            for j in range(2):
                for c in range(2):
                    P.op("pool", lambda e, j=j, c=c: e.memset(QK[j][c][32:64, :], 0.0), writes=[t_QK[j][c]])
                P.op("pool", lambda e, j=j: e.tensor_copy(out=QK[j][1][0:32, :], in_=krT[0:32, :]), reads=t_kr, writes=[t_QK[j][1]])
            sidx = 0
            for h in range(8):
                j = h % 2
                QT, KT = QK[j]
                tq, tk = t_QK[j]
                self.dma("sp", uqb[j], evb["ev_uq"][h].rearrange("p (k j) -> p k j", k=3), reads=rd_w, writes=[t_ub[j]])
                self.dma("sp", ukb[j], evb["ev_ukvk"][h].rearrange("p (k j) -> p k j", k=2), reads=rd_w, writes=[t_ub[j]])
                for tb in range(4):
                    cols = slice(tb * 512, (tb + 1) * 512)
                    for kc in range(3):
                        P.op("pe", lambda e, j=j, kc=kc, cols=cols: e.matmul(ps[:, 7, :], uqb[j][:, kc, :], cqT[:, kc, cols],
                                                                             start=(kc == 0), stop=(kc == 2)),
                             reads=[t_ub[j], t_cq[tb]], writes=[ps_t[7]])
                    P.op("act", lambda e, QT=QT, cols=cols: e.activation(out=QT[64:128, cols], in_=ps[64:128, 7, :], func=AF.Identity,
                                                                         scale=sc_mla),
                         reads=[ps_t[7]], writes=[tq])
                    rope_rot(7, QT, cols, sc_mla, [], [tq])
                    for kc in range(2):
                        P.op("pe", lambda e, j=j, kc=kc, cols=cols: e.matmul(ps[:, 6, :], ukb[j][:, kc, :], ckvT[:, kc, cols],
                                                                             start=(kc == 0), stop=(kc == 1)),
                             reads=[t_ub[j], t_ckv[tb]], writes=[ps_t[6]])
                    P.op("act", lambda e, KT=KT, cols=cols: e.copy(out=KT[64:128, cols], in_=ps[64:128, 6, :]),
                         reads=[ps_t[6]], writes=[tk])
                for Q in range(4):
                    specs = []
                    for kt in range(4 * Q + 4):
                        jj = kt - 4 * Q
                        c0 = max(0, jj) * 128
                        ex = []
                        if jj >= 0:
                            ex.append((ident[:], Md[:], c0, c0 + 128, [t_ident, t_cst]))
                        specs.append(dict(c0=c0, c1=512, lhsT=KT[:, kt * 128:(kt + 1) * 128], rhs=QT[:, Q * 512 + c0:(Q + 1) * 512],
                                          rd=[tq, tk], extra=ex, v=Vm[:, kt, h, :], vrd=[t_Vm[kt]]))
                    sidx = run_tiles(specs, 3, E, t_E, sidx)
                    rl = fw_[3]
                    act_recip(rl[64:128, :], ps[64:128, 3, :], [ps_t[3]], [t_fw[3]])
                    hp = (h % 2) * 64
                    P.op("dve", lambda e, rl=rl, hp=hp, h=h, Q=Q: e.tensor_tensor(
                        out=oM[hp:hp + 64, h // 2, Q * 512:(Q + 1) * 512], in0=ps[0:64, 3, :], in1=rl[64:128, :], op=ALU.mult),
                        reads=[ps_t[3], t_fw[3]], writes=[t_oM[h // 2][Q]])
            self.dbg_out("dbg_oM", oM, [128, 4, S], BF16, [t for r in t_oM for t in r])

            P.barrier()
            off[0] = base_off
            oN = self.view(alloc(4 * S), [128, 4, S], BF16)
            NQ = self.view(alloc(4 * S), [128, 4, S], BF16)
            t_NQ = [[T() for _ in range(4)] for _ in range(4)]
            fm = [self.view(alloc(S), [128, S], BF16) for _ in range(4)]
            t_fm = [[T() for _ in range(4)] for _ in range(4)]
            srcK, srcV, kselT, kwinT = fm
            ghl = self.view(alloc(2 * S), [128, 2, S], BF16)
            t_ghl = [T() for _ in range(4)]
            Vn = self.view(alloc(NT * 512), [128, NT, 4, 128], BF16)
            t_Vn = [T() for _ in range(NT)]
            maskT = self.view(alloc(S), [128, S], BF16)
            t_mask = [T() for _ in range(NT)]
            wvn_off = alloc(2048)
            wvn = self.view(wvn_off, [128, 8, 256], BF16)
            t_wvn = T()
            assert wvn_off + 4096 <= AR_EL
            P.op("pool", lambda e: e.memset(Vn.rearrange("p t h d -> p (t h) d")[:, :, 64:128], 1.0), writes=t_Vn)
            self.dma("sp", wvn, evb["ev_wvn"].rearrange("p (k n) -> p k n", k=8), reads=rd_w, writes=[t_wvn])
            for t in range(NT):
                for kc in range(8):
                    P.op("pe", lambda e, t=t, kc=kc: e.matmul(ps[:, 6, 0:256], hT[:, kc, t * 128:(t + 1) * 128], wvn[:, kc, :],
                                                              start=(kc == 0), stop=(kc == 7)),
                         reads=[t_hT[t], t_wvn], writes=[ps_t[6]])
                P.op("act", lambda e, t=t: e.copy(out=Vn[:, t, :, 0:64], in_=ps[:, 6, 0:256].rearrange("p (h d) -> p h d", h=4)),
                     reads=[ps_t[6]], writes=[t_Vn[t]])
            for ch in range(6, 15):
                self.dma("sp", wc[:, 0], evb["ev_wc"][ch].rearrange("p (k j) -> p k j", k=8), reads=rd_w, writes=[t_wcb[0]])
                for tb in range(4):
                    cols = slice(tb * 512, (tb + 1) * 512)
                    bank = 6 + (tb % 2)
                    for kc in range(8):
                        P.op("pe", lambda e, kc=kc, cols=cols, bank=bank: e.matmul(ps[:, bank, :], wc[:, 0, kc, :], hT[:, kc, cols],
                                                                                   start=(kc == 0), stop=(kc == 7)),
                             reads=[t_wcb[0]] + t_hT[tb * 4:tb * 4 + 4], writes=[ps_t[bank]])
                    if ch <= 9:
                        P.op("act", lambda e, ch=ch, cols=cols, bank=bank: e.activation(out=NQ[:, ch - 6, cols], in_=ps[:, bank, :],
                                                                                       func=AF.Identity, scale=0.125),
                             reads=[ps_t[bank]], writes=[t_NQ[ch - 6][tb]])
                    elif ch <= 13:
                        P.op("act", lambda e, ch=ch, cols=cols, bank=bank: e.copy(out=fm[ch - 10][:, cols], in_=ps[:, bank, :]),
                             reads=[ps_t[bank]], writes=[t_fm[ch - 10][tb]])
                    else:
                        gt = fw_[0]
                        P.op("act", lambda e, gt=gt, bank=bank: e.activation(out=gt[0:24, :], in_=ps[0:24, bank, :], func=AF.Sigmoid),
                             reads=[ps_t[bank]], writes=[t_fw[0]])
                        P.op("dve", lambda e, gt=gt, cols=cols: e.tensor_copy(out=ghl[0:24, 0, cols], in_=gt[0:24, :]),
                             reads=[t_fw[0]], writes=[t_ghl[tb]])
                        P.op("dve", lambda e, gt=gt, cols=cols: e.tensor_tensor(out=ghl[0:24, 1, cols], in0=gt[0:24, :], in1=ghl[0:24, 0, cols],
                                                                                op=ALU.subtract),
                             reads=[t_fw[0], t_ghl[tb]], writes=[t_ghl[tb]])

            P.barrier()
            hx = hT.rearrange("p k s -> p (k s)")
            hoff = [0]

            def hview(n, shape, dt):
                o = hoff[0]
                hoff[0] = o + n + (n % 2)
                assert hoff[0] <= 8 * S
                if dt == BF16:
                    ap = hx[:, o:o + n]
                else:
                    ap = hx[:, o:o + n].bitcast(F32)
                if len(shape) == 3:
                    ap = ap.rearrange("p (a b) -> p a b", a=shape[1])
                elif len(shape) == 4:
                    ap = ap.rearrange("p (a b c) -> p a b c", a=shape[1], b=shape[2])
                return ap
            t_h = lambda: T()
            w1sb = hview(32 * 256, [128, 32, 256], BF16); t_w1 = T()
            after_w1 = hoff[0]
            hoff[0] = 0
            efp = hview(2 * 8 * 128, [128, 8, 128], F32); t_e = T()
            pbf = hview(8 * 128, [128, 8, 128], BF16); t_p = T()
            pT = hview(4 * 8 * 128, [128, 4, 8, 128], BF16); t_pT = [T() for _ in range(4)]
            pT2 = [pT, self.view(wvn_off, [128, 4, 8, 128], BF16)]
            t_pT2 = [t_pT, [T() for _ in range(4)]]
            assert hoff[0] <= after_w1
            hoff[0] = after_w1
            peT = hview(32, [128, 32], BF16); t_pe = T()
            w2k = hview(512, [128, 2, 2, 128], BF16); t_w2 = T()
            w2v = hview(128, [128, 2, 64], BF16)
            hid = hview(256, [128, 2, 128], BF16); t_hid = [T(), T()]
            xa = hview(256, [128, 128], F32); t_xa = T()
            xb = hview(256, [128, 128], F32); t_xb = T()
            kcT = hview(128, [128, 128], BF16); t_kc = T()
            vc = hview(128, [128, 2, 64], BF16); t_vc = T()
            pebias = hview(4, [128, 2], F32); t_peb = T()
            ssum = hview(32, [128, 16], F32); t_ss = T()
            scr = hview(2 * 64, [128, 64], F32); t_scr = T()
            m8 = hview(2 * 16, [128, 16], F32); t_m8 = T()
            mbb = hview(64, [128, 64], BF16); t_mb = T()
            EXP = hview(16 * 128, [128, 16, 128], BF16)
            selb = hview(24 * 128, [128, 24, 128], BF16)
            scab = hview(2 * NT * 32, [128, 2, NT, 32], BF16)
            t_c3 = T()
            self.dma("sp", EXP[0:64], cst_b["expmat"].rearrange("p (a b) -> p a b", a=16), reads=[t_cstb], writes=[t_c3])
            self.dma("sp", selb[0:24], cst_b["selb"].rearrange("p (a b) -> p a b", a=24), reads=[t_cstb], writes=[t_c3])
            for a_ in range(2):
                self.dma("sp", scab[:, a_], cst_b["scab"][a_].rearrange("p (a b) -> p a b", a=NT), reads=[t_cstb], writes=[t_c3])

            self.dma("sp", w2k, evb["ev_w2k"].rearrange("p (g c j) -> p g c j", g=2, c=2), reads=rd_w, writes=[t_w2])
            self.dma("sp", w2v, evb["ev_w2v"].rearrange("p (c j) -> p c j", c=2), reads=rd_w, writes=[t_w2])
            for kv in range(2):
                self.dma("sp", w1sb, evb["ev_w1"][kv].rearrange("p (l c) -> p l c", l=32), reads=rd_w, writes=[t_w1])
                self.dma("sp", peT[0:64, :], evb["ev_peT"][kv], reads=rd_w, writes=[t_pe])
                for cc in range(2):
                    for l_ in range(32):
                        P.op("pe", lambda e, cc=cc, l_=l_: e.matmul(ps[:, 7, 0:1], w1sb[0:64, l_, cc * 128:(cc + 1) * 128], peT[0:64, l_:l_ + 1],
                                                                    start=(l_ == 0), stop=(l_ == 31)),
                             reads=[t_w1, t_pe], writes=[ps_t[7]])
                    P.op("dve", lambda e, cc=cc: e.tensor_copy(out=pebias[:, cc:cc + 1], in_=ps[:, 7, 0:1]), reads=[ps_t[7]], writes=[t_peb])
                src = srcK if kv == 0 else srcV
                tsrc = t_fm[kv]
                for g in range(2):
                    gp = slice(g * 64, (g + 1) * 64)
                    for cc in range(2):
                        for l_ in range(32):
                            P.op("pe", lambda e, gp=gp, cc=cc, l_=l_, src=src: e.matmul(
                                ps[:, 6, 0:127], w1sb[gp, l_, cc * 128:(cc + 1) * 128], src[gp, l_:l_ + 2017:16],
                                start=(l_ == 0), stop=(l_ == 31)), reads=[t_w1] + tsrc, writes=[ps_t[6]])
                        P.op("act", lambda e, cc=cc: e.activation(out=xa[:, 0:127], in_=ps[:, 6, 0:127], func=AF.Identity,
                                                                  bias=pebias[:, cc:cc + 1], scale=1.0),
                             reads=[ps_t[6], t_peb], writes=[t_xa])
                        P.op("act", lambda e: e.activation(out=xb[:, 0:127], in_=xa[:, 0:127], func=AF.Square), reads=[t_xa], writes=[t_xb])
                        P.op("dve", lambda e: e.tensor_scalar(out=xb[:, 0:127], in0=xb[:, 0:127], scalar1=0.044715, scalar2=1.0,
                                                              op0=ALU.mult, op1=ALU.add), reads=[t_xb], writes=[t_xb])
                        P.op("dve", lambda e: e.tensor_tensor(out=xb[:, 0:127], in0=xb[:, 0:127], in1=xa[:, 0:127], op=ALU.mult),
                             reads=[t_xb, t_xa], writes=[t_xb])
                        P.op("act", lambda e: e.activation(out=xb[:, 0:127], in_=xb[:, 0:127], func=AF.Sigmoid, scale=1.5957691216057308),
                             reads=[t_xb], writes=[t_xb])
                        P.op("dve", lambda e, cc=cc: e.tensor_tensor(out=hid[:, cc, 0:127], in0=xa[:, 0:127], in1=xb[:, 0:127], op=ALU.mult),
                             reads=[t_xa, t_xb], writes=[t_hid[cc]])
                    if kv == 0:
                        for cc in range(2):
                            P.op("pe", lambda e, g=g, cc=cc: e.matmul(ps[:, 7, 0:127], w2k[:, g, cc, :], hid[:, cc, 0:127],
                                                                      start=(cc == 0), stop=(cc == 1)),
                                 reads=[t_w2] + t_hid, writes=[ps_t[7]])
                        P.op("dve", lambda e, gp=gp: e.tensor_copy(out=kcT[gp, 0:127], in_=ps[gp, 7, 0:127]), reads=[ps_t[7]], writes=[t_kc])
                    else:
                        for cc in range(2):
                            P.op("pe", lambda e, cc=cc: e.matmul(ps[0:127, 7, 0:64], hid[:, cc, 0:127], w2v[:, cc, :],
                                                                 start=(cc == 0), stop=(cc == 1)),
                                 reads=[t_w2] + t_hid, writes=[ps_t[7]])
                        P.op("dve", lambda e, g=g: e.tensor_copy(out=vc[0:127, g, :], in_=ps[0:127, 7, 0:64]), reads=[ps_t[7]], writes=[t_vc])
            P.barrier()
            self.dbg_out("dbg_kcT", kcT, [128, 128], BF16, [t_kc])
            self.dbg_out("dbg_vc", vc, [128, 2, 64], BF16, [t_vc])

            def cmp_parts(Qc, tt):
                qt = 4 * Qc + tt
                qc = slice(qt * 128, (qt + 1) * 128)
                pTq = pT2[Qc % 2]
                tpT = t_pT2[Qc % 2][tt]

                def part_a():
                    for h in range(8):
                        g, jn = h // 4, h % 4
                        gp = slice(g * 64, (g + 1) * 64)
                        bank = h // 4
                        cs = (h % 4) * 127
                        P.op("pe", lambda e, gp=gp, jn=jn, bank=bank, cs=cs: e.matmul(
                            ps[:, bank, cs:cs + 127], NQ[gp, jn, qc], kcT[gp, 0:127], start=True, stop=False),
                            reads=[t_NQ[jn][Qc], t_kc], writes=[ps_t[bank]])
                        P.op("pe", lambda e, h=h, bank=bank, cs=cs: e.matmul(
                            ps[:, bank, cs:cs + 127], ident[:], Bc[:, h, 120 - 8 * qt:247 - 8 * qt], start=False, stop=True),
                            reads=[t_ident, t_B], writes=[ps_t[bank]])
                    for h in range(8):
                        bank = h // 4
                        cs = (h % 4) * 127
                        P.op("act", lambda e, h=h, bank=bank, cs=cs: e.activation(out=efp[:, h, 0:127], in_=ps[:, bank, cs:cs + 127], func=AF.Exp,
                                                                                   accum_out=ssum[:, h:h + 1]),
                             reads=[ps_t[bank]], writes=[t_e, t_ss])
                    P.op("dve", lambda e: e.tensor_scalar_add(out=ssum[:, 8:16], in0=ssum[:, 0:8], scalar1=1e-30), reads=[t_ss], writes=[t_ss])
                    P.op("dve", lambda e: e.reciprocal(out=ssum[:, 8:16], in_=ssum[:, 8:16]), reads=[t_ss], writes=[t_ss])
                    for h in range(8):
                        P.op("dve", lambda e, h=h: e.tensor_scalar_mul(out=pbf[:, h, 0:127], in0=efp[:, h, 0:127], scalar1=ssum[:, 8 + h:9 + h]),
                             reads=[t_e, t_ss], writes=[t_p])

                def part_b():
                    psT = ps[:, 5, :].bitcast(BF16)
                    for h in range(8):
                        P.op("pe", lambda e, h=h: e.transpose(psT[0:127, h * 128:(h + 1) * 128], pbf[:, h, 0:127], ident[:]),
                             reads=[t_p, t_ident], writes=[ps_t[5]])
                    P.op("act", lambda e: e.copy(out=pTq[0:127, tt, :, :], in_=psT[0:127, :].rearrange("p (h q) -> p h q", h=8)),
                         reads=[ps_t[5]], writes=[tpT])

                def part_c():
                    for h in range(8):
                        g = h // 4
                        P.op("pe", lambda e, h=h, g=g: e.matmul(ps[:, 7, g * 32:(g + 1) * 32], pTq[0:127, tt, h, :], ovl[:, :],
                                                                start=(h % 4 == 0), stop=(h % 4 == 3)),
                             reads=[tpT, t_cst], writes=[ps_t[7]])
                    for g in range(2):
                        P.op("dve", lambda e, g=g: e.tensor_tensor(out=scr[:, g * 32:(g + 1) * 32], in0=ps[:, 7, g * 32:(g + 1) * 32],
                                                                   in1=scab[:, 0, qt, :], op=ALU.mult),
                             reads=[ps_t[7], t_c3], writes=[t_scr])
                        P.op("dve", lambda e, g=g: e.tensor_tensor(out=scr[:, g * 32:(g + 1) * 32], in0=scr[:, g * 32:(g + 1) * 32],
                                                                   in1=scab[:, 1, qt, :], op=ALU.add),
                             reads=[t_scr, t_c3], writes=[t_scr])
                        P.op("dve", lambda e, g=g: e.max(out=m8[:, g * 8:(g + 1) * 8], in_=scr[:, g * 32:(g + 1) * 32]),
                             reads=[t_scr], writes=[t_m8])
                        P.op("dve", lambda e, g=g: e.tensor_scalar(out=scr[:, g * 32:(g + 1) * 32], in0=scr[:, g * 32:(g + 1) * 32],
                                                                   scalar1=m8[:, g * 8 + 7:g * 8 + 8], scalar2=1.0,
                                                                   op0=ALU.is_ge, op1=ALU.subtract),
                             reads=[t_scr, t_m8], writes=[t_scr])
                    P.op("dve", lambda e: e.tensor_scalar_mul(out=mbb[:, :], in0=scr[:, :], scalar1=-NEG), reads=[t_scr], writes=[t_mb])

                def part_d():
                    psM = ps[:, 7, 256:512].bitcast(BF16)
                    P.op("pe", lambda e: e.transpose(psM[0:64, 0:128], mbb[:, :], ident[:]), reads=[t_mb, t_ident], writes=[ps_t[7]])
                    P.op("act", lambda e: e.copy(out=maskT[0:64, qc], in_=psM[0:64, 0:128]), reads=[ps_t[7]], writes=[t_mask[qt]])
                return [part_a, part_b, part_c, part_d]

            for Q in range(4):
                if Q == 0:
                    for tt in range(4):
                        for part in cmp_parts(0, tt):
                            part()
                nparts = [cmp_parts(Q + 1, tt) for tt in range(4)] if Q < 3 else None
                for h in range(8):
                    g, jn = h // 4, h % 4
                    gp = slice(g * 64, (g + 1) * 64)
                    mp = slice(g * 32, (g + 1) * 32)
                    qrd = [t_NQ[jn][Q]]
                    specs = []
                    for kt in range(4 * Q + 4):
                        jj = kt - 4 * Q
                        c0 = max(0, jj) * 128
                        ex = [(EXP[mp, kt, :], maskT[mp, Q * 512 + c0:(Q + 1) * 512], c0, 512, [t_c3] + t_mask[4 * Q:4 * Q + 4])]
                        if jj == -1:
                            ex.append((ident[:], Bsb[:, h, 1, :], 0, 128, [t_ident, t_B]))
                        elif jj == 3:
                            ex.append((ident[:], Bsb[:, h, 0, :], 384, 512, [t_ident, t_B]))
                        elif jj >= 0:
                            ex.append((ident[:], Bsb[:, h, 0:2, :].rearrange("p t q -> p (t q)"), c0, c0 + 256, [t_ident, t_B]))
                        specs.append(dict(c0=c0, c1=512, lhsT=kselT[gp, kt * 128:(kt + 1) * 128], rhs=NQ[gp, jn, Q * 512 + c0:(Q + 1) * 512],
                                          rd=qrd + t_fm[2], extra=ex, v=Vn[:, kt, 0 + g, :], vrd=[t_Vn[kt]]))
                    sidx = run_tiles(specs, 3, E, t_E, sidx)
                    if Q < 3:
                        nparts[h // 2][(h % 2) * 2]()
                    specs = []
                    for kt in range(max(0, 4 * Q - 2), 4 * Q + 4):
                        jj = kt - 4 * Q
                        lo = max(0, jj)
                        hi = min(3, jj + 2)
                        c0, c1 = lo * 128, (hi + 1) * 128
                        t0, t1 = lo - jj, hi - jj
                        ex = [(ident[:], Bsb[:, h, t0:t1 + 1, :].rearrange("p t q -> p (t q)"), c0, c1, [t_ident, t_B])]
                        specs.append(dict(c0=c0, c1=c1, lhsT=kwinT[gp, kt * 128:(kt + 1) * 128], rhs=NQ[gp, jn, Q * 512 + c0:Q * 512 + c1],
                                          rd=qrd + t_fm[3], extra=ex, v=Vn[:, kt, 2 + g, :], vrd=[t_Vn[kt]]))
                    sidx = run_tiles(specs, 4, E, t_E, sidx)
                    if Q < 3:
                        nparts[h // 2][(h % 2) * 2 + 1]()
                    for tt in range(4):
                        P.op("pe", lambda e, g=g, tt=tt, h=h, pTq=pT2[Q % 2]: e.matmul(ps[0:64, 6, tt * 128:(tt + 1) * 128], vc[0:127, g, :],
                                                                                        pTq[0:127, tt, h, :], start=True, stop=True),
                             reads=[t_vc, t_pT2[Q % 2][tt]], writes=[ps_t[6]])
                    acc = fw_[0]
                    hp = (h % 2) * 64
                    brs = [(0, 6, False), (1, 3, True), (2, 4, True)]
                    if self.nsa_only is not None:
                        brs = [brs[self.nsa_only]]
                    for bi_, (br, src_bank, norm) in enumerate(brs):
                        r = 3 * h + br
                        first_, last_ = (bi_ == 0), (bi_ == len(brs) - 1)
                        for part in range(2):
                            gbk = 5 if bi_ % 2 == 0 else 7
                            P.op("pe", lambda e, r=r, part=part, Q=Q, gbk=gbk: e.matmul(ps[:, gbk, :], selb[0:24, r, :],
                                                                                   ghl[0:24, part, Q * 512:(Q + 1) * 512],
                                                                                   start=(part == 0), stop=(part == 1)),
                                 reads=[t_c3, t_ghl[Q]], writes=[ps_t[gbk]])
                        f = fw_[1]
                        if norm:
                            act_recip(f[64:128, :], ps[64:128, src_bank, :], [ps_t[src_bank]], [t_fw[1]])
                            P.op("dve", lambda e, f=f, gbk=gbk: e.tensor_tensor(out=f[64:128, :], in0=ps[64:128, gbk, :], in1=f[64:128, :], op=ALU.mult),
                                 reads=[ps_t[gbk], t_fw[1]], writes=[t_fw[1]])
                        else:
                            P.op("dve", lambda e, f=f, gbk=gbk: e.tensor_copy(out=f[64:128, :], in_=ps[64:128, gbk, :]), reads=[ps_t[gbk]], writes=[t_fw[1]])
                        dst = oN[hp:hp + 64, h // 2, Q * 512:(Q + 1) * 512] if last_ else acc[0:64, :]
                        tdst = t_oN[h // 2][Q] if last_ else t_fw[0]
                        if first_:
                            P.op("dve", lambda e, f=f, src_bank=src_bank, dst=dst: e.tensor_tensor(out=dst, in0=ps[0:64, src_bank, :], in1=f[64:128, :],
                                                                                                  op=ALU.mult),
                                 reads=[ps_t[src_bank], t_fw[1]], writes=[tdst])
                        else:
                            tmp_ = fw_[2]
                            P.op("dve", lambda e, f=f, src_bank=src_bank, tmp_=tmp_: e.tensor_tensor(out=tmp_[0:64, :], in0=ps[0:64, src_bank, :],
                                                                                                    in1=f[64:128, :], op=ALU.mult),
                                 reads=[ps_t[src_bank], t_fw[1]], writes=[t_fw[2]])
                            P.op("pool", lambda e, tmp_=tmp_, dst=dst: e.tensor_tensor(out=dst, in0=acc[0:64, :], in1=tmp_[0:64, :], op=ALU.add),
                                 reads=[t_fw[0], t_fw[2]], writes=[tdst])
            self.dbg_out("dbg_oN", oN, [128, 4, S], BF16, [t for r in t_oN for t in r])
            self.dbg_out("dbg_maskT", maskT, [128, S], BF16, t_mask)

            P.barrier()
            off[0] = base_off + 4 * S
            wo = self.view(alloc(8 * D), [128, 8, D], BF16)
            t_wo = T()
            self.dma("sp", wo, evb["ev_wo"].rearrange("p (k n) -> p k n", k=8), reads=rd_w, writes=[t_wo])
            def mm_wo(t):
                yb = 3 if t % 2 == 0 else 5
                for dh in range(2):
                    for c in range(8):
                        src_ = oM if c < 4 else oN
                        tsrc_ = (t_oM if c < 4 else t_oN)[c % 4][t // 4]
                        P.op("pe", lambda e, t=t, dh=dh, c=c, src_=src_, yb=yb: e.matmul(
                            ps[:, yb + dh, :], src_[:, c % 4, t * 128:(t + 1) * 128], wo[:, c, dh * 512:(dh + 1) * 512],
                            start=(c == 0), stop=(c == 7)), reads=[tsrc_, t_wo], writes=[ps_t[yb + dh]])
            ep_pipeline(list(range(NT)), mm_wo, lambda t: 3 if t % 2 == 0 else 5, b, l, 0, xsrc, 7)

        for b in range(nseq):
            self.have_hT = False
            for pi, (l, sub) in enumerate(plan):
                xsrc = x_in if pi == 0 else out
                self.nxt = plan[pi + 1] if (pi + 1 < len(plan) and self.fuse) else None
                if sub == 1:
                    ffn(b, l, xsrc)
                elif l % 2 == 1:
                    diff_attn(b, l, xsrc)
                else:
                    even_attn(b, l, xsrc)
                self.have_hT = self.nxt is not None
        P.emit()


def host_prep(inputs, core, nseq, seq0=None):
    b0 = core * nseq if seq0 is None else seq0
    m = {}
    m["x"] = np.ascontiguousarray(inputs["x"][b0:b0 + nseq])
    m["cT"] = np.ascontiguousarray(inputs["c"][b0:b0 + nseq].reshape(nseq, 8, 128).transpose(2, 1, 0))
    for k in ["ada_w", "ada_b", "ln_g", "ln_b", "rel_bias"]:
        m[k] = np.ascontiguousarray(inputs[k])
    up = inputs["ffn_w_up"].reshape(DEPTH, 8, 128, NFC, 128)
    m["wu"] = np.ascontiguousarray(up.transpose(0, 3, 2, 1, 4))
    gt = inputs["ffn_w_gate"].reshape(DEPTH, 8, 128, NFC, 128)
    m["wg"] = np.ascontiguousarray(gt.transpose(0, 3, 2, 1, 4))
    dn = inputs["ffn_w_down"].reshape(DEPTH, NFC, 128, D)
    m["wd"] = np.ascontiguousarray(dn.transpose(0, 2, 1, 3))
    cw = np.concatenate([inputs["ffn_conv_w"].transpose(0, 2, 1), inputs["ffn_conv_b"][:, :, None]], axis=2)
    m["convp"] = np.ascontiguousarray(cw.astype(np.float32))
    w = inputs["od_w_in"]
    qk = w[:, :, :2048].reshape(2, 8, 128, 16, 128)
    m["od_wqk"] = np.ascontiguousarray(qk.transpose(0, 3, 2, 1, 4))
    v = w[:, :, 2048:].reshape(2, 8, 128, D)
    m["od_wv"] = np.ascontiguousarray(v.transpose(0, 2, 1, 3))
    wo = inputs["od_w_o"].reshape(2, 8, 128, D)
    m["od_wo"] = np.ascontiguousarray(wo.transpose(0, 2, 1, 3))
    m["diff_lambda"] = np.ascontiguousarray(inputs["diff_lambda"].reshape(2, 256))
    m["diff_subln"] = np.ascontiguousarray(inputs["diff_subln"].reshape(2, 128, 1))
    ew = inputs["ev_w_in"]
    def fm_chunk(W, cols, nk):
        Z = np.zeros((W.shape[0], W.shape[1], 128), np.float32)
        Z[:, :, :len(cols)] = W[:, :, cols]
        return Z.reshape(W.shape[0], nk, 128, 128).transpose(0, 2, 1, 3).reshape(W.shape[0], 128, nk * 128)
    sw = [(r + 16) % 32 for r in range(32)]
    lists = [list(range(c * 128, (c + 1) * 128)) for c in range(3)]
    lists += [list(range(384 + c * 128, 384 + (c + 1) * 128)) for c in range(2)]
    lists += [[640 + r for r in range(32)] + [640 + r for r in sw]]
    for c in range(4):
        lists += [[672 + c * 64 + d for d in range(64)] + [672 + (c + 4) * 64 + d for d in range(64)]]
    lists += [[1184 + d for d in range(0, 128)], [1184 + d for d in range(128, 256)],
              [1184 + d for d in range(256, 384)], [1184 + d for d in range(512, 640)]]
    lists += [list(range(1952, 1976))]
    m["ev_wc"] = np.ascontiguousarray(np.stack([fm_chunk(ew, L, 8) for L in lists], axis=1))
    vcols = [1184 + d for d in range(384, 512)] + [1184 + d for d in range(640, 768)]
    m["ev_wvn"] = np.ascontiguousarray(ew[:, :, vcols].reshape(2, 8, 128, 256).transpose(0, 2, 1, 3).reshape(2, 128, 8 * 256))
    uq = inputs["mla_w_uq"]
    m["ev_uq"] = np.ascontiguousarray(np.stack([fm_chunk(uq, [h * 96 + 64 + r for r in range(32)] + [h * 96 + 64 + r for r in sw]
                                                          + [h * 96 + d for d in range(64)], 3) for h in range(8)], axis=1))
    ukv = inputs["mla_w_ukv"]
    def kchunk(h):
        Z = np.zeros((2, 256, 128), np.float32)
        Z[:, :, 64:] = ukv[:, :, h * 128:h * 128 + 64]
        return Z.reshape(2, 2, 128, 128).transpose(0, 2, 1, 3).reshape(2, 128, 256)
    m["ev_ukvk"] = np.ascontiguousarray(np.stack([kchunk(h) for h in range(8)], axis=1))
    vc_ = [h * 128 + 64 + d for h in range(8) for d in range(64)]
    m["ev_ukvv"] = np.ascontiguousarray(ukv[:, :, vc_].reshape(2, 2, 128, 512).transpose(0, 2, 1, 3).reshape(2, 128, 1024))
    m["ev_qn"] = np.ascontiguousarray(inputs["mla_q_norm"].reshape(2, 3, 128).transpose(0, 2, 1))
    m["ev_kvn"] = np.ascontiguousarray(inputs["mla_kv_norm"].reshape(2, 2, 128).transpose(0, 2, 1))
    w1 = inputs["nsa_cmp_w1"].reshape(2, 2, 32, 64, 256).transpose(0, 1, 3, 2, 4)
    m["ev_w1"] = np.ascontiguousarray(np.concatenate([w1, w1], axis=2).reshape(2, 2, 128, 32 * 256))
    w2 = inputs["nsa_cmp_w2"]
    w2k = np.zeros((2, 128, 2, 2, 128), np.float32)
    for g in range(2):
        w2k[:, :, g, :, g * 64:(g + 1) * 64] = w2[:, 0].reshape(2, 2, 128, 64).transpose(0, 2, 1, 3)
    m["ev_w2k"] = np.ascontiguousarray(w2k.reshape(2, 128, 512))
    m["ev_w2v"] = np.ascontiguousarray(w2[:, 1].reshape(2, 2, 128, 64).transpose(0, 2, 1, 3).reshape(2, 128, 128))
    m["ev_peT"] = np.ascontiguousarray(inputs["nsa_cmp_pe"].transpose(0, 1, 3, 2))
    m["ev_wo"] = np.ascontiguousarray(inputs["ev_w_o"].reshape(2, 8, 128, D).transpose(0, 2, 1, 3).reshape(2, 128, 8 * D))
    m.update(static_consts())
    return m


_SC = {}


def static_consts():
    if _SC:
        return _SC
    inv = (1.0 / (np.float32(10000.0) ** (np.arange(0, 32, 2, dtype=np.float32) / np.float32(32)))).astype(np.float32)
    ang = (np.arange(S, dtype=np.float32)[:, None] * inv[None, :]).astype(np.float32)
    cos, sin = np.cos(ang).astype(np.float32), np.sin(ang).astype(np.float32)
    rt = np.zeros((64, S), np.float32)
    for r in range(32):
        rt[r] = cos[:, r % 16]
        rt[32 + r] = -sin[:, r] if r < 16 else sin[:, r - 16]
    _SC["ropeT"] = rt
    kk, qq = np.meshgrid(np.arange(128), np.arange(128), indexing="ij")
    _SC["cmask"] = np.stack([np.where(qq >= kk, 0.0, NEG), np.where(qq < kk, 0.0, NEG)]).astype(np.float32)
    ex = np.zeros((2, 32, 16, 128), np.float32)
    for kt in range(16):
        ex[:, 2 * kt, kt, 0:64] = 1.0
        ex[:, 2 * kt + 1, kt, 64:128] = 1.0
    _SC["expmat"] = ex.reshape(64, 16 * 128)
    starts = np.arange(127) * 16
    jb = np.arange(32)
    _SC["ovl"] = ((starts[:, None] < (jb[None, :] + 1) * 64) & (starts[:, None] + 32 > jb[None, :] * 64)).astype(np.float32)
    t = np.arange(S)
    cur = t // 64
    forced = (jb[None, :] == 0) | (jb[None, :] == cur[:, None]) | (jb[None, :] == cur[:, None] - 1)
    causal = jb[None, :] * 64 <= t[:, None]
    A = (causal & ~forced).astype(np.float32)
    Bf = np.where(~causal, -1.0, np.where(forced, 1e4, 0.0)).astype(np.float32)
    sc = np.stack([A, Bf]).reshape(2, NT, 128, 32).transpose(0, 2, 1, 3).reshape(2, 128, NT * 32)
    _SC["scab"] = np.ascontiguousarray(sc)
    sb = np.zeros((24, 24, 128), np.float32)
    for r in range(24):
        sb[r, r, :] = 1.0
    _SC["selb"] = sb.reshape(24, 24 * 128)
    _SC["onehot"] = onehot_consts()
    _SC["ident"] = np.eye(128, dtype=np.float32)
    return _SC


FULL_PLAN = [(l, s) for l in range(DEPTH) for s in range(2)]
NCORES = 8
_CACHE = {}


def kernel(**inputs):
    inputs = {k: np.asarray(v) for k, v in inputs.items()}
    B = inputs["x"].shape[0]
    nseq = B // NCORES
    kb = K(nseq, FULL_PLAN)
    in_maps = []
    shared = host_prep(inputs, 0, nseq)
    shared = {k: v for k, v in shared.items() if k in kb.dram_in}
    for core in range(NCORES):
        m = dict(shared)
        b0 = core * nseq
        m["x"] = np.ascontiguousarray(inputs["x"][b0:b0 + nseq])
        m["cT"] = np.ascontiguousarray(inputs["c"][b0:b0 + nseq].reshape(nseq, 8, 128).transpose(2, 1, 0))
        in_maps.append(m)
    res = run_bass_kernel_spmd(kb.nc, in_maps, core_ids=list(range(NCORES)))
    outs = [np.asarray(r["out"]) for r in res.results]
    return np.concatenate(outs, axis=0).astype(np.float32)
```

```python
import numpy as np
import math
import contextlib
from concourse.bass_utils import run_bass_kernel_spmd
import concourse.bass as bass
import concourse.mybir as mybir

F32 = mybir.dt.float32
BF16 = mybir.dt.bfloat16
ALU = mybir.AluOpType
AF = mybir.ActivationFunctionType
AX = mybir.AxisListType

SEM_LIM = 30000
NSLOT = 12


class Tok:
    __slots__ = ("w", "r", "const")

    def __init__(self, const=False):
        self.w = None
        self.r = []
        self.const = const


class Ins:
    __slots__ = ("eng", "fn", "deps", "dma", "pos", "sig", "slot", "slotcnt", "signo")


class Prog:
    ENGS = ["pe", "act", "dve", "pool", "sp"]

    def __init__(self, nc):
        self.nc = nc
        self.instrs = []
        self.ndma = {e: 0 for e in self.ENGS}
        self.slot_last = {}
        self.stack = contextlib.ExitStack()
        self.last = {}
        self.pending_bar = {}
        self.bar_skip = ()

    def sbuf(self, name, shape, dt):
        return self.stack.enter_context(self.nc.sbuf_tensor(name, list(shape), dt))

    def psum(self, name, shape, dt):
        return self.stack.enter_context(self.nc.psum_tensor(name, list(shape), dt))

    def op(self, eng, fn, reads=(), writes=(), dma=False):
        ins = Ins()
        ins.eng = eng
        ins.fn = fn
        ins.dma = dma
        ins.sig = False
        ins.signo = 0
        deps = set()
        for t in reads:
            if t.w is not None:
                deps.add(t.w)
        for t in writes:
            if t.w is not None:
                deps.add(t.w)
            deps.update(t.r)
        for t in reads:
            if not t.const:
                t.r.append(ins)
        for t in writes:
            t.w = ins
            t.r = []
        if dma:
            n = self.ndma[eng]
            self.ndma[eng] = n + 1
            ins.slot = n % NSLOT
            ins.slotcnt = n // NSLOT + 1
            prev = self.slot_last.get((eng, ins.slot))
            if prev is not None:
                deps.add(prev)
            self.slot_last[(eng, ins.slot)] = ins
        pb = self.pending_bar.pop(eng, None)
        if pb:
            deps.update(pb)
        deps.discard(ins)
        ins.deps = deps
        self.instrs.append(ins)
        self.last[eng] = ins
        return ins

    def barrier(self):
        bar = list(self.last.values()) + [v for (q_, s_), v in self.slot_last.items() if q_ not in self.bar_skip]
        for e in self.ENGS:
            self.pending_bar[e] = list(bar)

    def emit(self, final_waits=()):
        nc = self.nc
        per = {e: [] for e in self.ENGS}
        for ins in self.instrs:
            ins.pos = len(per[ins.eng])
            per[ins.eng].append(ins)

        def needs_wait(ins, d):
            if d.dma:
                return True
            if d.eng == ins.eng:
                if d.eng == "pe":
                    return False
                if ins.dma:
                    return True
                return (ins.pos - d.pos) <= 3
            return True

        for ins in self.instrs:
            best = {}
            for d in ins.deps:
                if not d.dma and needs_wait(ins, d):
                    if d.eng not in best or best[d.eng].pos < d.pos:
                        best[d.eng] = d
            nd = set(d for d in ins.deps if d.dma)
            for d in best.values():
                d.sig = True
                nd.add(d)
            ins.deps = nd
        for e in self.ENGS:
            n = 0
            for ins in per[e]:
                if ins.sig and not ins.dma:
                    n += 1
                    ins.signo = n
        nsig = {e: max([i.signo for i in per[e]] + [0]) for e in self.ENGS}
        esems = {}
        for e in self.ENGS:
            nep = (nsig[e] + SEM_LIM - 1) // SEM_LIM
            esems[e] = [self.stack.enter_context(nc.semaphore(f"s_{e}_{k}")) for k in range(max(nep, 1))]
        ssems = {}
        for e in self.ENGS:
            if self.ndma[e]:
                ssems[e] = [self.stack.enter_context(nc.semaphore(f"d_{e}_{k}")) for k in range(NSLOT)]
                assert (self.ndma[e] // NSLOT + 1) * 16 < 65000, "too many dmas on queue"
        self.stats = {e: [len(per[e]), nsig[e], self.ndma[e]] for e in self.ENGS}
        nwaits = {e: 0 for e in self.ENGS}

        def run(ename, handle):
            waited = {}
            for ins in per[ename]:
                need = {}
                for d in ins.deps:
                    if not needs_wait(ins, d):
                        continue
                    if d.dma:
                        key = ("s", d.eng, d.slot)
                        v = d.slotcnt * 16
                    else:
                        key = ("e", d.eng)
                        v = d.signo
                    if waited.get(key, 0) >= v:
                        continue
                    if need.get(key, 0) < v:
                        need[key] = v
                for key, v in need.items():
                    waited[key] = v
                    nwaits[ename] += 1
                    if key[0] == "s":
                        handle.wait_ge(ssems[key[1]][key[2]], v)
                    else:
                        ep = (v - 1) // SEM_LIM
                        handle.wait_ge(esems[key[1]][ep], (v - 1) % SEM_LIM + 1)
                bi = ins.fn(handle)
                if ins.dma:
                    bi.then_inc(ssems[ename][ins.slot], 16)
                elif ins.sig:
                    ep = (ins.signo - 1) // SEM_LIM
                    bi.then_inc(esems[ename][ep], 1)
            if ename == "sp":
                for e in self.ENGS:
                    if self.ndma[e]:
                        for s in range(NSLOT):
                            last = self.slot_last.get((e, s))
                            if last is not None:
                                handle.wait_ge(ssems[e][s], last.slotcnt * 16)

        with nc.Block() as block:
            @block.tensor
            def _(e):
                run("pe", e)

            @block.scalar
            def _(e):
                run("act", e)

            @block.vector
            def _(e):
                run("dve", e)

            @block.gpsimd
            def _(e):
                run("pool", e)

            @block.sync
            def _(e):
                run("sp", e)
        self.stats["waits"] = nwaits
        self.stack.close()


S = 2048
D = 1024
DEPTH = 4
DFF = 2816
NFC = DFF // 128
NT = S // 128
ALPHA = (2.0 * DEPTH) ** 0.25
LN_EPS = 1e-5
RMS_EPS = 1e-6
NEG = -30000.0
N_ATT = 2 * 128 * 128
N_CMP = 128 * 247
N_OH = N_ATT + N_CMP
AR_EL = 60 * 1024


def T(const=False):
    return Tok(const)


def t5_bucket_np(dist):
    n = np.maximum(dist, 0)
    nf = np.maximum(n, 1).astype(np.float32)
    large = 16 + (np.log(nf / np.float32(16)) / np.float32(math.log(128 / 16)) * np.float32(16)).astype(np.int32)
    large = np.minimum(large, 31)
    return np.where(n < 16, n, large)


def onehot_consts():
    oh = np.zeros((33, N_OH), np.float32)
    tau, k, q = np.meshgrid(np.arange(2), np.arange(128), np.arange(128), indexing="ij")
    dist = (q - k + 128 * tau).reshape(-1)
    col = np.arange(N_ATT)
    b = t5_bucket_np(dist)
    pos = dist >= 0
    np.add.at(oh, (b[pos], col[pos]), 1.0)
    np.add.at(oh, (np.full(pos.sum(), 31), col[pos]), -1.0)
    oh[32, col[~pos]] = 1.0
    p, m = np.meshgrid(np.arange(128), np.arange(247), indexing="ij")
    dist = (p - 16 * (m - 120) - 31).reshape(-1)
    col = N_ATT + np.arange(N_CMP)
    b = t5_bucket_np(dist)
    pos = dist >= 0
    np.add.at(oh, (b[pos], col[pos]), 1.0)
    np.add.at(oh, (np.full(pos.sum(), 31), col[pos]), -1.0)
    oh[32, col[~pos]] = 1.0
    return oh


class K:
    def __init__(self, nseq, plan, dbg=False, nsa_only=None):
        self.nsa_only = nsa_only
        self.fuse = True
        self.have_hT = False
        self.nxt = None
        self.nseq = nseq
        self.plan = plan
        self.dbg = dbg
        nc = self.nc = bass.Bass("TRN2", target_bir_lowering=False)
        self.P = Prog(nc)
        self.P.bar_skip = ("pool",)
        self.dram_in = {}
        self.dbg_names = set()
        self.build()

    def din(self, name, shape, dt=F32):
        t = self.nc.dram_tensor(name, list(shape), dt, kind="ExternalInput")
        self.dram_in[name] = (tuple(shape), dt)
        return t.ap()

    def dscr(self, name, shape, dt):
        return self.nc.dram_tensor(name, list(shape), dt, kind="Internal").ap()

    def dbg_out(self, name, src_ap, shape, dt, reads):
        if not self.dbg or name in self.dbg_names:
            return
        self.dbg_names.add(name)
        t = self.nc.dram_tensor(name, list(shape), dt, kind="ExternalOutput").ap()
        self.dma("sp", t, src_ap, reads=reads, writes=[Tok()])

    def dma(self, q, out, in_, reads=(), writes=()):
        return self.P.op(q, lambda e, o=out, i=in_: e.dma_start(out=o, in_=i), reads, writes, dma=True)

    def view(self, off, shape, dt):
        n = int(np.prod(shape[1:]))
        if dt == BF16:
            ap = self.ar[:, off:off + n]
        else:
            assert off % 2 == 0
            ap = self.ar[:, off:off + 2 * n].bitcast(F32)
        if len(shape) == 3:
            ap = ap.rearrange("p (a b) -> p a b", a=shape[1])
        elif len(shape) == 4:
            ap = ap.rearrange("p (a b c) -> p a b c", a=shape[1], b=shape[2])
        return ap

    def build(self):
        nc, P = self.nc, self.P
        nseq = self.nseq
        plan = self.plan
        layers = sorted(set(l for (l, s) in plan))
        has = lambda l, s: (l, s) in plan
        x_in = self.din("x", [nseq, S, D])
        cT = self.din("cT", [128, 8, nseq])
        out = nc.dram_tensor("out", [nseq, S, D], F32, kind="ExternalOutput").ap()
        ident_d = self.din("ident", [128, 128])
        ada_w = self.din("ada_w", [DEPTH, D, 6 * D])
        ada_b = self.din("ada_b", [DEPTH, 6 * D])
        ln_g = self.din("ln_g", [DEPTH, 2, D])
        ln_b = self.din("ln_b", [DEPTH, 2, D])
        wu_f = self.din("wu", [DEPTH, NFC, 128, 8, 128])
        wg_f = self.din("wg", [DEPTH, NFC, 128, 8, 128])
        wd_f = self.din("wd", [DEPTH, 128, NFC, D])
        cw_d = self.din("convp", [DEPTH, DFF, 4])
        relb = self.din("rel_bias", [32, 8])
        oh_d = self.din("onehot", [33, N_OH])
        odqk_f = self.din("od_wqk", [2, 16, 128, 8, 128])
        odv_f = self.din("od_wv", [2, 128, 8, D])
        odo_f = self.din("od_wo", [2, 128, 8, D])
        dlam = self.din("diff_lambda", [2, 256])
        dsub = self.din("diff_subln", [2, 128, 1])
        ev_specs = {"ev_wc": [15, 128, 8 * 128], "ev_wvn": [128, 8 * 256], "ev_uq": [8, 128, 3 * 128], "ev_ukvk": [8, 128, 2 * 128],
                    "ev_ukvv": [128, 2 * 512], "ev_w1": [2, 128, 32 * 256], "ev_w2k": [128, 2 * 2 * 128], "ev_w2v": [128, 2 * 64],
                    "ev_peT": [2, 64, 32], "ev_wo": [128, 8 * D]}
        ev_f = {k: self.din(k, [2] + v) for k, v in ev_specs.items()}
        ev_b = {k: self.dscr(k + "_b", [2] + v, BF16) for k, v in ev_specs.items()}
        ev_qn = self.din("ev_qn", [2, 128, 3])
        ev_kvn = self.din("ev_kvn", [2, 128, 2])
        rope_d = self.din("ropeT", [64, S])
        cmask_d = self.din("cmask", [2, 128, 128])
        exp_d = self.din("expmat", [64, 16 * 128])
        ovl_d = self.din("ovl", [127, 32])
        scab_d = self.din("scab", [2, 128, NT * 32])
        selb_d = self.din("selb", [24, 24 * 128])
        cst_b = {"expmat": self.dscr("expmat_b", [64, 16 * 128], BF16), "selb": self.dscr("selb_b", [24, 24 * 128], BF16),
                 "scab": self.dscr("scab_b", [2, 128, NT * 32], BF16)}
        t_cstb = T()
        wu_b = self.dscr("wu_b", [DEPTH, NFC, 128, 8 * 128], BF16)
        wg_b = self.dscr("wg_b", [DEPTH, NFC, 128, 8 * 128], BF16)
        wd_b = self.dscr("wd_b", [DEPTH, 128, NFC * D], BF16)
        odqk_b = self.dscr("odqk_b", [2, 16, 128, 8 * 128], BF16)
        odv_b = self.dscr("odv_b", [2, 128, 8 * D], BF16)
        odo_b = self.dscr("odo_b", [2, 128, 8 * D], BF16)
        mod_d = self.dscr("mod", [DEPTH, nseq, 6 * D], F32)
        G_d = self.dscr("Gd", [8, N_OH], F32)
        self.out = out

        ps = P.psum("ps", [128, 8, 512], F32)
        ps_t = [T() for _ in range(8)]
        ident = P.sbuf("identb", [128, 128], BF16)
        ident_f = P.sbuf("identf", [128, 128], F32)
        ones_b = P.sbuf("onesb", [128, 128], BF16)
        ones_f = P.sbuf("onesf", [128, 128], F32)
        t_ident = T(const=True)
        t_ones = T(const=True)
        hT = P.sbuf("hT", [128, 8, S], BF16)
        t_hT = [T() for _ in range(NT)]
        NB = 5
        bc = P.sbuf("bc", [128, NB, D], F32)
        t_bc = [T() for _ in range(NB)]
        xt = [P.sbuf(f"xt{i}", [128, D], F32) for i in range(2)]
        t_xt = [T(), T()]
        wk = [P.sbuf(f"wk{i}", [128, D], F32) for i in range(2)]
        t_wk = [T() for _ in range(2)]
        hb = [P.sbuf(f"hb{i}", [128, D], BF16) for i in range(2)]
        t_hb = [T(), T()]
        st6 = P.sbuf("st6", [128, 2, 2, 6], F32)
        mv = P.sbuf("mv", [128, 2, 4], F32)
        t_st = [T(), T()]
        t_mv = [T(), T()]
        halo = P.sbuf("halo", [128, NFC, 2], F32)
        t_halo = [T() for _ in range(NFC)]
        cwt = P.sbuf("cwt", [128, NFC, 4], F32)
        t_cw = T()
        cact = P.sbuf("cact", [128, 8, nseq], F32)
        t_cact = T()
        t_adb = [T(), T()]
        t_modsb = [T(), T()]
        tbl = P.sbuf("tbl", [33, 8], F32)
        t_tbl = T()
        Bsb = P.sbuf("Bsb", [128, 8, 3, 128], BF16)
        Bc = P.sbuf("Bc", [128, 8, 247], BF16)
        t_B = T()
        lamt = P.sbuf("lamt", [128, 264], F32)
        t_lam = T()
        subl = P.sbuf("subl", [128, 2], F32)
        t_subl = T()
        eps_t = P.sbuf("eps_t", [128, 2], F32)
        t_eps = T(const=True)
        Md = P.sbuf("Md", [128, 128], BF16)
        ovl = P.sbuf("ovl_s", [127, 32], BF16)
        Zt = P.sbuf("Zt", [128, 512], BF16)
        qkn = P.sbuf("qkn", [128, 8], F32)
        t_qkn = T()
        t_cst = T(const=True)
        self.ar = P.sbuf("arena", [128, AR_EL], BF16)

        t_mod = [[T() for _ in range(nseq)] for _ in range(DEPTH)]
        t_x = [[T() for _ in range(NT)] for _ in range(nseq)]
        t_wub = [T() for _ in range(DEPTH)]
        t_wdb = [T() for _ in range(DEPTH)]
        t_odb = [T() for _ in range(2)]
        t_evb = [T() for _ in range(2)]
        t_G = T()

        def conv(dst, src, tok):
            P.op("pool", lambda e, d=dst, s=src: e.dma_start(out=d, in_=s), writes=[tok], dma=True)
        self.dma("sp", ident_f[:], ident_d[:, :], writes=[t_ident])
        P.op("pool", lambda e: e.dma_start(out=ident[:], in_=ident_d[:, :]), writes=[t_ident], dma=True)
        P.op("pool", lambda e: e.memset(ones_b[:], 1.0), writes=[t_ones])
        P.op("pool", lambda e: e.memset(ones_f[:], 1.0), writes=[t_ones])
        P.op("pool", lambda e: e.memset(eps_t[:, 0:1], RMS_EPS), writes=[t_eps])
        P.op("pool", lambda e: e.memset(eps_t[:, 1:2], LN_EPS), writes=[t_eps])
        for l in layers:
            if has(l, 1):
                for fc0 in range(0, NFC, 11):
                    conv(wu_b[l, fc0:fc0 + 11], wu_f[l, fc0:fc0 + 11].rearrange("f p k j -> f p (k j)"), t_wub[l])
                    conv(wg_b[l, fc0:fc0 + 11], wg_f[l, fc0:fc0 + 11].rearrange("f p k j -> f p (k j)"), t_wub[l])
                conv(wd_b[l], wd_f[l].rearrange("p f d -> p (f d)"), t_wdb[l])
            if has(l, 0) and l % 2 == 0:
                i = l // 2
                for k in ev_specs:
                    if k == "ev_wc":
                        for c0_ in range(0, 15, 5):
                            conv(ev_b[k][i, c0_:c0_ + 5], ev_f[k][i, c0_:c0_ + 5], t_evb[i])
                    else:
                        conv(ev_b[k][i], ev_f[k][i], t_evb[i])
            if has(l, 0) and l % 2 == 1:
                i = l // 2
                conv(odqk_b[i], odqk_f[i].rearrange("f p k j -> f p (k j)"), t_odb[i])
                conv(odv_b[i], odv_f[i].rearrange("p k n -> p (k n)"), t_odb[i])
                conv(odo_b[i], odo_f[i].rearrange("p k n -> p (k n)"), t_odb[i])

        if any(s_ == 0 and l_ % 2 == 0 for (l_, s_) in plan):
            P.op("pool", lambda e: e.dma_start(out=Md[:], in_=cmask_d[0]), writes=[t_cst], dma=True)
            conv(cst_b["expmat"], exp_d, t_cstb)
            conv(cst_b["selb"], selb_d, t_cstb)
            conv(cst_b["scab"], scab_d, t_cstb)
            P.op("pool", lambda e: e.dma_start(out=ovl[:], in_=ovl_d[:, :]), writes=[t_cst], dma=True)
        P.op("pool", lambda e: e.memset(Zt[:], 0.0), writes=[t_cst])
        adw = [self.view(i * 8192, [128, 8, 512], F32) for i in range(2)]
        adb = [self.view(32768 + i * 1024, [128, 512], F32)[0:nseq] for i in range(2)]
        modsb = [self.view(36864 + i * 1024, [128, 512], F32)[0:nseq] for i in range(2)]
        t_adw = [T(), T()]
        self.dma("sp", cact[:], cT[:, :, :], writes=[t_cact])
        P.op("act", lambda e: e.activation(out=cact[:], in_=cact[:], func=AF.Silu), reads=[t_cact], writes=[t_cact])
        ib = 0
        for l in layers:
            for cb in range(12):
                a = adw[ib % 2]
                ta = t_adw[ib % 2]
                ab, tab = adb[ib % 2], t_adb[ib % 2]
                mb, tmb = modsb[ib % 2], t_modsb[ib % 2]
                ib += 1
                self.dma("sp", ab, ada_b[l:l + 1, cb * 512:(cb + 1) * 512].broadcast_to([nseq, 512]), writes=[tab])
                self.dma("sp", a, ada_w[l, :, cb * 512:(cb + 1) * 512].rearrange("(k p) n -> p k n", p=128), writes=[ta])
                bank = 6 + (cb % 2)
                for kc in range(8):
                    P.op("pe", lambda e, a=a, kc=kc, bank=bank: e.matmul(
                        ps[0:nseq, bank, :], cact[:, kc, :], a[:, kc, :], start=(kc == 0), stop=(kc == 7)),
                        reads=[t_cact, ta], writes=[ps_t[bank]])
                P.op("dve", lambda e, mb=mb, ab=ab, bank=bank: e.tensor_tensor(
                    out=mb, in0=ps[0:nseq, bank, :], in1=ab, op=ALU.add),
                    reads=[ps_t[bank], tab], writes=[tmb])
                self.dma("sp", mod_d[l, :, cb * 512:(cb + 1) * 512], mb, reads=[tmb], writes=t_mod[l])
        if self.dbg:
            self.dbg_out("dbg_mod", mod_d[layers[0]], [nseq, 6 * D], F32, t_mod[layers[0]])

        need_bias = any(s == 0 for (l, s) in plan)
        if need_bias:
            P.barrier()
            self.dma("sp", tbl[0:32, :], relb[:, :], writes=[t_tbl])
            P.op("pool", lambda e: e.memset(tbl[32:33, :], NEG), writes=[t_tbl])
            ohb = [self.view(i * 8192, [33, 4096], F32) for i in range(2)]
            gsb_ = [self.view(16384 + i * 8192, [8, 4096], F32) for i in range(2)]
            t_oh = [T(), T()]
            t_g = [T(), T()]
            nch = (N_OH + 4095) // 4096
            for c in range(nch):
                c0 = c * 4096
                w = min(4096, N_OH - c0)
                o_, to = ohb[c % 2], t_oh[c % 2]
                g_, tg = gsb_[c % 2], t_g[c % 2]
                self.dma("sp", o_[0:33, 0:w], oh_d[:, c0:c0 + w], writes=[to])
                for s0 in range(0, w, 512):
                    sw = min(512, w - s0)
                    bank = (s0 // 512) % 2
                    P.op("pe", lambda e, o_=o_, s0=s0, sw=sw, bank=bank: e.matmul(
                        ps[0:8, bank, 0:sw], tbl[0:33, :], o_[0:33, s0:s0 + sw], start=True, stop=True),
                        reads=[t_tbl, to], writes=[ps_t[bank]])
                    P.op("act", lambda e, g_=g_, s0=s0, sw=sw, bank=bank: e.copy(out=g_[0:8, s0:s0 + sw], in_=ps[0:8, bank, 0:sw]),
                         reads=[ps_t[bank]], writes=[tg])
                self.dma("sp", G_d[:, c0:c0 + w], g_[0:8, 0:w], reads=[tg], writes=[t_G])
            for tau in range(2):
                P.op("pool", lambda e, tau=tau: e.dma_start(
                    out=Bsb[:, :, tau, :], in_=G_d[:, tau * 16384:(tau + 1) * 16384].rearrange("h (k q) -> k h q", k=128)),
                    reads=[t_G], writes=[t_B], dma=True)
            P.op("pool", lambda e: e.dma_start(out=Bc[:], in_=G_d[:, N_ATT:N_OH].rearrange("h (p m) -> p h m", p=128)),
                 reads=[t_G], writes=[t_B], dma=True)
            for h_ in range(8):
                P.op("pool", lambda e, h_=h_: e.dma_start(out=Bsb[:, h_, 2, :], in_=cmask_d[1]), writes=[t_B], dma=True)
            self.dbg_out("dbg_Bsb", Bsb[:], [128, 8, 3, 128], BF16, [t_B])
            self.dbg_out("dbg_Bc", Bc[:], [128, 8, 247], BF16, [t_B])
        P.barrier()

        def act_recip(out_ap, in_ap, rd, wr):
            P.op("act", lambda e: e.activation(out=out_ap, in_=in_ap, func=AF.Ln), reads=rd, writes=wr)
            P.op("act", lambda e: e.activation(out=out_ap, in_=out_ap, func=AF.Exp, scale=-1.0), reads=wr, writes=wr)

        def act_rstd(out_ap, in_ap, scale, eps_ap, rd, wr):
            P.op("act", lambda e: e.activation(out=out_ap, in_=in_ap, func=AF.Ln, bias=eps_ap, scale=scale), reads=rd + [t_eps], writes=wr)
            P.op("act", lambda e: e.activation(out=out_ap, in_=out_ap, func=AF.Exp, scale=-0.5), reads=wr, writes=wr)

        def bc_load(slot, src, rd, plus_one):
            self.dma("sp", bc[:, slot, :], src.broadcast_to([128, D]), reads=rd, writes=[t_bc[slot]])
            if plus_one:
                P.op("pool", lambda e, slot=slot: e.tensor_scalar_add(out=bc[:, slot, :], in0=bc[:, slot, :], scalar1=1.0),
                     reads=[t_bc[slot]], writes=[t_bc[slot]])

        def emit_hT(b, l, sub, xsrc):
            load_mod_h(b, l, sub)
            for t in range(NT):
                xi = t % 2
                self.dma("sp", xt[xi][:], xsrc[b, t * 128:(t + 1) * 128, :], reads=[t_x[b][t]], writes=[t_xt[xi]])
                h_from_x(xi, t)()

        def h_from_x(xi, t, tbank=7, src=None, tsrc=None):
            if src is None:
                src, tsrc = xt[xi], t_xt[xi]
            hps = ps[:, 0:2, :].rearrange("p a n -> p (a n)")
            P.op("dve", lambda e: e.tensor_tensor(out=hps, in0=src[:], in1=bc[:, 0, :], op=ALU.mult),
                 reads=[tsrc, t_bc[0]], writes=[ps_t[0], ps_t[1]])
            P.op("dve", lambda e: e.tensor_tensor(out=hb[xi][:], in0=hps, in1=bc[:, 1, :], op=ALU.add),
                 reads=[ps_t[0], ps_t[1], t_bc[1]], writes=[t_hb[xi]])

            def fin():
                psb = ps[:, tbank, :].bitcast(BF16)
                for kc in range(8):
                    P.op("pe", lambda e, kc=kc: e.transpose(
                        psb[:, kc * 128:(kc + 1) * 128], hb[xi][:, kc * 128:(kc + 1) * 128], ident[:]),
                        reads=[t_hb[xi], t_ident], writes=[ps_t[tbank]])
                P.op("act", lambda e: e.copy(
                    out=hT[:, :, t * 128:(t + 1) * 128], in_=psb.rearrange("p (k j) -> p k j", k=8)),
                    reads=[ps_t[tbank]], writes=[t_hT[t]])
            return fin

        def load_mod_h(b, l, sub):
            o = 0 if sub == 0 else 3
            bc_load(0, mod_d[l, b:b + 1, (o + 1) * D:(o + 2) * D], [t_mod[l][b]], True)
            bc_load(1, mod_d[l, b:b + 1, (o + 0) * D:(o + 1) * D], [t_mod[l][b]], False)

        def begin_sublayer(b, l, sub, xsrc):
            if not self.have_hT:
                emit_hT(b, l, sub, xsrc)
            epi_setup(b, l, sub)
            if self.nxt is not None:
                load_mod_h(b, self.nxt[0], self.nxt[1])

        def epilogue(b, l, sub, t, ybank, xsrc, tbank=7):
            xi = t % 2
            w_ = wk[xi]
            tw_ = t_wk[xi]
            st_ = st6[:, xi]
            mv_ = mv[:, xi]
            nxt = self.nxt
            self.dma("sp", xt[xi][:], xsrc[b, t * 128:(t + 1) * 128, :], reads=[t_x[b][t]], writes=[t_xt[xi]])
            yv = ps[:, ybank:ybank + 2, :].rearrange("p a n -> p (a n)")
            yt = [ps_t[ybank], ps_t[ybank + 1]]
            P.op("dve", lambda e: e.tensor_tensor(out=yv, in0=yv, in1=bc[:, 2, :], op=ALU.mult), reads=yt + [t_bc[2]], writes=yt)
            P.op("dve", lambda e: e.scalar_tensor_tensor(out=w_[:], in0=xt[xi][:], scalar=ALPHA, in1=yv, op0=ALU.mult, op1=ALU.add),
                 reads=[t_xt[xi]] + yt, writes=[tw_])
            for hh in range(2):
                P.op("dve", lambda e, hh=hh: e.bn_stats(out=st_[:, hh, :], in_=w_[:, hh * 512:(hh + 1) * 512]),
                     reads=[tw_], writes=[t_st[xi]])
            P.op("dve", lambda e: e.bn_aggr(out=mv_[:, 0:2], in_=st_), reads=[t_st[xi]], writes=[t_mv[xi]])
            act_rstd(mv_[:, 2:3], mv_[:, 1:2], 1.0, eps_t[:, 1:2], [t_mv[xi]], [t_mv[xi]])
            P.op("dve", lambda e: e.scalar_tensor_tensor(out=mv_[:, 3:4], in0=mv_[:, 0:1], scalar=-1.0, in1=mv_[:, 2:3],
                                                         op0=ALU.mult, op1=ALU.mult), reads=[t_mv[xi]], writes=[t_mv[xi]])
            P.op("act", lambda e: e.activation(out=w_[:], in_=w_[:], func=AF.Identity, bias=mv_[:, 3:4], scale=mv_[:, 2:3]),
                 reads=[tw_, t_mv[xi]], writes=[tw_])

            def stage2():
                P.op("pool", lambda e: e.tensor_tensor(out=w_[:], in0=w_[:], in1=bc[:, 3, :], op=ALU.mult),
                     reads=[tw_, t_bc[3]], writes=[tw_])
                P.op("pool", lambda e: e.tensor_tensor(out=w_[:], in0=w_[:], in1=bc[:, 4, :], op=ALU.add),
                     reads=[tw_, t_bc[4]], writes=[tw_])
                self.dma("sp", out[b, t * 128:(t + 1) * 128, :], w_[:], reads=[tw_], writes=[t_x[b][t]])
                if nxt is not None:
                    return h_from_x(xi, t, tbank, w_, tw_)
                return None
            return stage2

        def ep_pipeline(tiles, mm, ybank_of, b, l, sub, xsrc, tbank):
            s2_prev = None
            fins = []
            for t in tiles:
                mm(t)
                while len(fins) > 1:
                    f_ = fins.pop(0)
                    if f_ is not None:
                        f_()
                s2_cur = epilogue(b, l, sub, t, ybank_of(t), xsrc, tbank)
                if s2_prev is not None:
                    fins.append(s2_prev())
                s2_prev = s2_cur
            while len(fins) > 1:
                f_ = fins.pop(0)
                if f_ is not None:
                    f_()
            if s2_prev is not None:
                fins.append(s2_prev())
            for f_ in fins:
                if f_ is not None:
                    f_()

        def epi_setup(b, l, sub):
            o = 2 if sub == 0 else 5
            bc_load(2, mod_d[l, b:b + 1, o * D:(o + 1) * D], [t_mod[l][b]], True)
            bc_load(3, ln_g[l, sub:sub + 1, :], [], False)
            bc_load(4, ln_b[l, sub:sub + 1, :], [], False)

        def ffn(b, l, xsrc):
            P.barrier()
            wd = self.view(0, [128, NFC, D], BF16)
            t_wd = T()
            actT = self.view(22528, [128, NFC, 1024], BF16)
            t_act = [T() for _ in range(NFC)]
            NW = 4
            wug = [self.view(45056 + i * 2048, [128, 2, 8, 128], BF16) for i in range(NW)]
            t_wug = [T() for _ in range(NW)]
            gsb = [self.view(53248 + i * 1032, [128, 514], F32) for i in range(2)]
            t_gsb = [T(), T()]
            tmp = [self.view(55312 + i * 1024, [128, 512], F32) for i in range(3)]
            t_tmp = [T() for _ in range(3)]
            begin_sublayer(b, l, 1, xsrc)
            self.dbg_out("dbg_hT", hT[:], [128, 8, S], BF16, t_hT)
            self.dma("sp", wd, wd_b[l].rearrange("p (f d) -> p f d", f=NFC), reads=[t_wdb[l]], writes=[t_wd])
            self.dma("sp", cwt[:], cw_d[l].rearrange("(f p) k -> p f k", p=128), writes=[t_cw])
            iw = 0
            ih = 0
            for tb in range(2):
                for fc in range(NFC):
                    w = wug[iw % NW]
                    tw = t_wug[iw % NW]
                    iw += 1
                    self.dma("sp", w[:, 0], wu_b[l, fc].rearrange("p (k j) -> p k j", k=8), reads=[t_wub[l]], writes=[tw])
                    self.dma("sp", w[:, 1], wg_b[l, fc].rearrange("p (k j) -> p k j", k=8), reads=[t_wub[l]], writes=[tw])
                    for half in range(2):
                        c0 = tb * 1024 + half * 512
                        a0 = half * 512
                        hts = t_hT[c0 // 128:c0 // 128 + 4]
                        bu = ih % 2
                        bg = 2 + ih % 2
                        g = gsb[ih % 2]
                        tg = t_gsb[ih % 2]
                        ih += 1
                        for kc in range(8):
                            P.op("pe", lambda e, w=w, kc=kc, bu=bu, c0=c0: e.matmul(ps[:, bu, :], w[:, 0, kc, :], hT[:, kc, c0:c0 + 512],
                                                                                   start=(kc == 0), stop=(kc == 7)),
                                 reads=[tw] + hts, writes=[ps_t[bu]])
                        for kc in range(8):
                            P.op("pe", lambda e, w=w, kc=kc, bg=bg, c0=c0: e.matmul(ps[:, bg, :], w[:, 1, kc, :], hT[:, kc, c0:c0 + 512],
                                                                                   start=(kc == 0), stop=(kc == 7)),
                                 reads=[tw] + hts, writes=[ps_t[bg]])
                        if c0 == 0:
                            P.op("pool", lambda e, g=g: e.memset(g[:, 0:2], 0.0), writes=[tg])
                        else:
                            P.op("pool", lambda e, g=g, fc=fc: e.tensor_copy(out=g[:, 0:2], in_=halo[:, fc, :]),
                                 reads=[t_halo[fc]], writes=[tg])
                        P.op("act", lambda e, g=g, bg=bg: e.copy(out=g[:, 2:514], in_=ps[:, bg, :]), reads=[ps_t[bg]], writes=[tg])
                        P.op("pool", lambda e, g=g, fc=fc: e.tensor_copy(out=halo[:, fc, :], in_=g[:, 512:514]),
                             reads=[tg], writes=[t_halo[fc]])
                        P.op("dve", lambda e, g=g, fc=fc: e.tensor_scalar(out=tmp[0], in0=g[:, 2:514], scalar1=cwt[:, fc, 2:3],
                                                                          scalar2=cwt[:, fc, 3:4], op0=ALU.mult, op1=ALU.add),
                             reads=[tg, t_cw], writes=[t_tmp[0]])
                        P.op("dve", lambda e, g=g, fc=fc: e.scalar_tensor_tensor(out=tmp[1], in0=g[:, 1:513], scalar=cwt[:, fc, 1:2],
                                                                                 in1=tmp[0], op0=ALU.mult, op1=ALU.add),
                             reads=[tg, t_cw, t_tmp[0]], writes=[t_tmp[1]])
                        P.op("dve", lambda e, g=g, fc=fc: e.scalar_tensor_tensor(out=tmp[2], in0=g[:, 0:512], scalar=cwt[:, fc, 0:1],
                                                                                 in1=tmp[1], op0=ALU.mult, op1=ALU.add),
                             reads=[tg, t_cw, t_tmp[1]], writes=[t_tmp[2]])
                        P.op("act", lambda e: e.activation(out=tmp[0], in_=tmp[2], func=AF.Silu),
                             reads=[t_tmp[2]], writes=[t_tmp[0]])
                        P.op("dve", lambda e, fc=fc, bu=bu, a0=a0: e.tensor_tensor(out=actT[:, fc, a0:a0 + 512], in0=tmp[0], in1=ps[:, bu, :],
                                                                                  op=ALU.mult),
                             reads=[t_tmp[0], ps_t[bu]], writes=[t_act[fc]])
                def mm_down(t):
                    tt = t % 8
                    yb = 4 if tt % 2 == 0 else 6
                    for dh in range(2):
                        for fc in range(NFC):
                            P.op("pe", lambda e, fc=fc, tt=tt, dh=dh, yb=yb: e.matmul(
                                ps[:, yb + dh, :], actT[:, fc, tt * 128:(tt + 1) * 128], wd[:, fc, dh * 512:(dh + 1) * 512],
                                start=(fc == 0), stop=(fc == NFC - 1)),
                                reads=[t_act[fc], t_wd], writes=[ps_t[yb + dh]])
                ep_pipeline([tb * 8 + tt for tt in range(8)], mm_down, lambda t: 4 if t % 2 == 0 else 6, b, l, 1, xsrc, 2)

        def diff_attn(b, l, xsrc):
            i = l // 2
            lam_init = 0.8 - 0.6 * math.exp(-0.3 * l)
            P.barrier()
            V = self.view(0, [128, NT, D], BF16)
            t_V = [T() for _ in range(NT)]
            oT = self.view(16384, [128, 8, S], BF16)
            t_oT = [[T() for _ in range(4)] for _ in range(8)]
            qk = [[self.view(32768 + (2 * j + c) * 2048, [128, S], BF16) for c in range(2)] for j in range(2)]
            t_qk = [[T(), T()], [T(), T()]]
            E = [self.view(40960 + j * 512, [128, 512], BF16) for j in range(4)]
            t_E = [T() for _ in range(4)]
            wo = self.view(43008, [128, 8, D], BF16)
            t_wo = T()
            wqk = [self.view(51200 + j * 2048, [128, 2, 8, 128], BF16) for j in range(2)]
            t_wqk = [T(), T()]
            f32w = [self.view(55296 + j * 1024, [128, 512], F32) for j in range(4)]
            t_f = [T() for _ in range(4)]

            begin_sublayer(b, l, 0, xsrc)
            self.dma("sp", lamt[:, 0:256], dlam[i:i + 1, :].broadcast_to([128, 256]), writes=[t_lam])
            P.op("dve", lambda e: e.tensor_tensor(out=lamt[:, 0:64], in0=lamt[:, 0:64], in1=lamt[:, 64:128], op=ALU.mult),
                 reads=[t_lam], writes=[t_lam])
            P.op("dve", lambda e: e.tensor_tensor(out=lamt[:, 128:192], in0=lamt[:, 128:192], in1=lamt[:, 192:256], op=ALU.mult),
                 reads=[t_lam], writes=[t_lam])
            P.op("dve", lambda e: e.reduce_sum(out=lamt[:, 256:257], in_=lamt[:, 0:64], axis=AX.X), reads=[t_lam], writes=[t_lam])
            P.op("dve", lambda e: e.reduce_sum(out=lamt[:, 257:258], in_=lamt[:, 128:192], axis=AX.X), reads=[t_lam], writes=[t_lam])
            P.op("act", lambda e: e.activation(out=lamt[:, 258:260], in_=lamt[:, 256:258], func=AF.Exp), reads=[t_lam], writes=[t_lam])
            P.op("dve", lambda e: e.scalar_tensor_tensor(out=lamt[:, 260:261], in0=lamt[:, 259:260], scalar=-lam_init,
                                                         in1=lamt[:, 258:259], op0=ALU.add, op1=ALU.subtract),
                 reads=[t_lam], writes=[t_lam])
            self.dma("sp", subl[:, 0:1], dsub[i], writes=[t_subl])
            P.op("pool", lambda e: e.tensor_scalar_mul(out=subl[:, 0:1], in0=subl[:, 0:1], scalar1=1.0 - lam_init),
                 reads=[t_subl], writes=[t_subl])
            self.dma("sp", wo, odv_b[i].rearrange("p (k n) -> p k n", k=8), reads=[t_odb[i]], writes=[t_wo])
            for t in range(NT):
                for dh in range(2):
                    bank = 5 + dh
                    for kc in range(8):
                        P.op("pe", lambda e, t=t, dh=dh, kc=kc, bank=bank: e.matmul(
                            ps[:, bank, :], hT[:, kc, t * 128:(t + 1) * 128], wo[:, kc, dh * 512:(dh + 1) * 512],
                            start=(kc == 0), stop=(kc == 7)), reads=[t_hT[t], t_wo], writes=[ps_t[bank]])
                    if dh == 0:
                        P.op("act", lambda e, t=t, bank=bank: e.copy(out=V[:, t, 0:512], in_=ps[:, bank, :]),
                             reads=[ps_t[bank]], writes=[t_V[t]])
                    else:
                        P.op("dve", lambda e, t=t, bank=bank: e.tensor_copy(out=V[:, t, 512:1024], in_=ps[:, bank, :]),
                             reads=[ps_t[bank]], writes=[t_V[t]])
            Esum = [self.view(59392 + m_ * 1024, [128, 512], F32) for m_ in range(2)]
            t_Es = [T(), T()]

            def hq_block(h, Q, qT, kT, tq, tk, sidx, par, deferred):
                nk = 4 * Q + 4
                tiles = [(kt, m) for kt in range(nk) for m in range(2)]
                ob = (3, 5) if par == 0 else (4, 6)
                LAG = 2
                pend = []

                def do_scores(kt, m, sidx):
                    jj = kt - 4 * Q
                    c0 = max(0, jj) * 128
                    sb = sidx % 3
                    eb = sidx % 4
                    q0 = Q * 512
                    hasb = kt >= 4 * Q - 1
                    P.op("pe", lambda e: e.matmul(
                        ps[:, sb, c0:512], kT[m * 64:(m + 1) * 64, kt * 128:(kt + 1) * 128],
                        qT[m * 64:(m + 1) * 64, q0 + c0:q0 + 512], start=True, stop=not hasb),
                        reads=[tq, tk], writes=[ps_t[sb]])
                    if hasb:
                        if jj < 0:
                            rhs = Bsb[:, h, 1, :]
                            cs, ce = 0, 128
                        elif jj == 3:
                            rhs = Bsb[:, h, 0, :]
                            cs, ce = 384, 512
                        else:
                            rhs = Bsb[:, h, 0:2, :].rearrange("p t q -> p (t q)")
                            cs, ce = c0, c0 + 256
                        P.op("pe", lambda e: e.matmul(ps[:, sb, cs:ce], ident[:], rhs, start=False, stop=True),
                             reads=[t_ident, t_B], writes=[ps_t[sb]])
                    P.op("act", lambda e: e.activation(out=E[eb][:, c0:512], in_=ps[:, sb, c0:512], func=AF.Exp),
                         reads=[ps_t[sb]], writes=[t_E[eb]])
                    if kt == 0:
                        P.op("dve", lambda e: e.tensor_copy(out=Esum[m][:, :], in_=E[eb][:, :]), reads=[t_E[eb]], writes=[t_Es[m]])
                    else:
                        P.op("dve", lambda e: e.tensor_tensor(out=Esum[m][:, c0:512], in0=Esum[m][:, c0:512], in1=E[eb][:, c0:512], op=ALU.add),
                             reads=[t_E[eb], t_Es[m]], writes=[t_Es[m]])
                    return (kt, m, c0, eb)

                def do_pv(kt, m, c0, eb):
                    P.op("pe", lambda e: e.matmul(ps[:, ob[m], c0:512], V[:, kt, h * 128:(h + 1) * 128], E[eb][:, c0:512],
                                                  start=(kt == 0), stop=(kt == nk - 1)),
                         reads=[t_V[kt], t_E[eb]], writes=[ps_t[ob[m]]])

                for ti, (kt, m) in enumerate(tiles):
                    pend.append(do_scores(kt, m, sidx))
                    sidx += 1
                    if len(pend) > LAG:
                        do_pv(*pend.pop(0))
                    if ti == 5 and deferred is not None:
                        deferred()
                        deferred = None
                while pend:
                    do_pv(*pend.pop(0))
                if deferred is not None:
                    deferred()
                r0, r1, u0, u1 = f32w
                P.op("pe", lambda e: e.matmul(ps[:, 7, :], ones_f[:], Esum[0], start=True, stop=True), reads=[t_ones, t_Es[0]], writes=[ps_t[7]])
                act_recip(r0, ps[:, 7, :], [ps_t[7]], [t_f[0]])
                P.op("pe", lambda e: e.matmul(ps[:, 7, :], ones_f[:], Esum[1], start=True, stop=True), reads=[t_ones, t_Es[1]], writes=[ps_t[7]])
                act_recip(r1, ps[:, 7, :], [ps_t[7]], [t_f[1]])
                P.op("dve", lambda e: e.tensor_tensor(out=u0, in0=ps[:, ob[0], :], in1=r0, op=ALU.mult),
                     reads=[ps_t[ob[0]], t_f[0]], writes=[t_f[2]])
                P.op("dve", lambda e: e.tensor_tensor(out=u1, in0=ps[:, ob[1], :], in1=r1, op=ALU.mult),
                     reads=[ps_t[ob[1]], t_f[1]], writes=[t_f[3]])
                P.op("dve", lambda e: e.scalar_tensor_tensor(out=u0, in0=u1, scalar=lamt[:, 260:261], in1=u0,
                                                             op0=ALU.mult, op1=ALU.add),
                     reads=[t_f[3], t_f[2], t_lam], writes=[t_f[2]])
                P.op("dve", lambda e: e.tensor_tensor(out=r0, in0=u0, in1=u0, op=ALU.mult), reads=[t_f[2]], writes=[t_f[0]])

                def tail():
                    P.op("pe", lambda e: e.matmul(ps[:, 7, :], ones_f[:], r0, start=True, stop=True),
                         reads=[t_ones, t_f[0]], writes=[ps_t[7]])
                    act_rstd(r1, ps[:, 7, :], 1.0 / 128.0, eps_t[:, 0:1], [ps_t[7]], [t_f[1]])
                    P.op("dve", lambda e: e.scalar_tensor_tensor(out=oT[:, h, Q * 512:(Q + 1) * 512], in0=u0, scalar=subl[:, 0:1], in1=r1,
                                                                 op0=ALU.mult, op1=ALU.mult),
                         reads=[t_f[2], t_f[1], t_subl], writes=[t_oT[h][Q]])
                return sidx, tail

            sidx = 0
            par = 0
            deferred = None
            for h in range(8):
                j2 = h % 2
                w = wqk[j2]
                tw = t_wqk[j2]
                qT, kT = qk[j2]
                tq, tk = t_qk[j2]
                self.dma("sp", w[:, 0], odqk_b[i, h].rearrange("p (k j) -> p k j", k=8), reads=[t_odb[i]], writes=[tw])
                self.dma("sp", w[:, 1], odqk_b[i, 8 + h].rearrange("p (k j) -> p k j", k=8), reads=[t_odb[i]], writes=[tw])
                for c in range(2):
                    for tb in range(4):
                        for kc in range(8):
                            P.op("pe", lambda e, w=w, c=c, kc=kc, tb=tb: e.matmul(
                                ps[:, 7, :], w[:, c, kc, :], hT[:, kc, tb * 512:(tb + 1) * 512], start=(kc == 0), stop=(kc == 7)),
                                reads=[tw] + t_hT[tb * 4:tb * 4 + 4], writes=[ps_t[7]])
                        dst = qT if c == 0 else kT
                        P.op("act", lambda e, dst=dst, tb=tb, c=c: e.activation(
                            out=dst[:, tb * 512:(tb + 1) * 512], in_=ps[:, 7, :], func=AF.Identity, scale=(0.125 if c == 0 else 1.0)),
                            reads=[ps_t[7]], writes=[tq if c == 0 else tk])
                for Q in range(4):
                    sidx, deferred = hq_block(h, Q, qT, kT, tq, tk, sidx, par, deferred)
                    par ^= 1
            if deferred is not None:
                deferred()
            self.dbg_out("dbg_oT", oT, [128, 8, S], BF16, [t for r in t_oT for t in r])
            self.dma("sp", wo, odo_b[i].rearrange("p (k n) -> p k n", k=8), reads=[t_odb[i]], writes=[t_wo])
            def mm_wo(t):
                yb = 3 if t % 2 == 0 else 5
                for dh in range(2):
                    for h in range(8):
                        P.op("pe", lambda e, t=t, dh=dh, h=h, yb=yb: e.matmul(
                            ps[:, yb + dh, :], oT[:, h, t * 128:(t + 1) * 128], wo[:, h, dh * 512:(dh + 1) * 512],
                            start=(h == 0), stop=(h == 7)), reads=[t_oT[h][t // 4], t_wo], writes=[ps_t[yb + dh]])
            ep_pipeline(list(range(NT)), mm_wo, lambda t: 3 if t % 2 == 0 else 5, b, l, 0, xsrc, 7)

        def run_tiles(specs, acc, E, t_E, sidx, LAG=2):
            full0 = (specs[0]["c0"] == 0 and specs[0]["c1"] == 512)
            if not full0:
                P.op("pe", lambda e: e.matmul(ps[:, acc, :], Zt[:, 0:128], Zt[:], start=True, stop=False),
                     reads=[t_cst], writes=[ps_t[acc]])
            pend = []
            n = len(specs)

            def scores(sp, sidx):
                sb = sidx % 3
                eb = sidx % 4
                c0, c1 = sp["c0"], sp["c1"]
                ex = sp["extra"]
                P.op("pe", lambda e: e.matmul(ps[:, sb, c0:c1], sp["lhsT"], sp["rhs"], start=True, stop=(len(ex) == 0)),
                     reads=sp["rd"], writes=[ps_t[sb]])
                for xi_, (xl, xr, cs, ce, xrd) in enumerate(ex):
                    P.op("pe", lambda e, xl=xl, xr=xr, cs=cs, ce=ce, last=(xi_ == len(ex) - 1): e.matmul(
                        ps[:, sb, cs:ce], xl, xr, start=False, stop=last), reads=xrd, writes=[ps_t[sb]])
                P.op("act", lambda e: e.activation(out=E[eb][:, c0:c1], in_=ps[:, sb, c0:c1], func=AF.Exp),
                     reads=[ps_t[sb]], writes=[t_E[eb]])
                return (sp, eb)

            def pv(sp, eb, idx):
                c0, c1 = sp["c0"], sp["c1"]
                P.op("pe", lambda e: e.matmul(ps[:, acc, c0:c1], sp["v"], E[eb][:, c0:c1], start=(full0 and idx == 0), stop=(idx == n - 1)),
                     reads=sp["vrd"] + [t_E[eb]], writes=[ps_t[acc]])

            done = 0
            for sp in specs:
                pend.append(scores(sp, sidx))
                sidx += 1
                if len(pend) > LAG:
                    a_, b_ = pend.pop(0)
                    pv(a_, b_, done)
                    done += 1
            while pend:
                a_, b_ = pend.pop(0)
                pv(a_, b_, done)
                done += 1
            return sidx

        def even_attn(b, l, xsrc):
            i = l // 2
            evb = {k: v[i] for k, v in ev_b.items()}
            rd_w = [t_evb[i]]
            sc_mla = 96 ** -0.5
            P.barrier()
            off = [0]

            def alloc(n):
                o = off[0]
                off[0] = o + n + (n % 2)
                assert off[0] <= AR_EL, off[0]
                return o

            oM = self.view(alloc(4 * S), [128, 4, S], BF16)
            t_oM = [[T() for _ in range(4)] for _ in range(4)]
            t_oN = [[T() for _ in range(4)] for _ in range(4)]
            E = [self.view(alloc(512), [128, 512], BF16) for _ in range(4)]
            t_E = [T() for _ in range(4)]
            fw_ = [self.view(alloc(1024), [128, 512], F32) for _ in range(4)]
            t_fw = [T() for _ in range(4)]
            wcb = [self.view(alloc(3 * 1024), [128, 3, 8, 128], BF16) for _ in range(1)]
            t_wcb = [T()]
            base_off = off[0]
            cqT = self.view(alloc(3 * S), [128, 3, S], BF16)
            ckvT = self.view(alloc(2 * S), [128, 2, S], BF16)
            krT = self.view(alloc(S), [128, S], BF16)
            t_cq = [T() for _ in range(4)]
            t_ckv = [T() for _ in range(4)]
            t_kr = [T() for _ in range(4)]
            rope = self.view(alloc(2 * S), [128, S], F32)
            t_rope = T()
            lat = self.view(alloc(3 * 1024), [128, 3, 512], F32)
            t_lat = [T() for _ in range(3)]
            Vm = self.view(alloc(NT * 1024), [128, NT, 8, 128], BF16)
            t_Vm = [T() for _ in range(NT)]
            QK = [[self.view(alloc(S), [128, S], BF16) for _ in range(2)] for _ in range(2)]
            t_QK = [[T(), T()], [T(), T()]]
            wcoff = base_off - 3 * 1024
            uqb = [self.view(wcoff + j_ * 384, [128, 3, 128], BF16) for j_ in range(2)]
            ukb = [self.view(wcoff + 768 + j_ * 256, [128, 2, 128], BF16) for j_ in range(2)]
            t_ub = [T(), T()]
            ukv = self.view(wcoff + 1280, [128, 2, 512], BF16)
            t_ukv = T()

            begin_sublayer(b, l, 0, xsrc)
            self.dma("sp", rope[0:64, :], rope_d[:, :], writes=[t_rope])
            self.dma("sp", qkn[:, 0:3], ev_qn[i], writes=[t_qkn])
            self.dma("sp", qkn[:, 3:5], ev_kvn[i], writes=[t_qkn])
            wc = wcb[0]
            for (nch, ch0, gcol, dstT, tdst) in [(3, 0, 0, cqT, t_cq), (2, 3, 3, ckvT, t_ckv)]:
                for c in range(nch):
                    self.dma("sp", wc[:, c], evb["ev_wc"][ch0 + c].rearrange("p (k j) -> p k j", k=8), reads=rd_w, writes=[t_wcb[0]])
                for tb in range(4):
                    for c in range(nch):
                        bank = 6 + (c % 2)
                        for kc in range(8):
                            P.op("pe", lambda e, c=c, kc=kc, tb=tb, bank=bank: e.matmul(
                                ps[:, bank, :], wc[:, c, kc, :], hT[:, kc, tb * 512:(tb + 1) * 512], start=(kc == 0), stop=(kc == 7)),
                                reads=[t_wcb[0]] + t_hT[tb * 4:tb * 4 + 4], writes=[ps_t[bank]])
                        P.op("act", lambda e, c=c, bank=bank: e.copy(out=lat[:, c, :], in_=ps[:, bank, :]),
                             reads=[ps_t[bank]], writes=[t_lat[c]])
                        sq = fw_[c % 2]
                        P.op("act", lambda e, sq=sq, bank=bank: e.activation(out=sq, in_=ps[:, bank, :], func=AF.Square),
                             reads=[ps_t[bank]], writes=[t_fw[c % 2]])
                        P.op("pe", lambda e, sq=sq, c=c, nch=nch: e.matmul(ps[:, 5, :], ones_f[:], sq, start=(c == 0), stop=(c == nch - 1)),
                             reads=[t_ones, t_fw[c % 2]], writes=[ps_t[5]])
                    rs = fw_[2]
                    act_rstd(rs, ps[:, 5, :], 1.0 / (128.0 * nch), eps_t[:, 0:1], [ps_t[5]], [t_fw[2]])
                    for c in range(nch):
                        P.op("dve", lambda e, c=c, tb=tb, rs=rs, dstT=dstT, gcol=gcol: e.scalar_tensor_tensor(
                            out=dstT[:, c, tb * 512:(tb + 1) * 512], in0=lat[:, c, :], scalar=qkn[:, gcol + c:gcol + c + 1], in1=rs,
                            op0=ALU.mult, op1=ALU.mult), reads=[t_lat[c], t_fw[2], t_qkn], writes=[tdst[tb]])

            def rope_rot(psb, dst, cols, scale, rd, wr):
                tA, tB = fw_[0], fw_[1]
                P.op("dve", lambda e: e.scalar_tensor_tensor(out=tA[0:32, :], in0=ps[32:64, psb, :], scalar=scale, in1=rope[32:64, cols],
                                                             op0=ALU.mult, op1=ALU.mult),
                     reads=[ps_t[psb], t_rope] + rd, writes=[t_fw[0]])
                P.op("dve", lambda e: e.scalar_tensor_tensor(out=tB[0:32, :], in0=ps[0:32, psb, :], scalar=scale, in1=rope[0:32, cols],
                                                             op0=ALU.mult, op1=ALU.mult),
                     reads=[ps_t[psb], t_rope] + rd, writes=[t_fw[1]])
                P.op("pool", lambda e: e.tensor_tensor(out=dst[0:32, cols], in0=tA[0:32, :], in1=tB[0:32, :], op=ALU.add),
                     reads=[t_fw[0], t_fw[1]], writes=wr)

            self.dma("sp", wc[:, 0], evb["ev_wc"][5].rearrange("p (k j) -> p k j", k=8), reads=rd_w, writes=[t_wcb[0]])
            for tb in range(4):
                for kc in range(8):
                    P.op("pe", lambda e, kc=kc, tb=tb: e.matmul(ps[:, 7, :], wc[:, 0, kc, :], hT[:, kc, tb * 512:(tb + 1) * 512],
                                                                start=(kc == 0), stop=(kc == 7)),
                         reads=[t_wcb[0]] + t_hT[tb * 4:tb * 4 + 4], writes=[ps_t[7]])
                rope_rot(7, krT, slice(tb * 512, (tb + 1) * 512), 1.0, [], [t_kr[tb]])

            P.barrier()
            P.op("pool", lambda e: e.memset(Vm.rearrange("p t h d -> p (t h) d")[:, :, 64:128], 1.0), writes=t_Vm)
            self.dma("sp", ukv, evb["ev_ukvv"].rearrange("p (k n) -> p k n", k=2), reads=rd_w, writes=[t_ukv])
            for t in range(NT):
                for kc in range(2):
                    P.op("pe", lambda e, t=t, kc=kc: e.matmul(ps[:, 6, :], ckvT[:, kc, t * 128:(t + 1) * 128], ukv[:, kc, :],
                                                              start=(kc == 0), stop=(kc == 1)),
                         reads=[t_ckv[t // 4], t_ukv], writes=[ps_t[6]])
                P.op("act", lambda e, t=t: e.copy(out=Vm[:, t, :, 0:64], in_=ps[:, 6, :].rearrange("p (h d) -> p h d", h=8)),
                     reads=[ps_t[6]], writes=[t_Vm[t]])
            for j in range(2):
                for c in range(2):
                    P.op("pool", lambda e, j=j, c=c: e.memset(QK[j][c][32:64, :], 0.0), writes=[t_QK[j][c]])
                P.op("pool", lambda e, j=j: e.tensor_copy(out=QK[j][1][0:32, :], in_=krT[0:32, :]), reads=t_kr, writes=[t_QK[j][1]])
            sidx = 0
            for h in range(8):
                j = h % 2
                QT, KT = QK[j]
                tq, tk = t_QK[j]
                self.dma("sp", uqb[j], evb["ev_uq"][h].rearrange("p (k j) -> p k j", k=3), reads=rd_w, writes=[t_ub[j]])
                self.dma("sp", ukb[j], evb["ev_ukvk"][h].rearrange("p (k j) -> p k j", k=2), reads=rd_w, writes=[t_ub[j]])
                for tb in range(4):
                    cols = slice(tb * 512, (tb + 1) * 512)
                    qb_ = 7 if tb % 2 == 0 else 5
                    kb_ = 6 if tb % 2 == 0 else 4
                    for kc in range(3):
                        P.op("pe", lambda e, j=j, kc=kc, cols=cols, qb_=qb_: e.matmul(ps[:, qb_, :], uqb[j][:, kc, :], cqT[:, kc, cols],
                                                                                      start=(kc == 0), stop=(kc == 2)),
                             reads=[t_ub[j], t_cq[tb]], writes=[ps_t[qb_]])
                    P.op("act", lambda e, QT=QT, cols=cols, qb_=qb_: e.activation(out=QT[64:128, cols], in_=ps[64:128, qb_, :], func=AF.Identity,
                                                                                  scale=sc_mla),
                         reads=[ps_t[qb_]], writes=[tq])
                    rope_rot(qb_, QT, cols, sc_mla, [], [tq])
                    for kc in range(2):
                        P.op("pe", lambda e, j=j, kc=kc, cols=cols, kb_=kb_: e.matmul(ps[:, kb_, :], ukb[j][:, kc, :], ckvT[:, kc, cols],
                                                                                      start=(kc == 0), stop=(kc == 1)),
                             reads=[t_ub[j], t_ckv[tb]], writes=[ps_t[kb_]])
                    P.op("act", lambda e, KT=KT, cols=cols, kb_=kb_: e.copy(out=KT[64:128, cols], in_=ps[64:128, kb_, :]),
                         reads=[ps_t[kb_]], writes=[tk])
                for Q in range(4):
                    specs = []
                    for kt in range(4 * Q + 4):
                        jj = kt - 4 * Q
                        c0 = max(0, jj) * 128
                        ex = []
                        if jj >= 0:
                            ex.append((ident[:], Md[:], c0, c0 + 128, [t_ident, t_cst]))
                        specs.append(dict(c0=c0, c1=512, lhsT=KT[:, kt * 128:(kt + 1) * 128], rhs=QT[:, Q * 512 + c0:(Q + 1) * 512],
                                          rd=[tq, tk], extra=ex, v=Vm[:, kt, h, :], vrd=[t_Vm[kt]]))
                    sidx = run_tiles(specs, 3, E, t_E, sidx)
                    rl = fw_[3]
                    act_recip(rl[64:128, :], ps[64:128, 3, :], [ps_t[3]], [t_fw[3]])
                    hp = (h % 2) * 64
                    P.op("dve", lambda e, rl=rl, hp=hp, h=h, Q=Q: e.tensor_tensor(
                        out=oM[hp:hp + 64, h // 2, Q * 512:(Q + 1) * 512], in0=ps[0:64, 3, :], in1=rl[64:128, :], op=ALU.mult),
                        reads=[ps_t[3], t_fw[3]], writes=[t_oM[h // 2][Q]])
            self.dbg_out("dbg_oM", oM, [128, 4, S], BF16, [t for r in t_oM for t in r])

            P.barrier()
            off[0] = base_off
            oN = self.view(alloc(4 * S), [128, 4, S], BF16)
            NQ = self.view(alloc(4 * S), [128, 4, S], BF16)
            t_NQ = [[T() for _ in range(4)] for _ in range(4)]
            fm = [self.view(alloc(S), [128, S], BF16) for _ in range(4)]
            t_fm = [[T() for _ in range(4)] for _ in range(4)]
            srcK, srcV, kselT, kwinT = fm
            ghl = self.view(alloc(2 * S), [128, 2, S], BF16)
            t_ghl = [T() for _ in range(4)]
            Vn = self.view(alloc(NT * 512), [128, NT, 4, 128], BF16)
            t_Vn = [T() for _ in range(NT)]
            maskT = self.view(alloc(S), [128, S], BF16)
            t_mask = [T() for _ in range(NT)]
            wvn_off = alloc(2048)
            wvn = self.view(wvn_off, [128, 8, 256], BF16)
            t_wvn = T()
            assert wvn_off + 4096 <= AR_EL
            P.op("pool", lambda e: e.memset(Vn.rearrange("p t h d -> p (t h) d")[:, :, 64:128], 1.0), writes=t_Vn)
            self.dma("sp", wvn, evb["ev_wvn"].rearrange("p (k n) -> p k n", k=8), reads=rd_w, writes=[t_wvn])
            for t in range(NT):
                for kc in range(8):
                    P.op("pe", lambda e, t=t, kc=kc: e.matmul(ps[:, 6, 0:256], hT[:, kc, t * 128:(t + 1) * 128], wvn[:, kc, :],
                                                              start=(kc == 0), stop=(kc == 7)),
                         reads=[t_hT[t], t_wvn], writes=[ps_t[6]])
                P.op("act", lambda e, t=t: e.copy(out=Vn[:, t, :, 0:64], in_=ps[:, 6, 0:256].rearrange("p (h d) -> p h d", h=4)),
                     reads=[ps_t[6]], writes=[t_Vn[t]])
            for ch in range(6, 15):
                self.dma("sp", wc[:, 0], evb["ev_wc"][ch].rearrange("p (k j) -> p k j", k=8), reads=rd_w, writes=[t_wcb[0]])
                for tb in range(4):
                    cols = slice(tb * 512, (tb + 1) * 512)
                    bank = 6 + (tb % 2)
                    for kc in range(8):
                        P.op("pe", lambda e, kc=kc, cols=cols, bank=bank: e.matmul(ps[:, bank, :], wc[:, 0, kc, :], hT[:, kc, cols],
                                                                                   start=(kc == 0), stop=(kc == 7)),
                             reads=[t_wcb[0]] + t_hT[tb * 4:tb * 4 + 4], writes=[ps_t[bank]])
                    if ch <= 9:
                        P.op("act", lambda e, ch=ch, cols=cols, bank=bank: e.activation(out=NQ[:, ch - 6, cols], in_=ps[:, bank, :],
                                                                                       func=AF.Identity, scale=0.125),
                             reads=[ps_t[bank]], writes=[t_NQ[ch - 6][tb]])
                    elif ch <= 13:
                        P.op("act", lambda e, ch=ch, cols=cols, bank=bank: e.copy(out=fm[ch - 10][:, cols], in_=ps[:, bank, :]),
                             reads=[ps_t[bank]], writes=[t_fm[ch - 10][tb]])
                    else:
                        gt = fw_[0]
                        P.op("act", lambda e, gt=gt, bank=bank: e.activation(out=gt[0:24, :], in_=ps[0:24, bank, :], func=AF.Sigmoid),
                             reads=[ps_t[bank]], writes=[t_fw[0]])
                        P.op("dve", lambda e, gt=gt, cols=cols: e.tensor_copy(out=ghl[0:24, 0, cols], in_=gt[0:24, :]),
                             reads=[t_fw[0]], writes=[t_ghl[tb]])
                        P.op("dve", lambda e, gt=gt, cols=cols: e.tensor_tensor(out=ghl[0:24, 1, cols], in0=gt[0:24, :], in1=ghl[0:24, 0, cols],
                                                                                op=ALU.subtract),
                             reads=[t_fw[0], t_ghl[tb]], writes=[t_ghl[tb]])

            P.barrier()
            hx = hT.rearrange("p k s -> p (k s)")
            hoff = [0]

            def hview(n, shape, dt):
                o = hoff[0]
                hoff[0] = o + n + (n % 2)
                assert hoff[0] <= 8 * S
                if dt == BF16:
                    ap = hx[:, o:o + n]
                else:
                    ap = hx[:, o:o + n].bitcast(F32)
                if len(shape) == 3:
                    ap = ap.rearrange("p (a b) -> p a b", a=shape[1])
                elif len(shape) == 4:
                    ap = ap.rearrange("p (a b c) -> p a b c", a=shape[1], b=shape[2])
                return ap
            t_h = lambda: T()
            w1sb = hview(32 * 256, [128, 32, 256], BF16); t_w1 = T()
            after_w1 = hoff[0]
            hoff[0] = 0
            efp = hview(2 * 8 * 128, [128, 8, 128], F32); t_e = T()
            pbf = hview(8 * 128, [128, 8, 128], BF16); t_p = T()
            pT = hview(4 * 8 * 128, [128, 4, 8, 128], BF16); t_pT = [T() for _ in range(4)]
            pT2 = [pT, self.view(wvn_off, [128, 4, 8, 128], BF16)]
            t_pT2 = [t_pT, [T() for _ in range(4)]]
            assert hoff[0] <= after_w1
            hoff[0] = after_w1
            peT = hview(32, [128, 32], BF16); t_pe = T()
            w2k = hview(512, [128, 2, 2, 128], BF16); t_w2 = T()
            w2v = hview(128, [128, 2, 64], BF16)
            hid = hview(256, [128, 2, 128], BF16); t_hid = [T(), T()]
            xa = hview(256, [128, 128], F32); t_xa = T()
            xb = hview(256, [128, 128], F32); t_xb = T()
            kcT = hview(128, [128, 128], BF16); t_kc = T()
            vc = hview(128, [128, 2, 64], BF16); t_vc = T()
            pebias = hview(4, [128, 2], F32); t_peb = T()
            ssum = hview(32, [128, 16], F32); t_ss = T()
            scr = hview(2 * 64, [128, 64], F32); t_scr = T()
            m8 = hview(2 * 16, [128, 16], F32); t_m8 = T()
            mbb = hview(64, [128, 64], BF16); t_mb = T()
            EXP = hview(16 * 128, [128, 16, 128], BF16)
            selb = hview(24 * 128, [128, 24, 128], BF16)
            scab = hview(2 * NT * 32, [128, 2, NT, 32], BF16)
            t_c3 = T()
            self.dma("sp", EXP[0:64], cst_b["expmat"].rearrange("p (a b) -> p a b", a=16), reads=[t_cstb], writes=[t_c3])
            self.dma("sp", selb[0:24], cst_b["selb"].rearrange("p (a b) -> p a b", a=24), reads=[t_cstb], writes=[t_c3])
            for a_ in range(2):
                self.dma("sp", scab[:, a_], cst_b["scab"][a_].rearrange("p (a b) -> p a b", a=NT), reads=[t_cstb], writes=[t_c3])

            self.dma("sp", w2k, evb["ev_w2k"].rearrange("p (g c j) -> p g c j", g=2, c=2), reads=rd_w, writes=[t_w2])
            self.dma("sp", w2v, evb["ev_w2v"].rearrange("p (c j) -> p c j", c=2), reads=rd_w, writes=[t_w2])
            for kv in range(2):
                self.dma("sp", w1sb, evb["ev_w1"][kv].rearrange("p (l c) -> p l c", l=32), reads=rd_w, writes=[t_w1])
                self.dma("sp", peT[0:64, :], evb["ev_peT"][kv], reads=rd_w, writes=[t_pe])
                for cc in range(2):
                    for l_ in range(32):
                        P.op("pe", lambda e, cc=cc, l_=l_: e.matmul(ps[:, 7, 0:1], w1sb[0:64, l_, cc * 128:(cc + 1) * 128], peT[0:64, l_:l_ + 1],
                                                                    start=(l_ == 0), stop=(l_ == 31)),
                             reads=[t_w1, t_pe], writes=[ps_t[7]])
                    P.op("dve", lambda e, cc=cc: e.tensor_copy(out=pebias[:, cc:cc + 1], in_=ps[:, 7, 0:1]), reads=[ps_t[7]], writes=[t_peb])
                src = srcK if kv == 0 else srcV
                tsrc = t_fm[kv]
                for g in range(2):
                    gp = slice(g * 64, (g + 1) * 64)
                    for cc in range(2):
                        for l_ in range(32):
                            P.op("pe", lambda e, gp=gp, cc=cc, l_=l_, src=src: e.matmul(
                                ps[:, 6, 0:127], w1sb[gp, l_, cc * 128:(cc + 1) * 128], src[gp, l_:l_ + 2017:16],
                                start=(l_ == 0), stop=(l_ == 31)), reads=[t_w1] + tsrc, writes=[ps_t[6]])
                        P.op("act", lambda e, cc=cc: e.activation(out=xa[:, 0:127], in_=ps[:, 6, 0:127], func=AF.Identity,
                                                                  bias=pebias[:, cc:cc + 1], scale=1.0),
                             reads=[ps_t[6], t_peb], writes=[t_xa])
                        P.op("act", lambda e: e.activation(out=xb[:, 0:127], in_=xa[:, 0:127], func=AF.Square), reads=[t_xa], writes=[t_xb])
                        P.op("dve", lambda e: e.tensor_scalar(out=xb[:, 0:127], in0=xb[:, 0:127], scalar1=0.044715, scalar2=1.0,
                                                              op0=ALU.mult, op1=ALU.add), reads=[t_xb], writes=[t_xb])
                        P.op("dve", lambda e: e.tensor_tensor(out=xb[:, 0:127], in0=xb[:, 0:127], in1=xa[:, 0:127], op=ALU.mult),
                             reads=[t_xb, t_xa], writes=[t_xb])
                        P.op("act", lambda e: e.activation(out=xb[:, 0:127], in_=xb[:, 0:127], func=AF.Sigmoid, scale=1.5957691216057308),
                             reads=[t_xb], writes=[t_xb])
                        P.op("dve", lambda e, cc=cc: e.tensor_tensor(out=hid[:, cc, 0:127], in0=xa[:, 0:127], in1=xb[:, 0:127], op=ALU.mult),
                             reads=[t_xa, t_xb], writes=[t_hid[cc]])
                    if kv == 0:
                        for cc in range(2):
                            P.op("pe", lambda e, g=g, cc=cc: e.matmul(ps[:, 7, 0:127], w2k[:, g, cc, :], hid[:, cc, 0:127],
                                                                      start=(cc == 0), stop=(cc == 1)),
                                 reads=[t_w2] + t_hid, writes=[ps_t[7]])
                        P.op("dve", lambda e, gp=gp: e.tensor_copy(out=kcT[gp, 0:127], in_=ps[gp, 7, 0:127]), reads=[ps_t[7]], writes=[t_kc])
                    else:
                        for cc in range(2):
                            P.op("pe", lambda e, cc=cc: e.matmul(ps[0:127, 7, 0:64], hid[:, cc, 0:127], w2v[:, cc, :],
                                                                 start=(cc == 0), stop=(cc == 1)),
                                 reads=[t_w2] + t_hid, writes=[ps_t[7]])
                        P.op("dve", lambda e, g=g: e.tensor_copy(out=vc[0:127, g, :], in_=ps[0:127, 7, 0:64]), reads=[ps_t[7]], writes=[t_vc])
            P.barrier()
            self.dbg_out("dbg_kcT", kcT, [128, 128], BF16, [t_kc])
            self.dbg_out("dbg_vc", vc, [128, 2, 64], BF16, [t_vc])

            def cmp_parts(Qc, tt):
                qt = 4 * Qc + tt
                qc = slice(qt * 128, (qt + 1) * 128)
                pTq = pT2[Qc % 2]
                tpT = t_pT2[Qc % 2][tt]

                def part_a():
                    for h in range(8):
                        g, jn = h // 4, h % 4
                        gp = slice(g * 64, (g + 1) * 64)
                        bank = h // 4
                        cs = (h % 4) * 127
                        P.op("pe", lambda e, gp=gp, jn=jn, bank=bank, cs=cs: e.matmul(
                            ps[:, bank, cs:cs + 127], NQ[gp, jn, qc], kcT[gp, 0:127], start=True, stop=False),
                            reads=[t_NQ[jn][Qc], t_kc], writes=[ps_t[bank]])
                        P.op("pe", lambda e, h=h, bank=bank, cs=cs: e.matmul(
                            ps[:, bank, cs:cs + 127], ident[:], Bc[:, h, 120 - 8 * qt:247 - 8 * qt], start=False, stop=True),
                            reads=[t_ident, t_B], writes=[ps_t[bank]])
                    for h in range(8):
                        bank = h // 4
                        cs = (h % 4) * 127
                        P.op("act", lambda e, h=h, bank=bank, cs=cs: e.activation(out=efp[:, h, 0:127], in_=ps[:, bank, cs:cs + 127], func=AF.Exp,
                                                                                   accum_out=ssum[:, h:h + 1]),
                             reads=[ps_t[bank]], writes=[t_e, t_ss])
                    P.op("dve", lambda e: e.tensor_scalar_add(out=ssum[:, 8:16], in0=ssum[:, 0:8], scalar1=1e-30), reads=[t_ss], writes=[t_ss])
                    P.op("dve", lambda e: e.reciprocal(out=ssum[:, 8:16], in_=ssum[:, 8:16]), reads=[t_ss], writes=[t_ss])
                    for h in range(8):
                        P.op("dve", lambda e, h=h: e.tensor_scalar_mul(out=pbf[:, h, 0:127], in0=efp[:, h, 0:127], scalar1=ssum[:, 8 + h:9 + h]),
                             reads=[t_e, t_ss], writes=[t_p])

                def part_b():
                    psT = ps[:, 5, :].bitcast(BF16)
                    for h in range(8):
                        P.op("pe", lambda e, h=h: e.transpose(psT[0:127, h * 128:(h + 1) * 128], pbf[:, h, 0:127], ident[:]),
                             reads=[t_p, t_ident], writes=[ps_t[5]])
                    P.op("act", lambda e: e.copy(out=pTq[0:127, tt, :, :], in_=psT[0:127, :].rearrange("p (h q) -> p h q", h=8)),
                         reads=[ps_t[5]], writes=[tpT])

                def part_c():
                    for h in range(8):
                        g = h // 4
                        P.op("pe", lambda e, h=h, g=g: e.matmul(ps[:, 7, g * 32:(g + 1) * 32], pTq[0:127, tt, h, :], ovl[:, :],
                                                                start=(h % 4 == 0), stop=(h % 4 == 3)),
                             reads=[tpT, t_cst], writes=[ps_t[7]])
                    for g in range(2):
                        P.op("dve", lambda e, g=g: e.tensor_tensor(out=scr[:, g * 32:(g + 1) * 32], in0=ps[:, 7, g * 32:(g + 1) * 32],
                                                                   in1=scab[:, 0, qt, :], op=ALU.mult),
                             reads=[ps_t[7], t_c3], writes=[t_scr])
                        P.op("dve", lambda e, g=g: e.tensor_tensor(out=scr[:, g * 32:(g + 1) * 32], in0=scr[:, g * 32:(g + 1) * 32],
                                                                   in1=scab[:, 1, qt, :], op=ALU.add),
                             reads=[t_scr, t_c3], writes=[t_scr])
                        P.op("dve", lambda e, g=g: e.max(out=m8[:, g * 8:(g + 1) * 8], in_=scr[:, g * 32:(g + 1) * 32]),
                             reads=[t_scr], writes=[t_m8])
                        P.op("dve", lambda e, g=g: e.tensor_scalar(out=scr[:, g * 32:(g + 1) * 32], in0=scr[:, g * 32:(g + 1) * 32],
                                                                   scalar1=m8[:, g * 8 + 7:g * 8 + 8], scalar2=1.0,
                                                                   op0=ALU.is_ge, op1=ALU.subtract),
                             reads=[t_scr, t_m8], writes=[t_scr])
                    P.op("dve", lambda e: e.tensor_scalar_mul(out=mbb[:, :], in0=scr[:, :], scalar1=-NEG), reads=[t_scr], writes=[t_mb])

                def part_d():
                    psM = ps[:, 7, 256:512].bitcast(BF16)
                    P.op("pe", lambda e: e.transpose(psM[0:64, 0:128], mbb[:, :], ident[:]), reads=[t_mb, t_ident], writes=[ps_t[7]])
                    P.op("act", lambda e: e.copy(out=maskT[0:64, qc], in_=psM[0:64, 0:128]), reads=[ps_t[7]], writes=[t_mask[qt]])
                return [part_a, part_b, part_c, part_d]

            for Q in range(4):
                if Q == 0:
                    for tt in range(4):
                        for part in cmp_parts(0, tt):
                            part()
                nparts = [cmp_parts(Q + 1, tt) for tt in range(4)] if Q < 3 else None
                for h in range(8):
                    g, jn = h // 4, h % 4
                    gp = slice(g * 64, (g + 1) * 64)
                    mp = slice(g * 32, (g + 1) * 32)
                    qrd = [t_NQ[jn][Q]]
                    specs = []
                    for kt in range(4 * Q + 4):
                        jj = kt - 4 * Q
                        c0 = max(0, jj) * 128
                        ex = [(EXP[mp, kt, :], maskT[mp, Q * 512 + c0:(Q + 1) * 512], c0, 512, [t_c3] + t_mask[4 * Q:4 * Q + 4])]
                        if jj == -1:
                            ex.append((ident[:], Bsb[:, h, 1, :], 0, 128, [t_ident, t_B]))
                        elif jj == 3:
                            ex.append((ident[:], Bsb[:, h, 0, :], 384, 512, [t_ident, t_B]))
                        elif jj >= 0:
                            ex.append((ident[:], Bsb[:, h, 0:2, :].rearrange("p t q -> p (t q)"), c0, c0 + 256, [t_ident, t_B]))
                        specs.append(dict(c0=c0, c1=512, lhsT=kselT[gp, kt * 128:(kt + 1) * 128], rhs=NQ[gp, jn, Q * 512 + c0:(Q + 1) * 512],
                                          rd=qrd + t_fm[2], extra=ex, v=Vn[:, kt, 0 + g, :], vrd=[t_Vn[kt]]))
                    sidx = run_tiles(specs, 3, E, t_E, sidx)
                    if Q < 3:
                        nparts[h // 2][(h % 2) * 2]()
                    specs = []
                    for kt in range(max(0, 4 * Q - 2), 4 * Q + 4):
                        jj = kt - 4 * Q
                        lo = max(0, jj)
                        hi = min(3, jj + 2)
                        c0, c1 = lo * 128, (hi + 1) * 128
                        t0, t1 = lo - jj, hi - jj
                        ex = [(ident[:], Bsb[:, h, t0:t1 + 1, :].rearrange("p t q -> p (t q)"), c0, c1, [t_ident, t_B])]
                        specs.append(dict(c0=c0, c1=c1, lhsT=kwinT[gp, kt * 128:(kt + 1) * 128], rhs=NQ[gp, jn, Q * 512 + c0:Q * 512 + c1],
                                          rd=qrd + t_fm[3], extra=ex, v=Vn[:, kt, 2 + g, :], vrd=[t_Vn[kt]]))
                    sidx = run_tiles(specs, 4, E, t_E, sidx)
                    if Q < 3:
                        nparts[h // 2][(h % 2) * 2 + 1]()
                    for tt in range(4):
                        P.op("pe", lambda e, g=g, tt=tt, h=h, pTq=pT2[Q % 2]: e.matmul(ps[0:64, 6, tt * 128:(tt + 1) * 128], vc[0:127, g, :],
                                                                                        pTq[0:127, tt, h, :], start=True, stop=True),
                             reads=[t_vc, t_pT2[Q % 2][tt]], writes=[ps_t[6]])
                    acc = fw_[0]
                    hp = (h % 2) * 64
                    brs = [(0, 6, False), (1, 3, True), (2, 4, True)]
                    if self.nsa_only is not None:
                        brs = [brs[self.nsa_only]]
                    for bi_, (br, src_bank, norm) in enumerate(brs):
                        r = 3 * h + br
                        first_, last_ = (bi_ == 0), (bi_ == len(brs) - 1)
                        for part in range(2):
                            gbk = 5 if bi_ % 2 == 0 else 7
                            P.op("pe", lambda e, r=r, part=part, Q=Q, gbk=gbk: e.matmul(ps[:, gbk, :], selb[0:24, r, :],
                                                                                   ghl[0:24, part, Q * 512:(Q + 1) * 512],
                                                                                   start=(part == 0), stop=(part == 1)),
                                 reads=[t_c3, t_ghl[Q]], writes=[ps_t[gbk]])
                        f = fw_[1]
                        if norm:
                            act_recip(f[64:128, :], ps[64:128, src_bank, :], [ps_t[src_bank]], [t_fw[1]])
                            P.op("dve", lambda e, f=f, gbk=gbk: e.tensor_tensor(out=f[64:128, :], in0=ps[64:128, gbk, :], in1=f[64:128, :], op=ALU.mult),
                                 reads=[ps_t[gbk], t_fw[1]], writes=[t_fw[1]])
                        else:
                            P.op("dve", lambda e, f=f, gbk=gbk: e.tensor_copy(out=f[64:128, :], in_=ps[64:128, gbk, :]), reads=[ps_t[gbk]], writes=[t_fw[1]])
                        dst = oN[hp:hp + 64, h // 2, Q * 512:(Q + 1) * 512] if last_ else acc[0:64, :]
                        tdst = t_oN[h // 2][Q] if last_ else t_fw[0]
                        if first_:
                            P.op("dve", lambda e, f=f, src_bank=src_bank, dst=dst: e.tensor_tensor(out=dst, in0=ps[0:64, src_bank, :], in1=f[64:128, :],
                                                                                                  op=ALU.mult),
                                 reads=[ps_t[src_bank], t_fw[1]], writes=[tdst])
                        else:
                            tmp_ = fw_[2]
                            P.op("dve", lambda e, f=f, src_bank=src_bank, tmp_=tmp_: e.tensor_tensor(out=tmp_[0:64, :], in0=ps[0:64, src_bank, :],
                                                                                                    in1=f[64:128, :], op=ALU.mult),
                                 reads=[ps_t[src_bank], t_fw[1]], writes=[t_fw[2]])
                            P.op("pool", lambda e, tmp_=tmp_, dst=dst: e.tensor_tensor(out=dst, in0=acc[0:64, :], in1=tmp_[0:64, :], op=ALU.add),
                                 reads=[t_fw[0], t_fw[2]], writes=[tdst])
            self.dbg_out("dbg_oN", oN, [128, 4, S], BF16, [t for r in t_oN for t in r])
            self.dbg_out("dbg_maskT", maskT, [128, S], BF16, t_mask)

            P.barrier()
            off[0] = base_off + 4 * S
            wo = self.view(alloc(8 * D), [128, 8, D], BF16)
            t_wo = T()
            self.dma("sp", wo, evb["ev_wo"].rearrange("p (k n) -> p k n", k=8), reads=rd_w, writes=[t_wo])
            def mm_wo(t):
                yb = 3 if t % 2 == 0 else 5
                for dh in range(2):
                    for c in range(8):
                        src_ = oM if c < 4 else oN
                        tsrc_ = (t_oM if c < 4 else t_oN)[c % 4][t // 4]
                        P.op("pe", lambda e, t=t, dh=dh, c=c, src_=src_, yb=yb: e.matmul(
                            ps[:, yb + dh, :], src_[:, c % 4, t * 128:(t + 1) * 128], wo[:, c, dh * 512:(dh + 1) * 512],
                            start=(c == 0), stop=(c == 7)), reads=[tsrc_, t_wo], writes=[ps_t[yb + dh]])
            ep_pipeline(list(range(NT)), mm_wo, lambda t: 3 if t % 2 == 0 else 5, b, l, 0, xsrc, 7)

        for b in range(nseq):
            self.have_hT = False
            for pi, (l, sub) in enumerate(plan):
                xsrc = x_in if pi == 0 else out
                self.nxt = plan[pi + 1] if (pi + 1 < len(plan) and self.fuse) else None
                if sub == 1:
                    ffn(b, l, xsrc)
                elif l % 2 == 1:
                    diff_attn(b, l, xsrc)
                else:
                    even_attn(b, l, xsrc)
                self.have_hT = self.nxt is not None
        P.emit()


def host_prep(inputs, core, nseq, seq0=None):
    b0 = core * nseq if seq0 is None else seq0
    m = {}
    m["x"] = np.ascontiguousarray(inputs["x"][b0:b0 + nseq])
    m["cT"] = np.ascontiguousarray(inputs["c"][b0:b0 + nseq].reshape(nseq, 8, 128).transpose(2, 1, 0))
    for k in ["ada_w", "ada_b", "ln_g", "ln_b", "rel_bias"]:
        m[k] = np.ascontiguousarray(inputs[k])
    up = inputs["ffn_w_up"].reshape(DEPTH, 8, 128, NFC, 128)
    m["wu"] = np.ascontiguousarray(up.transpose(0, 3, 2, 1, 4))
    gt = inputs["ffn_w_gate"].reshape(DEPTH, 8, 128, NFC, 128)
    m["wg"] = np.ascontiguousarray(gt.transpose(0, 3, 2, 1, 4))
    dn = inputs["ffn_w_down"].reshape(DEPTH, NFC, 128, D)
    m["wd"] = np.ascontiguousarray(dn.transpose(0, 2, 1, 3))
    cw = np.concatenate([inputs["ffn_conv_w"].transpose(0, 2, 1), inputs["ffn_conv_b"][:, :, None]], axis=2)
    m["convp"] = np.ascontiguousarray(cw.astype(np.float32))
    w = inputs["od_w_in"]
    qk = w[:, :, :2048].reshape(2, 8, 128, 16, 128)
    m["od_wqk"] = np.ascontiguousarray(qk.transpose(0, 3, 2, 1, 4))
    v = w[:, :, 2048:].reshape(2, 8, 128, D)
    m["od_wv"] = np.ascontiguousarray(v.transpose(0, 2, 1, 3))
    wo = inputs["od_w_o"].reshape(2, 8, 128, D)
    m["od_wo"] = np.ascontiguousarray(wo.transpose(0, 2, 1, 3))
    m["diff_lambda"] = np.ascontiguousarray(inputs["diff_lambda"].reshape(2, 256))
    m["diff_subln"] = np.ascontiguousarray(inputs["diff_subln"].reshape(2, 128, 1))
    ew = inputs["ev_w_in"]
    def fm_chunk(W, cols, nk):
        Z = np.zeros((W.shape[0], W.shape[1], 128), np.float32)
        Z[:, :, :len(cols)] = W[:, :, cols]
        return Z.reshape(W.shape[0], nk, 128, 128).transpose(0, 2, 1, 3).reshape(W.shape[0], 128, nk * 128)
    sw = [(r + 16) % 32 for r in range(32)]
    lists = [list(range(c * 128, (c + 1) * 128)) for c in range(3)]
    lists += [list(range(384 + c * 128, 384 + (c + 1) * 128)) for c in range(2)]
    lists += [[640 + r for r in range(32)] + [640 + r for r in sw]]
    for c in range(4):
        lists += [[672 + c * 64 + d for d in range(64)] + [672 + (c + 4) * 64 + d for d in range(64)]]
    lists += [[1184 + d for d in range(0, 128)], [1184 + d for d in range(128, 256)],
              [1184 + d for d in range(256, 384)], [1184 + d for d in range(512, 640)]]
    lists += [list(range(1952, 1976))]
    m["ev_wc"] = np.ascontiguousarray(np.stack([fm_chunk(ew, L, 8) for L in lists], axis=1))
    vcols = [1184 + d for d in range(384, 512)] + [1184 + d for d in range(640, 768)]
    m["ev_wvn"] = np.ascontiguousarray(ew[:, :, vcols].reshape(2, 8, 128, 256).transpose(0, 2, 1, 3).reshape(2, 128, 8 * 256))
    uq = inputs["mla_w_uq"]
    m["ev_uq"] = np.ascontiguousarray(np.stack([fm_chunk(uq, [h * 96 + 64 + r for r in range(32)] + [h * 96 + 64 + r for r in sw]
                                                          + [h * 96 + d for d in range(64)], 3) for h in range(8)], axis=1))
    ukv = inputs["mla_w_ukv"]
    def kchunk(h):
        Z = np.zeros((2, 256, 128), np.float32)
        Z[:, :, 64:] = ukv[:, :, h * 128:h * 128 + 64]
        return Z.reshape(2, 2, 128, 128).transpose(0, 2, 1, 3).reshape(2, 128, 256)
    m["ev_ukvk"] = np.ascontiguousarray(np.stack([kchunk(h) for h in range(8)], axis=1))
    vc_ = [h * 128 + 64 + d for h in range(8) for d in range(64)]
    m["ev_ukvv"] = np.ascontiguousarray(ukv[:, :, vc_].reshape(2, 2, 128, 512).transpose(0, 2, 1, 3).reshape(2, 128, 1024))
    m["ev_qn"] = np.ascontiguousarray(inputs["mla_q_norm"].reshape(2, 3, 128).transpose(0, 2, 1))
    m["ev_kvn"] = np.ascontiguousarray(inputs["mla_kv_norm"].reshape(2, 2, 128).transpose(0, 2, 1))
    w1 = inputs["nsa_cmp_w1"].reshape(2, 2, 32, 64, 256).transpose(0, 1, 3, 2, 4)
    m["ev_w1"] = np.ascontiguousarray(np.concatenate([w1, w1], axis=2).reshape(2, 2, 128, 32 * 256))
    w2 = inputs["nsa_cmp_w2"]
    w2k = np.zeros((2, 128, 2, 2, 128), np.float32)
    for g in range(2):
        w2k[:, :, g, :, g * 64:(g + 1) * 64] = w2[:, 0].reshape(2, 2, 128, 64).transpose(0, 2, 1, 3)
    m["ev_w2k"] = np.ascontiguousarray(w2k.reshape(2, 128, 512))
    m["ev_w2v"] = np.ascontiguousarray(w2[:, 1].reshape(2, 2, 128, 64).transpose(0, 2, 1, 3).reshape(2, 128, 128))
    m["ev_peT"] = np.ascontiguousarray(inputs["nsa_cmp_pe"].transpose(0, 1, 3, 2))
    m["ev_wo"] = np.ascontiguousarray(inputs["ev_w_o"].reshape(2, 8, 128, D).transpose(0, 2, 1, 3).reshape(2, 128, 8 * D))
    m.update(static_consts())
    return m


_SC = {}


def static_consts():
    if _SC:
        return _SC
    inv = (1.0 / (np.float32(10000.0) ** (np.arange(0, 32, 2, dtype=np.float32) / np.float32(32)))).astype(np.float32)
    ang = (np.arange(S, dtype=np.float32)[:, None] * inv[None, :]).astype(np.float32)
    cos, sin = np.cos(ang).astype(np.float32), np.sin(ang).astype(np.float32)
    rt = np.zeros((64, S), np.float32)
    for r in range(32):
        rt[r] = cos[:, r % 16]
        rt[32 + r] = -sin[:, r] if r < 16 else sin[:, r - 16]
    _SC["ropeT"] = rt
    kk, qq = np.meshgrid(np.arange(128), np.arange(128), indexing="ij")
    _SC["cmask"] = np.stack([np.where(qq >= kk, 0.0, NEG), np.where(qq < kk, 0.0, NEG)]).astype(np.float32)
    ex = np.zeros((2, 32, 16, 128), np.float32)
    for kt in range(16):
        ex[:, 2 * kt, kt, 0:64] = 1.0
        ex[:, 2 * kt + 1, kt, 64:128] = 1.0
    _SC["expmat"] = ex.reshape(64, 16 * 128)
    starts = np.arange(127) * 16
    jb = np.arange(32)
    _SC["ovl"] = ((starts[:, None] < (jb[None, :] + 1) * 64) & (starts[:, None] + 32 > jb[None, :] * 64)).astype(np.float32)
    t = np.arange(S)
    cur = t // 64
    forced = (jb[None, :] == 0) | (jb[None, :] == cur[:, None]) | (jb[None, :] == cur[:, None] - 1)
    causal = jb[None, :] * 64 <= t[:, None]
    A = (causal & ~forced).astype(np.float32)
    Bf = np.where(~causal, -1.0, np.where(forced, 1e4, 0.0)).astype(np.float32)
    sc = np.stack([A, Bf]).reshape(2, NT, 128, 32).transpose(0, 2, 1, 3).reshape(2, 128, NT * 32)
    _SC["scab"] = np.ascontiguousarray(sc)
    sb = np.zeros((24, 24, 128), np.float32)
    for r in range(24):
        sb[r, r, :] = 1.0
    _SC["selb"] = sb.reshape(24, 24 * 128)
    _SC["onehot"] = onehot_consts()
    _SC["ident"] = np.eye(128, dtype=np.float32)
    return _SC


FULL_PLAN = [(l, s) for l in range(DEPTH) for s in range(2)]
NCORES = 8
_CACHE = {}


def kernel(**inputs):
    inputs = {k: np.asarray(v) for k, v in inputs.items()}
    B = inputs["x"].shape[0]
    nseq = B // NCORES
    kb = K(nseq, FULL_PLAN)
    in_maps = []
    shared = host_prep(inputs, 0, nseq)
    shared = {k: v for k, v in shared.items() if k in kb.dram_in}
    for core in range(NCORES):
        m = dict(shared)
        b0 = core * nseq
        m["x"] = np.ascontiguousarray(inputs["x"][b0:b0 + nseq])
        m["cT"] = np.ascontiguousarray(inputs["c"][b0:b0 + nseq].reshape(nseq, 8, 128).transpose(2, 1, 0))
        in_maps.append(m)
    res = run_bass_kernel_spmd(kb.nc, in_maps, core_ids=list(range(NCORES)))
    outs = [np.asarray(r["out"]) for r in res.results]
    return np.concatenate(outs, axis=0).astype(np.float32)
```
